# Optimizing a Trainium2 kernel written in Bass

```python
import jax, jax.numpy as jnp
from jax import lax
import numpy as np

D_MODEL = 2048
BATCH = 2
SEQ = 8192
DEPTH = 1

GDN_QK_HEADS = 16
GDN_V_HEADS = 32
GDN_HEAD_DIM = 128
GDN_QK_WIDTH = GDN_QK_HEADS * GDN_HEAD_DIM
GDN_V_WIDTH = GDN_V_HEADS * GDN_HEAD_DIM
MLSTM_HEADS = 8
MLSTM_QK_DIM = D_MODEL // 16
MLSTM_V_DIM = D_MODEL // 8
MLSTM_QK_WIDTH = MLSTM_HEADS * MLSTM_QK_DIM
MLSTM_V_WIDTH = MLSTM_HEADS * MLSTM_V_DIM
CONV_WIDTH = 4
CHUNK = 64
NORM_EPS = 1e-6

IN_SPLITS = (
    2 * GDN_QK_WIDTH + GDN_V_WIDTH,
    GDN_V_HEADS,
    GDN_V_HEADS,
    GDN_V_WIDTH,
    2 * MLSTM_QK_WIDTH,
    MLSTM_V_WIDTH,
    MLSTM_HEADS,
    MLSTM_HEADS,
    MLSTM_V_WIDTH,
    MLSTM_V_WIDTH,
    D_MODEL,
    D_MODEL,
)
IN_WIDTH = sum(IN_SPLITS)

kernel_name = "hybrid_gdn_mlstm_gated_merge"


def rms_norm(x, w):
    xf = x.astype(jnp.float32)
    y = xf * lax.rsqrt(jnp.mean(xf * xf, axis=-1, keepdims=True) + NORM_EPS)
    return (y * w.astype(jnp.float32)).astype(x.dtype)


def l2_normalize(x):
    xf = x.astype(jnp.float32)
    return xf * lax.rsqrt(jnp.sum(xf * xf, axis=-1, keepdims=True) + NORM_EPS)


def causal_conv_silu(x, w):
    k_width, s = w.shape[0], x.shape[1]
    xp = jnp.pad(x, ((0, 0), (k_width - 1, 0), (0, 0)))
    y = sum(xp[:, j:j + s] * w[j] for j in range(k_width))
    return jax.nn.silu(y)


def to_chunks(x):
    b, s, h = x.shape[:3]
    x = x.reshape((b, s // CHUNK, CHUNK, h) + x.shape[3:])
    return jnp.moveaxis(x, (1, 3), (0, 2))


def from_chunks(x):
    x = jnp.moveaxis(x, (0, 2), (1, 3))
    b, nc, c, h, d = x.shape
    return x.reshape(b, nc * c, h, d)


def gated_delta_rule(q, k, v, g, beta):
    f32 = jnp.float32
    qc, kc, vc = (to_chunks(t.astype(f32)) for t in (q, k, v))
    gc, bc = to_chunks(g.astype(f32)), to_chunks(beta.astype(f32))
    dv = vc.shape[-1]
    causal = jnp.tril(jnp.ones((CHUNK, CHUNK), bool))
    strict = jnp.tril(jnp.ones((CHUNK, CHUNK), bool), -1)
    g_cum = jnp.cumsum(gc, axis=-1)
    decay = jnp.exp(jnp.where(causal, g_cum[..., :, None] - g_cum[..., None, :], -jnp.inf))
    k_beta = kc * bc[..., None]
    v_beta = vc * bc[..., None]
    lower = jnp.where(strict, jnp.einsum('nbhid,nbhjd->nbhij', k_beta, kc) * decay, 0.0)
    unit_lower = lower + jnp.eye(CHUNK, dtype=f32)
    rhs = jnp.concatenate([v_beta, k_beta * jnp.exp(g_cum)[..., None]], axis=-1)
    sol = lax.linalg.triangular_solve(unit_lower, rhs, left_side=True, lower=True, unit_diagonal=True)
    u, w = sol[..., :dv], sol[..., dv:]
    attn = jnp.einsum('nbhid,nbhjd->nbhij', qc, kc) * decay
    q_decay = qc * jnp.exp(g_cum)[..., None]
    k_tail = kc * jnp.exp(g_cum[..., -1:] - g_cum)[..., None]
    g_total = jnp.exp(g_cum[..., -1])

    def step(state, inp):
        attn_c, u_c, w_c, qd_c, kt_c, gt_c = inp
        v_new = u_c - jnp.einsum('bhck,bhkv->bhcv', w_c, state)
        o = jnp.einsum('bhck,bhkv->bhcv', qd_c, state) + jnp.einsum('bhij,bhjv->bhiv', attn_c, v_new)
        state = state * gt_c[..., None, None] + jnp.einsum('bhck,bhcv->bhkv', kt_c, v_new)
        return state, o

    b, h, dk = qc.shape[1], qc.shape[2], qc.shape[-1]
    s0 = jnp.zeros((b, h, dk, dv), f32)
    _, o = lax.scan(step, s0, (attn, u, w, q_decay, k_tail, g_total))
    return from_chunks(o)


def mlstm_chunkwise(q, k, v, i_pre, f_pre):
    f32 = jnp.float32
    qc, kc, vc = (to_chunks(t.astype(f32)) for t in (q, k, v))
    ic = to_chunks(i_pre.astype(f32))
    b_cum = jnp.cumsum(to_chunks(jax.nn.log_sigmoid(f_pre.astype(f32))), axis=-1)
    causal = jnp.tril(jnp.ones((CHUNK, CHUNK), bool))

    def step(carry, inp):
        c_state, n_state, m_state = carry
        q_c, k_c, v_c, i_c, b_c = inp
        log_d = jnp.where(causal, b_c[..., :, None] - b_c[..., None, :] + i_c[..., None, :], -jnp.inf)
        m_inter = b_c + m_state[..., None]
        m_t = jnp.maximum(m_inter, jnp.max(log_d, axis=-1))
        w_intra = jnp.exp(log_d - m_t[..., None])
        w_inter = jnp.exp(m_inter - m_t)
        s = jnp.einsum('bhid,bhjd->bhij', q_c, k_c) * w_intra
        num = (w_inter[..., None] * jnp.einsum('bhck,bhkv->bhcv', q_c, c_state)
               + jnp.einsum('bhij,bhjv->bhiv', s, v_c))
        den = w_inter * jnp.einsum('bhck,bhk->bhc', q_c, n_state) + jnp.sum(s, axis=-1)
        h = num / jnp.maximum(jnp.abs(den), jnp.exp(-m_t))[..., None]
        b_last = b_c[..., -1]
        log_end = b_last[..., None] - b_c + i_c
        m_new = jnp.maximum(b_last + m_state, jnp.max(log_end, axis=-1))
        wk = jnp.exp(log_end - m_new[..., None])
        carry_decay = jnp.exp(b_last + m_state - m_new)
        c_state = carry_decay[..., None, None] * c_state + jnp.einsum('bhc,bhck,bhcv->bhkv', wk, k_c, v_c)
        n_state = carry_decay[..., None] * n_state + jnp.einsum('bhc,bhck->bhk', wk, k_c)
        return (c_state, n_state, m_new), h

    b, h, dk, dv = qc.shape[1], qc.shape[2], qc.shape[-1], vc.shape[-1]
    init = (jnp.zeros((b, h, dk, dv), f32), jnp.zeros((b, h, dk), f32), jnp.zeros((b, h), f32))
    _, hs = lax.scan(step, init, (qc, kc, vc, ic, b_cum))
    return from_chunks(hs)


def hybrid_layer(x, c, w_ada, b_ada, norm_pre_w, w_in, gdn_conv_w, gdn_A_log, gdn_dt_bias, gdn_norm_w,
                 mlstm_conv_w, mlstm_b_i, mlstm_b_f, mlstm_norm_w, w_proj_gdn, w_proj_mlstm, w_out, norm_post_w):
    b, s, _ = x.shape
    shift, scale, gate = jnp.split(jax.nn.silu(c) @ w_ada + b_ada, 3, axis=-1)
    h = rms_norm(x, norm_pre_w) * (1.0 + scale[:, None]) + shift[:, None]
    proj = h @ w_in
    (gdn_qkv, gdn_a, gdn_b, gdn_z, ml_qk, ml_v, ml_i, ml_f, ml_o, ml_z, gate_a, gate_b) = jnp.split(
        proj, np.cumsum(IN_SPLITS)[:-1].tolist(), axis=-1)

    qkv = causal_conv_silu(gdn_qkv, gdn_conv_w)
    q_a, k_a, v_a = jnp.split(qkv, [GDN_QK_WIDTH, 2 * GDN_QK_WIDTH], axis=-1)
    rep = GDN_V_HEADS // GDN_QK_HEADS
    q_a = jnp.repeat(l2_normalize(q_a.reshape(b, s, GDN_QK_HEADS, GDN_HEAD_DIM)) * GDN_HEAD_DIM ** -0.5, rep, axis=2)
    k_a = jnp.repeat(l2_normalize(k_a.reshape(b, s, GDN_QK_HEADS, GDN_HEAD_DIM)), rep, axis=2)
    v_a = v_a.reshape(b, s, GDN_V_HEADS, GDN_HEAD_DIM)
    g_log = -jnp.exp(gdn_A_log.astype(jnp.float32)) * jax.nn.softplus(gdn_a.astype(jnp.float32) + gdn_dt_bias)
    beta = jax.nn.sigmoid(gdn_b.astype(jnp.float32))
    o_a = gated_delta_rule(q_a, k_a, v_a, g_log, beta)
    o_a = rms_norm(o_a, gdn_norm_w) * jax.nn.silu(gdn_z.astype(jnp.float32)).reshape(b, s, GDN_V_HEADS, GDN_HEAD_DIM)
    y_a = o_a.reshape(b, s, GDN_V_WIDTH).astype(x.dtype) @ w_proj_gdn

    qk = causal_conv_silu(ml_qk, mlstm_conv_w)
    q_b, k_b = jnp.split(qk, 2, axis=-1)
    q_b = q_b.reshape(b, s, MLSTM_HEADS, MLSTM_QK_DIM) * MLSTM_QK_DIM ** -0.5
    k_b = k_b.reshape(b, s, MLSTM_HEADS, MLSTM_QK_DIM)
    v_b = ml_v.reshape(b, s, MLSTM_HEADS, MLSTM_V_DIM)
    h_b = mlstm_chunkwise(q_b, k_b, v_b, ml_i + mlstm_b_i, ml_f + mlstm_b_f)
    h_b = rms_norm(h_b, mlstm_norm_w.reshape(MLSTM_HEADS, MLSTM_V_DIM)).reshape(b, s, MLSTM_V_WIDTH)
    h_b = jax.nn.sigmoid(ml_o) * h_b * jax.nn.silu(ml_z)
    y_b = h_b.astype(x.dtype) @ w_proj_mlstm

    merged = jax.nn.sigmoid(gate_a) * y_a + jax.nn.sigmoid(gate_b) * y_b
    out = merged @ w_out
    return x + gate[:, None] * rms_norm(out, norm_post_w)


def setup_inputs(seed: int = 0) -> dict:
    key = jax.random.key(seed)
    ks = jax.random.split(key, 20)
    d, f32 = D_MODEL, jnp.float32
    nrm = lambda k, shape, scale: jax.random.normal(k, shape, f32) * scale
    dt = jnp.exp(jax.random.uniform(ks[7], (DEPTH, GDN_V_HEADS), f32, np.log(1e-3), np.log(1e-1)))
    return {
        "x": nrm(ks[0], (BATCH, SEQ, d), 1.0),
        "c": nrm(ks[1], (BATCH, d), 1.0),
        "w_ada": nrm(ks[2], (DEPTH, d, 3 * d), d ** -0.5),
        "b_ada": nrm(ks[3], (DEPTH, 3 * d), 0.02),
        "norm_pre_w": 1.0 + nrm(ks[4], (DEPTH, d), 0.02),
        "w_in": nrm(ks[5], (DEPTH, d, IN_WIDTH), d ** -0.5),
        "gdn_conv_w": nrm(ks[6], (DEPTH, CONV_WIDTH, 2 * GDN_QK_WIDTH + GDN_V_WIDTH), CONV_WIDTH ** -0.5),
        "gdn_A_log": jnp.log(jax.random.uniform(ks[8], (DEPTH, GDN_V_HEADS), f32, 1.0, 16.0)),
        "gdn_dt_bias": dt + jnp.log(-jnp.expm1(-dt)),
        "gdn_norm_w": 1.0 + nrm(ks[9], (DEPTH, GDN_HEAD_DIM), 0.02),
        "mlstm_conv_w": nrm(ks[10], (DEPTH, CONV_WIDTH, 2 * MLSTM_QK_WIDTH), CONV_WIDTH ** -0.5),
        "mlstm_b_i": nrm(ks[11], (DEPTH, MLSTM_HEADS), 0.1),
        "mlstm_b_f": jnp.linspace(3.0, 6.0, MLSTM_HEADS, dtype=f32)[None] + nrm(ks[12], (DEPTH, MLSTM_HEADS), 0.1),
        "mlstm_norm_w": 1.0 + nrm(ks[13], (DEPTH, MLSTM_V_WIDTH), 0.02),
        "w_proj_gdn": nrm(ks[14], (DEPTH, GDN_V_WIDTH, d), GDN_V_WIDTH ** -0.5),
        "w_proj_mlstm": nrm(ks[15], (DEPTH, MLSTM_V_WIDTH, d), MLSTM_V_WIDTH ** -0.5),
        "w_out": nrm(ks[16], (DEPTH, d, d), d ** -0.5),
        "norm_post_w": 1.0 + nrm(ks[17], (DEPTH, d), 0.02),
    }


def reference(x, c, w_ada, b_ada, norm_pre_w, w_in, gdn_conv_w, gdn_A_log, gdn_dt_bias, gdn_norm_w,
              mlstm_conv_w, mlstm_b_i, mlstm_b_f, mlstm_norm_w, w_proj_gdn, w_proj_mlstm, w_out, norm_post_w):
    for l in range(DEPTH):
        x = hybrid_layer(x, c, w_ada[l], b_ada[l], norm_pre_w[l], w_in[l], gdn_conv_w[l], gdn_A_log[l],
                         gdn_dt_bias[l], gdn_norm_w[l], mlstm_conv_w[l], mlstm_b_i[l], mlstm_b_f[l],
                         mlstm_norm_w[l], w_proj_gdn[l], w_proj_mlstm[l], w_out[l], norm_post_w[l])
    return x
```

```python
import math
from contextlib import ExitStack

import numpy as np
import concourse.bass as bass
import concourse.mybir as mybir
from concourse.bass_utils import run_bass_kernel_spmd

F32 = mybir.dt.float32
BF16 = mybir.dt.bfloat16
F32R = mybir.dt.float32r
AF = mybir.ActivationFunctionType
ALU = mybir.AluOpType
AX = mybir.AxisListType

D = 2048
KC = 16
SEQ = 8192
TT = 512
NB = 4
IN_W = 24656
EPS = 1e-6
NEG = -30000.0
GROUPS = [[0, 1, 2, 3], [4, 5, 6, 7]]

DEBUG = {"on": False, "nt": 16, "taps": [], "phase2": True}


class Buf:
    __slots__ = ("name", "w", "r", "psum")

    def __init__(self, name="", psum=False):
        self.name = name
        self.w = None
        self.r = {}
        self.psum = psum


def bufs(n, name=""):
    return [Buf(f"{name}{i}") for i in range(n)]


class Sched:
    ENG = ("pe", "act", "dve", "pool", "sp")

    def __init__(self, nc):
        self.nc = nc
        self.q = {e: [] for e in self.ENG}
        self.sem = {}
        self.cnt = {}
        self.waited = {}
        self.epoch = 0

    def new_epoch(self):
        self.epoch += 1

    def _sem(self, agent):
        if agent not in self.sem:
            self.sem[agent] = self.nc.alloc_semaphore(name="s_" + agent.replace("@", "_").replace(":", "_"))
            self.cnt[agent] = 0
        return self.sem[agent]

    def _deps(self, eng, reads, writes):
        deps = {}
        pre = eng + "@"
        for b in reads:
            if b.w is not None:
                a, c = b.w
                if b.psum and a.startswith(pre):
                    continue
                if deps.get(a, 0) < c:
                    deps[a] = c
        for b in writes:
            if b.w is not None:
                a, c = b.w
                if not (b.psum and a.startswith(pre)) and deps.get(a, 0) < c:
                    deps[a] = c
            for a, c in b.r.items():
                if b.psum and a.startswith(pre):
                    continue
                if deps.get(a, 0) < c:
                    deps[a] = c
        waits = []
        for a, c in deps.items():
            if eng == "pe" and a.startswith("pe@"):
                continue
            if self.waited.get((eng, a), 0) >= c:
                continue
            self.waited[(eng, a)] = c
            waits.append((self.sem[a], c))
        return waits

    def _post(self, agent, c, reads, writes):
        for b in reads:
            if b.r.get(agent, 0) < c:
                b.r[agent] = c
        for b in writes:
            b.w = (agent, c)
            b.r = {}

    def op(self, eng, fn, reads=(), writes=()):
        waits = self._deps(eng, reads, writes)
        agent = f"{eng}@{self.epoch}"
        sem = self._sem(agent)
        self.cnt[agent] += 1
        self._post(agent, self.cnt[agent], reads, writes)
        self.q[eng].append((waits, fn, sem, 1))

    def dma(self, eng, slot, fn, reads=(), writes=(), inc=16):
        waits = self._deps(eng, reads, writes)
        agent = "dma:" + slot
        sem = self._sem(agent)
        self.cnt[agent] += inc
        self._post(agent, self.cnt[agent], reads, writes)
        self.q[eng].append((waits, fn, sem, inc))

    def group_done(self, slot, bl):
        agent = "dma:" + slot
        for b in bl:
            b.w = (agent, self.cnt[agent])

    def barrier(self, exclude=()):
        snap = [(a, c) for a, c in self.cnt.items() if c > 0 and a not in exclude]
        for e in self.ENG:
            waits = []
            for a, c in snap:
                if self.waited.get((e, a), 0) >= c:
                    continue
                self.waited[(e, a)] = c
                waits.append((self.sem[a], c))
            if waits:
                self.q[e].append((waits, None, None, 0))

    def wait_all_dma(self, eng):
        waits = [(self.sem[a], c) for a, c in self.cnt.items() if a.startswith("dma:")]
        self.q[eng].append((waits, None, None, 0))

    def emit(self):
        names = {"pe": "tensor", "act": "scalar", "dve": "vector", "pool": "gpsimd", "sp": "sync"}
        with self.nc.Block() as block:
            for e in self.ENG:
                items = self.q[e]
                if not items:
                    continue

                def body(eng, items=items):
                    for waits, fn, sem, inc in items:
                        for s, v in waits:
                            eng.wait_ge(s, v)
                        if fn is not None:
                            fn(eng).then_inc(sem, inc)

                getattr(block, names[e])(body)


def make_consts():
    idx = np.arange(128)
    s = idx[:, None]
    t = idx[None, :]
    same = (s // 64) == (t // 64)
    c = np.zeros((128, 8, 128), np.float32)
    c[:, 0] = np.eye(128)
    c[:, 1] = ((s <= t) & same)
    c[:, 2] = same
    c[:, 3] = (s <= t)
    c[:, 4] = 1.0
    c[:, 5] = np.where((t < s) & same, 0.0, -NEG)
    c[:, 6] = np.where((t >= s) & same, 0.0, NEG)
    c[:, 7] = np.where(t >= s, 0.0, NEG)
    return c


C_ID, C_TRI64, C_BLK64, C_TRI128, C_ONES, C_MASKL, C_MASKU, C_MASKU128 = range(8)


class Ops:
    def __init__(self, S):
        self.S = S

    def tt(self, eng, out, in0, in1, op, reads, writes):
        self.S.op(eng, lambda e: e.tensor_tensor(out=out, in0=in0, in1=in1, op=op), reads, writes)

    def ts(self, eng, out, in0, s1, op0, reads, writes, s2=None, op1=None):
        if op1 is None:
            self.S.op(eng, lambda e: e.tensor_scalar(out=out, in0=in0, scalar1=s1, scalar2=None, op0=op0), reads, writes)
        else:
            self.S.op(eng, lambda e: e.tensor_scalar(out=out, in0=in0, scalar1=s1, scalar2=s2, op0=op0, op1=op1), reads, writes)

    def stt(self, out, in0, scalar, in1, op0, op1, reads, writes):
        self.S.op("dve", lambda e: e.scalar_tensor_tensor(out=out, in0=in0, scalar=scalar, in1=in1, op0=op0, op1=op1), reads, writes)

    def act(self, out, in_, func, reads, writes, bias=None, scale=None, accum=None):
        kw = {}
        if bias is not None:
            kw["bias"] = bias
        if scale is not None:
            kw["scale"] = scale
        if accum is not None:
            kw["accum_out"] = accum
        self.S.op("act", lambda e: e.activation(out=out, in_=in_, func=func, **kw), reads, writes)

    def cp(self, eng, out, in_, reads, writes):
        self.S.op(eng, lambda e: e.tensor_copy(out=out, in_=in_), reads, writes)

    def rec(self, out, in_, reads, writes):
        self.S.op("dve", lambda e: e.reciprocal(out=out, in_=in_), reads, writes)

    def red(self, out, in_, reads, writes):
        self.S.op("dve", lambda e: e.tensor_reduce(out=out, in_=in_, axis=AX.X, op=ALU.add), reads, writes)

    def memset(self, eng, out, val, writes):
        self.S.op(eng, lambda e: e.memset(out, val), (), writes)

    def mm(self, out, lhsT, rhs, reads, writes):
        self.S.op("pe", lambda e: e.matmul(out, lhsT=lhsT, rhs=rhs, start=True, stop=True), reads, writes)

    def mmg(self, out, pairs, reads, writes):
        pairs = list(pairs)

        def fn(e):
            n = len(pairs)
            for i, (l, r) in enumerate(pairs):
                ins = e.matmul(out, lhsT=l, rhs=r, start=(i == 0), stop=(i == n - 1))
            return ins
        self.S.op("pe", fn, reads, writes)

    def tr(self, out, in_, ident, reads, writes):
        self.S.op("pe", lambda e: e.transpose(out=out, in_=in_, identity=ident), reads, writes)

    def dma(self, eng, slot, out, in_, reads, writes):
        self.S.dma(eng, slot, lambda e: e.dma_start(out=out, in_=in_), reads, writes)


def RR(ap):
    return ap


def bc_mid(ap, n):
    return ap.unsqueeze(1).broadcast_to([128, n, ap.shape[-1]])


def bc_last(ap, n):
    return ap.unsqueeze(2).broadcast_to([128, ap.shape[-1], n])


class Bank:
    def __init__(self, t, i):
        self.t = t
        self.b = Buf(f"bank{i}", psum=True)
        self.B = [self.b]

    def s4(self):
        return self.t[:].rearrange("p (s d) -> p s d", s=4)

    def flat(self):
        return self.t[:]


def build_program():
    nt_run = DEBUG["nt"]
    nc = bass.Bass("TRN2", target_bir_lowering=False)
    S = Sched(nc)
    O = Ops(S)

    def din(name, shape, dt=F32):
        return nc.dram_tensor(name, list(shape), dt, kind="ExternalInput").ap()

    x_full = din("x_full", [SEQ, D])
    x_own = din("x_own", [2048, D])
    c_t = din("c_t", [128, KC])
    w_ada = din("w_ada", [D, 3 * D])
    b_ada_t = din("b_ada_t", [128, 32])
    b_gate_row = din("b_gate_row", [1, D])
    normw_t = din("normw_t", [128, KC])
    w_fm = din("w_fm", [20, 128, KC, 128])
    w_tm = din("w_tm", [10, 128, KC, 256])
    w_sm = din("w_sm", [128, KC, 20])
    w_gate = din("w_gate", [32, 128, KC, 128])
    convw_d = din("convw", [128, 20, 4])
    alog_d = din("alog", [1, 8])
    dtb_d = din("dtb", [1, 8])
    gnw_d = din("gnw", [1, 128])
    mbi_d = din("mbi", [1, 2])
    mbf_d = din("mbf", [1, 2])
    mnw_d = din("mnw", [1, 512])
    wpa = din("wpa", [16, 128, 32, 128])
    wpb = din("wpb", [16, 128, 16, 128])
    wout = din("wout", [8, 128, KC, 256])
    npw_d = din("npw", [1, D])
    consts_d = din("consts", [128, 8, 128])
    y_out = nc.dram_tensor("y_out", [2048, D], F32, kind="ExternalOutput").ap()

    wfm_b = nc.dram_tensor("wfm_b", [20, 128, KC, 128], BF16).ap()
    wtm_b = nc.dram_tensor("wtm_b", [10, 128, KC, 256], BF16).ap()
    wsm_b = nc.dram_tensor("wsm_b", [128, KC, 20], BF16).ap()
    wgate_b = nc.dram_tensor("wgate_b", [32, 128, KC, 128], BF16).ap()
    wpa_b = nc.dram_tensor("wpa_b", [16, 128, 32, 128], BF16).ap()
    wpb_b = nc.dram_tensor("wpb_b", [16, 128, 16, 128], BF16).ap()
    wout_b = nc.dram_tensor("wout_b", [8, 128, KC, 256], BF16).ap()
    gp_d = nc.dram_tensor("gp_d", [128, D], F32).ap()
    xsrc = nc.dram_tensor("xsrc", [SEQ, 1536], BF16).ap()
    xdst = nc.dram_tensor("xdst", [4 * SEQ, 1536], BF16).ap()

    taps = DEBUG["taps"]

    def tap(name, ap, shape, reads):
        if not DEBUG["on"]:
            return
        o = nc.dram_tensor("tap_" + name, list(shape), F32, kind="ExternalOutput").ap()
        O.dma("sp", "tap_" + name, o, ap, reads, [Buf()])
        taps.append(name)

    es = ExitStack()
    with es:
        def sb(name, shape, dt=F32):
            return es.enter_context(nc.sbuf_tensor(name, list(shape), dt))

        banks = [Bank(es.enter_context(nc.psum_tensor(f"bank{i}", [128, 512], F32)), i) for i in range(8)]
        bctr = [0]

        def job():
            bk = banks[bctr[0] % 8]
            bctr[0] += 1
            return bk

        b_wfm = bufs(20, "wfm")
        b_wtm = bufs(10, "wtm")
        b_wsm = Buf("wsm")
        b_wgate = bufs(32, "wg")
        b_wpa = bufs(16, "wpa")
        b_wpb = bufs(16, "wpb")
        b_wout = bufs(8, "wo")

        def cast(dst, src, lo, hi, bl, slot):
            O.dma("pool", slot, dst[lo:hi], src[lo:hi], (), bl[lo:hi])

        cast_jobs = [(f"c1f{i}", wfm_b, w_fm, i, i + 4, b_wfm) for i in range(0, 20, 4)]
        cast_jobs += [("c1s", wsm_b, w_sm, None, None, [b_wsm])]
        cast_jobs += [(f"c1t{i}", wtm_b, w_tm, i, i + 2, b_wtm) for i in range(0, 10, 2)]
        CAST_AGENTS = tuple("dma:" + j[0] for j in cast_jobs)

        def issue_cast(after):
            if not cast_jobs:
                return
            slot, dst, src_, lo, hi, bl = cast_jobs.pop(0)
            if lo is None:
                O.dma("pool", slot, dst, src_, after, bl)
            else:
                O.dma("pool", slot, dst[lo:hi], src_[lo:hi], after, bl[lo:hi])
        cst = sb("cst", [128, 8, 128])
        b_cst = Buf("cst")
        O.dma("sp", "cst", cst[:], consts_d, (), [b_cst])
        ident = cst[:, C_ID, :]
        ones = cst[:, C_ONES, :]
        identb = sb("identb", [128, 128], BF16)
        b_identb = Buf("identb")
        O.cp("dve", identb[:], ident, [b_cst], [b_identb])

        prm = sb("prm", [128, 256])
        b_prm = Buf("prm")
        o_ct, o_bada, o_nw, o_alog, o_dtb, o_bi, o_bf = 0, 16, 48, 64, 72, 80, 82
        o_csil, o_ada, o_G, o_negA = 96, 112, 160, 176
        O.dma("sp", "prm", prm[:, o_ct:o_ct + 16], c_t, (), [b_prm])
        O.dma("sp", "prm", prm[:, o_bada:o_bada + 32], b_ada_t, (), [b_prm])
        O.dma("sp", "prm", prm[:, o_nw:o_nw + 16], normw_t, (), [b_prm])
        O.dma("sp", "prm", prm[:, o_alog:o_alog + 8], alog_d.broadcast_to([128, 8]), (), [b_prm])
        O.dma("sp", "prm", prm[:, o_dtb:o_dtb + 8], dtb_d.broadcast_to([128, 8]), (), [b_prm])
        O.dma("sp", "prm", prm[:, o_bi:o_bi + 2], mbi_d.broadcast_to([128, 2]), (), [b_prm])
        O.dma("sp", "prm", prm[:, o_bf:o_bf + 2], mbf_d.broadcast_to([128, 2]), (), [b_prm])
        convw = sb("convw_sb", [128, 20, 4])
        b_convw = Buf("convw")
        O.dma("sp", "prm", convw[:], convw_d, (), [b_convw])
        gnw = sb("gnw_sb", [128, 128])
        mnw = sb("mnw_sb", [128, 512])
        b_gnw, b_mnw = Buf(), Buf()
        O.dma("sp", "prm", gnw[:], gnw_d.broadcast_to([128, 128]), (), [b_gnw])
        O.dma("sp", "prm", mnw[:], mnw_d.broadcast_to([128, 512]), (), [b_mnw])

        S.group_done("prm", [b_prm, b_convw, b_gnw, b_mnw])
        dtb = prm[:, o_dtb:o_dtb + 8]
        negA = prm[:, o_negA:o_negA + 8]
        csil = prm[:, o_csil:o_csil + 16]
        Gs = prm[:, o_G:o_G + 16]
        shift = prm[:, o_ada:o_ada + 16]

        O.act(csil, prm[:, o_ct:o_ct + 16], AF.Silu, [b_prm], [b_prm])
        O.act(negA, prm[:, o_alog:o_alog + 8], AF.Exp, [b_prm], [b_prm])
        O.ts("dve", negA, negA, -1.0, ALU.mult, [b_prm], [b_prm])
        with ExitStack() as es0:
            wada_t = [es0.enter_context(nc.sbuf_tensor(f"wada{i}", [128, KC, 512], F32)) for i in range(2)]
            b_wada = bufs(2, "wada")
            w_ada_v = w_ada.rearrange("(k p) n -> p k n", p=128)
            J = job()
            for pc in range(8):
                sl = pc % 2
                O.dma("sp", f"wada{sl}", wada_t[sl][:], w_ada_v[:, :, pc * 512:(pc + 1) * 512], (), [b_wada[sl]])
                issue_cast([b_wada[sl]])
                if pc >= 5:
                    issue_cast([b_wada[sl]])
                for jj in range(4):
                    j = pc * 4 + jj
                    O.mmg(J.t[:, j:j + 1], [(wada_t[sl][:, k, jj * 128:(jj + 1) * 128], csil[:, k:k + 1]) for k in range(KC)],
                          [b_wada[sl], b_prm], J.B)
            O.tt("dve", prm[:, o_ada:o_ada + 32], J.t[:, 0:32], prm[:, o_bada:o_bada + 32], ALU.add, [b_prm], [b_prm] + J.B)
            O.stt(Gs, prm[:, o_ada + 16:o_ada + 32], 1.0, prm[:, o_nw:o_nw + 16], ALU.add, ALU.mult, [b_prm], [b_prm])
            b_gpd = Buf("gpd")
            if DEBUG["phase2"]:
                csr = es0.enter_context(nc.sbuf_tensor("csr", [128, KC, 128], F32))
                rows = es0.enter_context(nc.sbuf_tensor("rows", [128, 2, D], F32))
                GPt = es0.enter_context(nc.sbuf_tensor("GPt", [128, D], F32))
                b_csr, b_rows, b_GPt = Buf(), Buf(), Buf()
                O.dma("sp", "rw", rows[:, 0, :], b_gate_row.broadcast_to([128, D]), (), [b_rows])
                O.dma("sp", "rw", rows[:, 1, :], npw_d.broadcast_to([128, D]), (), [b_rows])
                O.cp("dve", csr[:], bc_last(csil, 128), [b_prm], [b_csr])
                for pc in range(4):
                    sl = pc % 2
                    O.dma("sp", f"wada{sl}", wada_t[sl][:], w_ada_v[:, :, 4096 + pc * 512:4096 + (pc + 1) * 512], (), [b_wada[sl]])
                    Jg = job()
                    O.mmg(Jg.t[:], [(csr[:, k, :], wada_t[sl][:, k, :]) for k in range(KC)], [b_csr, b_wada[sl]], Jg.B)
                    gsl = GPt[:, pc * 512:(pc + 1) * 512]
                    O.tt("dve", gsl, Jg.t[:], rows[:, 0, pc * 512:(pc + 1) * 512], ALU.add, [b_rows], [b_GPt] + Jg.B)
                    O.tt("dve", gsl, gsl, rows[:, 1, pc * 512:(pc + 1) * 512], ALU.mult, [b_GPt, b_rows], [b_GPt])
                O.dma("sp", "gpd", gp_d, GPt[:], [b_GPt], [b_gpd])
        while cast_jobs:
            issue_cast(())
        tap("G", Gs, [128, 16], [b_prm])
        tap("shift", shift, [128, 16], [b_prm])
        S.barrier(exclude=CAST_AGENTS)

        es1 = ExitStack()
        es1.__enter__()

        def sb1(name, shape, dt=F32):
            return es1.enter_context(nc.sbuf_tensor(name, list(shape), dt))

        xt = [sb1(f"xt{i}", [128, D]) for i in range(1)]
        b_xt = bufs(1, "xt")
        junk = sb1("junk", [128, 1024])
        b_junk = Buf("junk")
        junk_bf = junk[:].bitcast(BF16)
        st = sb1("st", [128, 64])
        b_st = Buf("st")
        hT = sb1("hT", [128, KC, TT], BF16)
        b_hT = bufs(NB, "hT")
        NWR = 3
        wr = [sb1(f"wr{i}", [128, 4096], BF16) for i in range(NWR)]
        b_wr = bufs(NWR, "wr")
        qn = sb1("qn", [128, 4, TT])
        kn = sb1("kn", [128, 4, TT])
        b_qn = bufs(4, "qn")
        b_kn = bufs(4, "kn")
        mq = sb1("mq", [128, 2, TT])
        mk = sb1("mk", [128, 2, TT])
        b_mq = bufs(2, "mq")
        b_mk = bufs(2, "mk")
        cpool = sb1("cpool", [128, 2 * TT + 3 * (TT + 4)])
        cacc = [cpool[:, i * TT:(i + 1) * TT] for i in range(2)]
        cw = [cpool[:, 2 * TT + i * (TT + 4):2 * TT + i * (TT + 4) + TT + 3] for i in range(3)]
        b_cw = bufs(3, "cw")
        b_cacc = bufs(2, "cacc")
        xt2 = cpool[:, 0:D]
        b_xt2 = b_cacc + b_cw[0:2]
        halo = sb1("halo", [128, 20, 3])
        b_halo = bufs(20, "halo")
        vtm = sb1("vtm", [128, NB, 8, 128])
        b_vtm = [[Buf() for _ in range(8)] for _ in range(NB)]
        ktm = [sb1(f"ktm{i}", [128, 4, 128]) for i in range(2)]
        b_ktm = bufs(2, "ktm")
        mktm = sb1("mktm", [128, 2, 128])
        b_mktm = Buf("mktm")
        zsg = sb1("zsg", [128, NB, 1024], BF16)
        b_zsg = [[Buf() for _ in range(4)] for _ in range(NB)]
        Vp = sb1("Vp", [128, NB, 2, 257])
        b_Vp = [[Buf() for _ in range(2)] for _ in range(NB)]
        mgw = sb1("mgw", [128, NB, 512], BF16)
        b_mgw = [[Buf() for _ in range(2)] for _ in range(NB)]
        gt = sb1("gt", [128, NB, 112])
        b_gt = Buf("gt")

        class GS:
            def __init__(self, i):
                def t(n, dt=F32):
                    return sb1(f"g{n}{i}", [128, 2, 128], dt), Buf(f"g{n}{i}")
                self.A, self.bA = t("A")
                X0 = sb1(f"gX0{i}", [128, 2, 2, 128])
                X1 = sb1(f"gX1{i}", [128, 2, 2, 128])
                self.X = [X0, X1]
                self.Gp1, self.bGp1 = X0[:, :, 0, :], Buf(f"gGp1{i}")
                self.Ap0, self.bAp0 = X0[:, :, 1, :], Buf(f"gAp0{i}")
                self.Gp0, self.bGp0 = X1[:, :, 0, :], Buf(f"gGp0{i}")
                self.Ap1, self.bAp1 = X1[:, :, 1, :], Buf(f"gAp1{i}")
                self.Lp1, self.bLp1 = t("Lp1")
                self.WTb, self.bWTb = t("WTb", BF16)
                self.qdTb, self.bqdTb = t("qdTb", BF16)
                self.aTb, self.baTb = t("aTb", BF16)
                self.kttb, self.bkttb = t("kttb", BF16)
                self.vnb, self.bvnb = t("vnb", BF16)
                self.egs = sb1(f"gegs{i}", [128, 4])
                self.begs = Buf(f"gegs{i}")
        NS = 4
        gss = [GS(i) for i in range(NS)]
        Sst = sb1("Sst", [128, 8, 128])
        b_S = bufs(8, "S")
        Cst = sb1("Cst", [128, 2, 257])
        b_C = bufs(2, "C")
        otm = [sb1(f"otm{i}", [128, 8, 128]) for i in range(2)]
        b_otm = [bufs(4, f"otm{i}") for i in range(2)]
        og = [sb1(f"og{i}", [128, 1536], BF16) for i in range(2)]
        b_og = [[Buf(), Buf()] for _ in range(2)]
        gamT = [sb1(f"gamT{i}", [8, 128]) for i in range(2)]
        b_gamT = bufs(2, "gamT")
        bTt = [sb1(f"bTt{i}", [2, 128]) for i in range(2)]
        b_bTt = bufs(2, "bTt")
        T1 = sb1("T1s", [128, 2, 128])
        T2 = sb1("T2s", [128, 2, 128])
        T3 = sb1("T3s", [128, 2, 128])
        T4 = sb1("T4s", [128, 2, 128])
        b_T1, b_T2, b_T3, b_T4 = Buf("T1"), Buf("T2"), Buf("T3"), Buf("T4")
        Sb = sb1("Sb", [128, 8, 128], BF16)
        b_Sb = bufs(8, "Sb")
        mtmpU, b_mtmpU_alias = T2, b_T2
        sTm = sb1("sTm", [128, 2, 128])
        Dm = sb1("Dm", [128, 2, 128])
        Eb = sb1("Eb", [128, 2, 128])
        qbT = sb1("qbT", [128, 2, 128])
        kwm = sb1("kwm", [128, 2, 128])
        hbuf = sb1("hbuf", [128, 1, 256])
        b_sTm, b_Dm, b_Eb, b_qbT, b_kwm, b_hbuf = bufs(6, "mw")
        b_mtmpU = b_T2

        b_xsrc = bufs(SEQ // 128, "xsrc")
        b_ag = bufs(32, "ag")
        O.memset("pool", Sst[:], 0.0, b_S)
        O.memset("pool", Sb[:], 0.0, b_Sb)
        O.memset("pool", Cst[:], 0.0, b_C)
        O.memset("pool", halo[:], 0.0, b_halo)
        O.memset("pool", Vp[:], 1.0, [b for r in b_Vp for b in r])
        O.memset("pool", gt[:], 0.0, [b_gt])
        if DEBUG["on"] and nt_run < 16 and DEBUG["phase2"]:
            O.memset("pool", og[0][:], 0.0, b_og[0])
            for r_ in range(nt_run * NB, SEQ // 128):
                O.dma("sp", "zf", xsrc[r_ * 128:(r_ + 1) * 128, :], og[0][:], b_og[0], [b_xsrc[r_]])
            S.group_done("zf", b_xsrc[nt_run * NB:])

        LN_SQ = math.log(128.0 ** -0.5)
        wctr = [0]
        def issue_ag(c):
            S.dma("pool", f"cc{c % 4}", lambda e: e.collective_compute(
                "AllGather", ALU.bypass, replica_groups=GROUPS, ins=[xsrc[c * 256:(c + 1) * 256, :]], outs=[xdst[c * 1024:(c + 1) * 1024, :]]),
                [b_xsrc[2 * c], b_xsrc[2 * c + 1]], [b_ag[c]], inc=1)

        def load_w(src_ap, n, pattern, dims, src_bufs):
            sl = wctr[0] % NWR
            wctr[0] += 1
            view = wr[sl][:, 0:n].rearrange(pattern, **dims)
            O.dma("sp", f"wr{sl}", view, src_ap, src_bufs, [b_wr[sl]])
            return sl, view

        def stage_a(src, row0, blk, xslot):
            if blk % 2 == 0:
                xtile, bxl = xt[0][:], [b_xt[0]]
            else:
                xtile, bxl = xt2, b_xt2
            so = 3 * (blk % 2)
            O.dma("sp", f"xt{blk % 2}", xtile, src[row0:row0 + 128, :], (), bxl)
            O.act(junk_bf, xtile, AF.Square, bxl, [b_junk, b_st], accum=st[:, so:so + 1])
            O.act(st[:, so + 1:so + 2], st[:, so:so + 1], AF.Sqrt, [b_st], [b_st], bias=EPS, scale=1.0 / D)
            O.rec(st[:, so + 2:so + 3], st[:, so + 1:so + 2], [b_st], [b_st])
            O.ts("dve", xtile, xtile, st[:, so + 2:so + 3], ALU.mult, bxl + [b_st], bxl)
            for kq in range(4):
                J = job()
                for s_ in range(4):
                    k = kq * 4 + s_
                    O.tr(J.s4()[:, s_, :], xtile[:, k * 128:(k + 1) * 128], ident, bxl + [b_cst], J.B)
                for s_ in range(4):
                    k = kq * 4 + s_
                    if kq % 2 == 0:
                        O.act(hT[:, k, blk * 128:(blk + 1) * 128], J.s4()[:, s_, :], AF.Identity, [b_prm], [b_hT[blk]] + J.B,
                              bias=shift[:, k:k + 1], scale=Gs[:, k:k + 1])
                    else:
                        O.ts("dve", hT[:, k, blk * 128:(blk + 1) * 128], J.s4()[:, s_, :], Gs[:, k:k + 1], ALU.mult, [b_prm], [b_hT[blk]] + J.B,
                             s2=shift[:, k:k + 1], op1=ALU.add)

        for ti in range(nt_run):
            if ti % 4 == 0:
                S.new_epoch()
            tok0 = ti * TT
            for blk in range(NB):
                stage_a(x_full, tok0 + blk * 128, blk, 0)

            pending = []

            def flush(upto=10 ** 9):
                while pending and pending[0][0] <= upto:
                    pending.pop(0)[1]()

            part2 = [None]

            for jp in range(10):
                sl, wv = load_w(wfm_b[2 * jp:2 * jp + 2].rearrange("j p k c -> p j k c"), 4096, "p (j k c) -> p j k c", dict(j=2, k=KC), b_wfm[2 * jp:2 * jp + 2])
                for jj in range(2):
                    j = 2 * jp + jj
                    J = job()
                    O.mmg(J.t[:], [(wv[:, jj, k, :], hT[:, k, :]) for k in range(KC)], [b_wr[sl]] + b_hT, J.B)
                    flush(j - 2)
                    c_, bc = cw[j % 3], b_cw[j % 3]
                    ac, bac = cacc[j % 2], b_cacc[j % 2]
                    O.act(c_[:, 3:TT + 3], J.t[:], AF.Copy, [], [bc] + J.B)
                    O.cp("pool", c_[:, 0:3], halo[:, j, :], [b_halo[j]], [bc])
                    O.ts("dve", ac[:], c_[:, 3:TT + 3], convw[:, j, 3:4], ALU.mult, [bc, b_convw], [bac])
                    for d_ in (1, 2, 3):
                        O.stt(ac[:], c_[:, 3 - d_:TT + 3 - d_], convw[:, j, 3 - d_:4 - d_], ac[:], ALU.mult, ALU.add, [bc, b_convw, bac], [bac])
                    O.cp("pool", halo[:, j, :], c_[:, TT:TT + 3], [bc], [b_halo[j]])
                    if part2[0] is not None:
                        part2[0]()
                        part2[0] = None

                    def p2(j=j, c_=c_, bc=bc, ac=ac, bac=bac):
                        if j < 8:
                            isq = j < 4
                            hq = j % 4
                            dst, bd = (qn, b_qn[hq]) if isq else (kn, b_kn[hq])
                            O.act(dst[:, hq, :], ac[:], AF.Silu, [bac], [bd])
                            O.tt("pool", c_[:, 0:TT], dst[:, hq, :], dst[:, hq, :], ALU.mult, [bd], [bc])

                            def tail():
                                J2 = job()
                                O.mm(J2.t[:], ones, c_[:, 0:TT], [bc, b_cst], J2.B)
                                bx_ = Buf()
                                O.act(J2.t[:], J2.t[:], AF.Ln, [], J2.B + [bx_], bias=EPS, scale=1.0)
                                O.act(J2.t[:], J2.t[:], AF.Exp, [bx_], J2.B, bias=(LN_SQ if isq else 0.0), scale=-0.5)
                                O.tt("dve", dst[:, hq, :], dst[:, hq, :], J2.t[:], ALU.mult, [], [bd] + J2.B)
                            pending.append((j, tail))
                        elif j < 16:
                            hv = j - 8
                            O.act(c_[:, 0:TT], ac[:], AF.Silu, [bac], [bc])

                            def tail():
                                J2 = job()
                                for blk in range(NB):
                                    O.tr(J2.s4()[:, blk, :], c_[:, blk * 128:(blk + 1) * 128], ident, [bc, b_cst], J2.B)
                                O.cp("dve", vtm[:, :, hv, :], J2.s4(), [], [b_vtm[blk][hv] for blk in range(NB)] + J2.B)
                            pending.append((j, tail))
                        else:
                            h_ = j % 2
                            dst, bd = (mq, b_mq[h_]) if j < 18 else (mk, b_mk[h_])
                            O.act(dst[:, h_, :], ac[:], AF.Silu, [bac], [bd])
                    part2[0] = p2
            part2[0]()

            sl, wv = load_w(wsm_b, KC * 20, "p (k c) -> p k c", dict(k=KC), [b_wsm])
            J = job()
            for blk in range(NB):
                O.mmg(J.t[:, blk * 20:(blk + 1) * 20], [(hT[:, k, blk * 128:(blk + 1) * 128], wv[:, k, :]) for k in range(KC)],
                      [b_wr[sl], b_hT[blk]], J.B)
            flush()

            def G_(a, b_):
                return gt[:, :, a:b_]
            RG, WG = [b_gt], [b_gt]
            O.act(G_(0, 20), J.t[:, 0:80].rearrange("p (b c) -> p b c", b=NB), AF.Copy, [], WG + J.B)
            O.tt("dve", G_(20, 28), G_(0, 8), bc_mid(dtb, NB), ALU.add, [b_gt, b_prm], WG)
            O.act(G_(20, 28), G_(20, 28), AF.Exp, RG, WG)
            O.act(G_(20, 28), G_(20, 28), AF.Ln, RG, WG, bias=1.0, scale=1.0)
            O.tt("dve", G_(20, 28), G_(20, 28), bc_mid(negA, NB), ALU.mult, [b_gt, b_prm], WG)
            O.act(G_(28, 36), G_(8, 16), AF.Sigmoid, RG, WG)
            O.tt("dve", G_(36, 38), G_(16, 18), bc_mid(prm[:, o_bi:o_bi + 2], NB), ALU.add, [b_gt, b_prm], WG)
            O.tt("dve", G_(38, 40), G_(18, 20), bc_mid(prm[:, o_bf:o_bf + 2], NB), ALU.add, [b_gt, b_prm], WG)
            O.act(G_(38, 40), G_(38, 40), AF.Exp, RG, WG, scale=-1.0)
            O.act(G_(38, 40), G_(38, 40), AF.Ln, RG, WG, bias=1.0, scale=1.0)
            O.ts("dve", G_(38, 40), G_(38, 40), -1.0, ALU.mult, RG, WG)

            def stage_d2():
                J = job()
                for blk in range(NB):
                    for (cid, a, b_, o_) in ((C_TRI64, 20, 28, 0), (C_BLK64, 20, 28, 8), (C_TRI128, 38, 40, 16), (C_ONES, 38, 40, 18)):
                        base = blk * 20 + o_
                        O.mm(J.t[:, base:base + (b_ - a)], cst[:, cid, :], gt[:, blk, a:b_], [b_gt, b_cst], J.B)
                O.act(G_(40, 60), J.t[:, 0:80].rearrange("p (b c) -> p b c", b=NB), AF.Copy, [], WG + J.B)
                O.ts("dve", G_(60, 68), G_(40, 48), -1.0, ALU.mult, RG, WG)
                O.act(G_(96, 104), G_(40, 48), AF.Exp, RG, WG)
                O.tt("dve", G_(68, 76), G_(96, 104), G_(28, 36), ALU.mult, RG, WG)
                O.tt("dve", G_(76, 84), G_(48, 56), G_(40, 48), ALU.subtract, RG, WG)
                O.act(G_(76, 84), G_(76, 84), AF.Exp, RG, WG)
                O.tt("dve", G_(104, 106), G_(36, 38), G_(56, 58), ALU.subtract, RG, WG)
                O.ts("dve", G_(84, 86), G_(104, 106), LN_SQ, ALU.add, RG, WG)
                O.tt("dve", G_(86, 88), G_(104, 106), G_(58, 60), ALU.add, RG, WG)
                O.act(G_(86, 88), G_(86, 88), AF.Exp, RG, WG)
                O.act(G_(88, 90), G_(58, 60), AF.Exp, RG, WG)

            for pc in range(10):
                sl, wv = load_w(wtm_b[pc], 4096, "p (k c) -> p k c", dict(k=KC), [b_wtm[pc]])
                tg, hf = pc // 2, pc % 2
                for blk in range(NB):
                    J = job()
                    pa = J.t[:, 0:256]
                    O.mmg(pa, [(hT[:, k, blk * 128:(blk + 1) * 128], wv[:, k, :]) for k in range(KC)], [b_wr[sl], b_hT[blk]], J.B)
                    if pc == 1 and blk == 0:
                        stage_d2()
                    if tg < 2:
                        c0 = pc * 256
                        bz = b_zsg[blk][pc]
                        zv = zsg[:, blk, c0:c0 + 256]
                        O.act(zv, pa, AF.Silu, [], [bz] + J.B)
                        zv3 = zv.rearrange("p (h d) -> p h d", h=2)
                        O.tt("pool", zv3, zv3, bc_mid(gnw[:], 2), ALU.mult, [bz, b_gnw], [bz])
                    elif tg == 2:
                        O.act(Vp[:, blk, hf, 0:256], pa, AF.Copy, [], [b_Vp[blk][hf]] + J.B)
                    elif tg == 3:
                        mv_ = mgw[:, blk, hf * 256:(hf + 1) * 256]
                        O.act(mv_, pa, AF.Sigmoid, [], [b_mgw[blk][hf]] + J.B)
                        O.tt("pool", mv_, mv_, mnw[:, hf * 256:(hf + 1) * 256], ALU.mult, [b_mgw[blk][hf], b_mnw], [b_mgw[blk][hf]])
                    else:
                        ac, bac = cacc[blk % 2], b_cacc[blk % 2]
                        mv_ = mgw[:, blk, hf * 256:(hf + 1) * 256]
                        O.act(ac[:, 0:256], pa, AF.Silu, [], [bac] + J.B)
                        O.tt("dve", mv_, mv_, ac[:, 0:256], ALU.mult, [bac, b_mgw[blk][hf]], [b_mgw[blk][hf]])
            if ti >= 1 and DEBUG["phase2"]:
                issue_ag(2 * (ti - 1))
                issue_ag(2 * (ti - 1) + 1)
            if ti == min(1, nt_run - 1) and DEBUG["phase2"]:
                for i in range(0, 32, 8):
                    cast(wgate_b, w_gate, i, i + 8, b_wgate, "c2")
                for i in range(0, 16, 4):
                    cast(wpa_b, wpa, i, i + 4, b_wpa, "c2")
                for i in range(0, 16, 8):
                    cast(wpb_b, wpb, i, i + 8, b_wpb, "c2")
                for i in range(0, 8, 4):
                    cast(wout_b, wout, i, i + 4, b_wout, "c2")
                S.group_done("c2", b_wgate + b_wpa + b_wpb + b_wout)
            if ti == 0:
                tap("gt", gt[:].rearrange("p b c -> p (b c)"), [128, NB * 112], [b_gt])
                tap("qn", qn[:].rearrange("p h t -> p (h t)"), [128, 4 * TT], b_qn)
                tap("kn", kn[:].rearrange("p h t -> p (h t)"), [128, 4 * TT], b_kn)
                tap("mq", mq[:].rearrange("p h t -> p (h t)"), [128, 2 * TT], b_mq)
                tap("vtm", vtm[:].rearrange("p b h d -> p (b h d)"), [128, NB * 8 * 128], [b for r in b_vtm for b in r])

            def emit_gamT(blk):
                J = job()
                O.mm(J.t[0:8, 0:128], gt[:, blk, 20:28], cst[:, C_TRI64, :], [b_gt, b_cst], J.B)
                O.act(gamT[blk % 2][:], J.t[0:8, 0:128], AF.Copy, [], [b_gamT[blk % 2]] + J.B)

            def emit_bT(blk):
                J = job()
                O.mm(J.t[0:2, 0:128], gt[:, blk, 38:40], cst[:, C_TRI128, :], [b_gt, b_cst], J.B)
                O.act(bTt[blk % 2][:], J.t[0:2, 0:128], AF.Copy, [], [b_bTt[blk % 2]] + J.B)

            emit_gamT(0)
            emit_bT(0)

            def pair_gen(blk, hq, g):
                tsl = slice(blk * 128, (blk + 1) * 128)
                hv0 = 2 * hq
                kt_, bkt = ktm[blk % 2], b_ktm[blk % 2]
                ot_, bot = otm[blk % 2], b_otm[blk % 2][hq]

                def gc(a):
                    return gt[:, blk, a + hv0:a + hv0 + 2]
                if hq == 0:
                    J = job()
                    for h4 in range(4):
                        O.tr(J.s4()[:, h4, :], kn[:, h4, tsl], ident, [b_kn[h4], b_cst], J.B)
                    O.act(kt_[:], J.s4(), AF.Copy, [], [bkt] + J.B)
                Jk = job()
                O.mm(Jk.s4()[:, 0, :], kn[:, hq, tsl], kn[:, hq, tsl], [b_kn[hq]], Jk.B)
                O.mm(Jk.s4()[:, 1, :], kn[:, hq, tsl], qn[:, hq, tsl], [b_kn[hq], b_qn[hq]], Jk.B)
                if hq == 0 and blk + 1 < NB:
                    emit_gamT(blk + 1)
                Jr = job()
                prow = Jr.s4()[:, 0:2, :]
                for e_ in range(2):
                    hv = hv0 + e_
                    O.mm(Jr.s4()[:, e_, :], cst[0:8, C_ID, hv:hv + 1].broadcast_to([8, 128]), gamT[blk % 2][:], [b_gamT[blk % 2], b_cst], Jr.B)
                O.tt("dve", T1[:], prow, bc_mid(cst[:, C_MASKL, :], 2), ALU.add, [b_cst], [b_T1] + Jr.B)
                O.tt("dve", T2[:], prow, bc_mid(cst[:, C_MASKU, :], 2), ALU.add, [b_cst], [b_T2] + Jr.B)
                O.act(T4[:], prow, AF.Exp, [], [b_T4] + Jr.B)
                for e_ in range(2):
                    hv = hv0 + e_
                    O.act(g.A[:, e_, :], T1[:, e_, :], AF.Exp, [b_gt, b_T1], [g.bA], bias=gt[:, blk, 40 + hv:41 + hv], scale=-1.0)
                    O.act(T3[:, e_, :], T2[:, e_, :], AF.Exp, [b_gt, b_T2], [b_T3], bias=gt[:, blk, 60 + hv:61 + hv], scale=1.0)
                for e_ in range(2):
                    hv = hv0 + e_
                    O.stt(g.A[:, e_, :], Jk.s4()[:, 0, :], gt[:, blk, 28 + hv:29 + hv], g.A[:, e_, :], ALU.mult, ALU.mult, [b_gt], [g.bA] + Jk.B)
                O.tt("dve", g.aTb[:], bc_mid(Jk.s4()[:, 1, :], 2), T3[:], ALU.mult, [b_T3], [g.baTb] + Jk.B)
                O.cp("pool", g.egs[:, 0:2], T4[:, :, 63], [b_T4], [g.begs])
                O.cp("pool", g.egs[:, 2:4], T4[:, :, 127], [b_T4], [g.begs])
                O.tt("dve", g.qdTb[:], bc_mid(qn[:, hq, tsl], 2), T4[:], ALU.mult, [b_qn[hq], b_T4], [g.bqdTb])
                O.tt("pool", g.kttb[:], bc_mid(kt_[:, hq, :], 2), bc_last(gc(76), 128), ALU.mult, [bkt, b_gt], [g.bkttb])
                yield
                Ja = job()
                for e_ in range(2):
                    O.tr(Ja.s4()[:, e_, :], g.A[:, e_, :], ident, [g.bA, b_cst], Ja.B)
                O.act(g.Ap0[:], Ja.s4()[:, 0:2, :], AF.Copy, [], [g.bAp0] + Ja.B)
                O.tt("dve", g.Gp0[:], bc_mid(ident, 2), Ja.s4()[:, 0:2, :], ALU.subtract, [b_cst], [g.bGp0] + Ja.B)
                yield
                Lp = [(g.A, g.bA), (g.Lp1, g.bLp1)]
                Ap = [(g.Ap0, g.bAp0), (g.Ap1, g.bAp1)]
                Gp = [(g.Gp0, g.bGp0), (g.Gp1, g.bGp1)]
                cur = 0
                for rnd in range(6):
                    nxt = 1 - cur
                    (Lc, bLc), (Ac, bAc) = Lp[cur], Ap[cur]
                    (Ln, bLn), (An, bAn) = Lp[nxt], Ap[nxt]
                    (Gs_, bGs), (Gd, bGd) = Gp[nxt], Gp[cur]
                    comb = 1 <= rnd <= 3
                    if comb:
                        JX = job()
                        jx = JX.t[:].rearrange("p (e a d) -> p e a d", e=2, a=2)
                        for e_ in range(2):
                            O.mm(jx[:, e_, :, :], Lc[:, e_, :], g.X[cur][:, e_, :, :], [bLc, bGs, bAc], JX.B)
                    elif rnd >= 1:
                        J3 = job()
                        for e_ in range(2):
                            O.mm(J3.s4()[:, e_, :], Lc[:, e_, :], Gs_[:, e_, :], [bLc, bGs], J3.B)
                    if rnd <= 4:
                        J1 = job()
                        for e_ in range(2):
                            O.mm(J1.s4()[:, e_, :], Ac[:, e_, :], Lc[:, e_, :], [bAc, bLc], J1.B)
                    if rnd == 0:
                        J2 = job()
                        for e_ in range(2):
                            O.mm(J2.s4()[:, e_, :], Lc[:, e_, :], Ac[:, e_, :], [bAc, bLc], J2.B)
                    if comb:
                        O.tt("dve", Gd[:], jx[:, :, 0, :], Gs_[:], ALU.add, [bGs], [bGd] + JX.B)
                    elif rnd >= 1:
                        O.tt("dve", Gd[:], J3.s4()[:, 0:2, :], Gs_[:], ALU.add, [bGs], [bGd] + J3.B)
                    if rnd <= 4:
                        O.act(Ln[:], J1.s4()[:, 0:2, :], AF.Copy, [], [bLn] + J1.B)
                    if comb:
                        O.act(An[:], jx[:, :, 1, :], AF.Copy, [], [bAn] + JX.B)
                    elif rnd == 0:
                        O.act(An[:], J2.s4()[:, 0:2, :], AF.Copy, [], [bAn] + J2.B)
                    yield
                    cur = nxt
                TTt, b_TT = Gp[1]
                kbg, bkbg = g.A, g.bA
                vb, bvb = g.Lp1, g.bLp1
                U, bU = g.Ap1, g.bAp1
                O.tt("pool", kbg[:], bc_mid(kt_[:, hq, :], 2), bc_last(gc(68), 128), ALU.mult, [bkt, b_gt], [bkbg])
                O.tt("pool", vb[:], vtm[:, blk, hv0:hv0 + 2, :], bc_last(gc(28), 128), ALU.mult, [b_vtm[blk][hv0], b_vtm[blk][hv0 + 1], b_gt], [bvb])
                yield
                Jw = job()
                for e_ in range(2):
                    O.mm(Jw.s4()[:, e_, :], kbg[:, e_, :], TTt[:, e_, :], [bkbg, b_TT], Jw.B)
                O.act(g.WTb[:], Jw.s4()[:, 0:2, :], AF.Copy, [], [g.bWTb] + Jw.B)
                Ju = job()
                for e_ in range(2):
                    O.mm(Ju.s4()[:, e_, :], TTt[:, e_, :], vb[:, e_, :], [bvb, b_TT], Ju.B)
                O.act(U[:], Ju.s4()[:, 0:2, :], AF.Copy, [], [bU] + Ju.B)
                yield
                for c in range(2):
                    cs = slice(64 * c, 64 * c + 64)
                    J1 = job()
                    for e_ in range(2):
                        O.mm(J1.s4()[:, e_, :], g.WTb[:, e_, :], Sb[:, hv0 + e_, :], [g.bWTb, b_Sb[hv0 + e_]], J1.B)
                    O.tt("dve", g.vnb[cs, :, :], U[cs, :, :], J1.s4()[cs, 0:2, :], ALU.subtract, [bU], [g.bvnb] + J1.B)
                    yield
                    J2 = job()
                    for e_ in range(2):
                        O.mmg(J2.s4()[:, e_, :], [(g.qdTb[:, e_, :], Sb[:, hv0 + e_, :]), (g.aTb[cs, e_, :], g.vnb[cs, e_, :])],
                              [g.bqdTb, b_Sb[hv0 + e_], g.baTb, g.bvnb], J2.B)
                    J3 = job()
                    for e_ in range(2):
                        O.mm(J3.s4()[:, e_, :], g.kttb[cs, e_, :], g.vnb[cs, e_, :], [g.bkttb, g.bvnb], J3.B)
                    for e_ in range(2):
                        O.stt(Sst[:, hv0 + e_, :], Sst[:, hv0 + e_, :], g.egs[:, 2 * c + e_:2 * c + e_ + 1], J3.s4()[:, e_, :], ALU.mult, ALU.add,
                              [g.begs], [b_S[hv0 + e_]] + J3.B)
                    O.cp("pool", Sb[:, hv0:hv0 + 2, :], Sst[:, hv0:hv0 + 2, :], [b_S[hv0], b_S[hv0 + 1]], [b_Sb[hv0], b_Sb[hv0 + 1]])
                    O.act(ot_[cs, hv0:hv0 + 2, :], J2.s4()[cs, 0:2, :], AF.Copy, [], [bot] + J2.B)
                    yield

            def gdn_epilogue(blk):
                ogs = (ti * NB + blk) % 2
                ogt = og[ogs]
                ot_, bot = otm[blk % 2], b_otm[blk % 2]
                if ti == 0 and blk == 0:
                    tap("otm", ot_[:].rearrange("p h d -> p (h d)"), [128, 1024], bot)
                    tap("S0", Sst[:].rearrange("p h d -> p (h d)"), [128, 1024], b_S)
                j3 = junk[:].rearrange("p (h d) -> p h d", h=8)
                O.tt("dve", j3, ot_[:], ot_[:], ALU.mult, bot, [b_junk])
                O.red(st[:, 8:16], j3, [b_junk], [b_st])
                O.act(st[:, 8:16], st[:, 8:16], AF.Sqrt, [b_st], [b_st], bias=EPS, scale=1.0 / 128)
                O.rec(st[:, 16:24], st[:, 8:16], [b_st], [b_st])
                O.tt("dve", ot_[:], ot_[:], bc_last(st[:, 16:24], 128), ALU.mult, bot + [b_st], bot)
                O.tt("dve", ogt[:, 0:1024], ot_[:].rearrange("p h d -> p (h d)"), zsg[:, blk, :], ALU.mult, bot + b_zsg[blk], [b_og[ogs][0]])

            def mlstm_gen(blk):
                tsl = slice(blk * 128, (blk + 1) * 128)
                ogs = (ti * NB + blk) % 2
                ogt = og[ogs]
                J = job()
                for h_ in range(2):
                    O.tr(J.s4()[:, h_, :], mk[:, h_, tsl], ident, [b_mk[h_], b_cst], J.B)
                O.act(mktm[:], J.s4()[:, 0:2, :], AF.Copy, [], [b_mktm] + J.B)
                if blk + 1 < NB:
                    emit_bT(blk + 1)
                Jr = job()
                prow = Jr.s4()[:, 0:2, :]
                for h_ in range(2):
                    O.mm(Jr.s4()[:, h_, :], cst[0:2, C_ID, h_:h_ + 1].broadcast_to([2, 128]), bTt[blk % 2][:], [b_bTt[blk % 2], b_cst], Jr.B)
                Jq = job()
                for h_ in range(2):
                    O.mm(Jq.s4()[:, h_, :], mk[:, h_, tsl], mq[:, h_, tsl], [b_mk[h_], b_mq[h_]], Jq.B)
                O.tt("dve", mtmpU[:], prow, bc_mid(cst[:, C_MASKU128, :], 2), ALU.add, [b_cst], [b_mtmpU] + Jr.B)
                for h_ in range(2):
                    O.act(Dm[:, h_, :], mtmpU[:, h_, :], AF.Exp, [b_mtmpU, b_gt], [b_Dm], bias=gt[:, blk, 84 + h_:85 + h_], scale=1.0)
                O.act(Eb[:], prow, AF.Exp, [], [b_Eb] + Jr.B, bias=LN_SQ, scale=1.0)
                O.tt("dve", sTm[:], Jq.s4()[:, 0:2, :], Dm[:], ALU.mult, [b_Dm], [b_sTm] + Jq.B)
                O.tt("dve", qbT[:], mq[:, :, tsl], Eb[:], ALU.mult, b_mq + [b_Eb], [b_qbT])
                O.tt("pool", kwm[:], mktm[:], bc_last(gt[:, blk, 86:88], 128), ALU.mult, [b_mktm, b_gt], [b_kwm])
                yield
                for h_ in range(2):
                    Jn = job()
                    pnv = Jn.t[:]
                    O.mmg(pnv[:, 0:257], [(qbT[:, h_, :], Cst[:, h_, :]), (sTm[:, h_, :], Vp[:, blk, h_, :])],
                          [b_qbT, b_C[h_], b_sTm, b_Vp[blk][h_]], Jn.B)
                    Jc = job()
                    O.mm(Jc.t[:, 0:257], kwm[:, h_, :], Vp[:, blk, h_, :], [b_kwm, b_Vp[blk][h_]], Jc.B)
                    O.act(st[:, 23:24], pnv[:, 256:257], AF.Copy, [], [b_st] + Jn.B)
                    O.stt(st[:, 24:25], st[:, 23:24], -1.0, st[:, 23:24], ALU.mult, ALU.max, [b_st], [b_st])
                    O.ts("dve", st[:, 24:25], st[:, 24:25], 1.0, ALU.max, [b_st], [b_st])
                    O.rec(st[:, 25:26], st[:, 24:25], [b_st], [b_st])
                    O.act(hbuf[:, 0, :], pnv[:, 0:256], AF.Copy, [b_st], [b_hbuf] + Jn.B, scale=st[:, 25:26])
                    O.stt(Cst[:, h_, :], Cst[:, h_, :], gt[:, blk, 88 + h_:89 + h_], Jc.t[:, 0:257], ALU.mult, ALU.add, [b_gt], [b_C[h_]] + Jc.B)
                    O.act(junk[:, 0:256], hbuf[:, 0, :], AF.Square, [b_hbuf], [b_junk, b_st], accum=st[:, 26:27])
                    O.act(st[:, 27:28], st[:, 26:27], AF.Sqrt, [b_st], [b_st], bias=EPS, scale=1.0 / 256)
                    O.rec(st[:, 28:29], st[:, 27:28], [b_st], [b_st])
                    O.stt(ogt[:, 1024 + h_ * 256:1024 + (h_ + 1) * 256], hbuf[:, 0, :], st[:, 28:29], mgw[:, blk, h_ * 256:(h_ + 1) * 256], ALU.mult, ALU.mult,
                          [b_hbuf, b_st, b_mgw[blk][h_]], [b_og[ogs][1]])
                    yield

            def og_store(blk):
                ogs = (ti * NB + blk) % 2
                r0 = tok0 + blk * 128
                O.dma("sp", f"og{ogs}", xsrc[r0:r0 + 128, :], og[ogs][:], b_og[ogs], [b_xsrc[r0 // 128]])

            pair_q = [(blk, hq) for blk in range(NB) for hq in range(4)]
            free_sets = list(range(NS))
            active = []
            done_pairs = [0] * NB
            ml_q = list(range(NB))
            ml_active = None
            ml_done = [False] * NB
            epi_done = [False] * NB
            stored = [False] * NB

            def try_store():
                for b_ in range(NB):
                    if (not stored[b_]) and epi_done[b_] and ml_done[b_]:
                        og_store(b_)
                        stored[b_] = True

            def drain_ml(upto):
                nonlocal ml_active
                while True:
                    if ml_active is None:
                        if ml_q and ml_q[0] <= upto:
                            b_ = ml_q.pop(0)
                            ml_active = (mlstm_gen(b_), b_)
                        else:
                            return
                    try:
                        while True:
                            next(ml_active[0])
                    except StopIteration:
                        ml_done[ml_active[1]] = True
                        ml_active = None

            while pair_q or active or ml_q or ml_active is not None:
                while pair_q and free_sets:
                    blk_, hq_ = pair_q.pop(0)
                    si = free_sets.pop(0)
                    active.append([pair_gen(blk_, hq_, gss[si]), si, blk_])
                for item in list(active):
                    try:
                        next(item[0])
                    except StopIteration:
                        active.remove(item)
                        free_sets.append(item[1])
                        done_pairs[item[2]] += 1
                        if done_pairs[item[2]] == 4:
                            bb = item[2]
                            if bb >= 2:
                                drain_ml(bb - 2)
                                try_store()
                            gdn_epilogue(bb)
                            epi_done[bb] = True
                            try_store()
                if ml_active is None and ml_q and (ml_q[0] < 2 or stored[ml_q[0] - 2]):
                    b_ = ml_q.pop(0)
                    ml_active = (mlstm_gen(b_), b_)
                if ml_active is not None:
                    try:
                        next(ml_active[0])
                    except StopIteration:
                        ml_done[ml_active[1]] = True
                        ml_active = None
                        try_store()
                if not active and not pair_q and ml_active is None and ml_q and not (ml_q[0] < 2 or stored[ml_q[0] - 2]):
                    try_store()
            try_store()
            assert all(stored), stored

        es1.__exit__(None, None, None)

        if DEBUG["phase2"]:
            last_c = 2 * (nt_run - 1)
            n_c = 32 if (DEBUG["on"] and nt_run < 16) else 2 * nt_run
            for c in range(last_c, n_c):
                issue_ag(c)
            phase2(nc, S, O, job, locals())
        else:
            if DEBUG["on"]:
                o = nc.dram_tensor("tap_xsrc", [nt_run * TT, 1536], BF16, kind="ExternalOutput").ap()
                O.dma("pool", "tapx", o, xsrc[0:nt_run * TT, :], b_xsrc[0:nt_run * NB], [Buf()])
                taps.append("xsrc")
            S.barrier()
            zz = sb("zz", [128, D])
            bz = Buf()
            O.memset("pool", zz[:], 0.0, [bz])
            O.dma("sp", "yz", y_out[0:128, :], zz[:], [bz], [Buf()])
        S.wait_all_dma("sp")
        S.wait_all_dma("pool")
        S.emit()
    return nc


def phase2(nc, S, O, job, L):
    x_own, y_out, xdst = L["x_own"], L["y_out"], L["xdst"]
    wgate_b, wpa_b, wpb_b, wout_b = L["wgate_b"], L["wpa_b"], L["wpb_b"], L["wout_b"]
    b_wgate, b_wpa, b_wpb, b_wout = L["b_wgate"], L["b_wpa"], L["b_wpb"], L["b_wout"]
    cst, b_cst, b_prm = L["cst"], L["b_cst"], L["b_prm"]
    identb, b_identb = L["identb"], L["b_identb"]
    gp_d, b_gpd = L["gp_d"], L["b_gpd"]
    ident = cst[:, C_ID, :]
    Gs, shift = L["Gs"], L["shift"]
    S.new_epoch()
    S.barrier(exclude=("dma:cc0", "dma:cc1", "dma:cc2", "dma:cc3", "dma:c2", "dma:gpd") + L["CAST_AGENTS"])
    with ExitStack() as es:
        def sb(name, shape, dt=F32):
            return es.enter_context(nc.sbuf_tensor(name, list(shape), dt))
        GP = sb("GP", [128, D])
        b_GP = Buf("GP")
        O.dma("sp", "gpl", GP[:], gp_d, [b_gpd], [b_GP])
        xo = sb("xo", [128, D])
        b_xo = Buf("xo")
        junk = sb("junk2", [128, D], BF16)
        b_junk = Buf()
        st = sb("st2", [128, 16])
        b_st = Buf()
        hT2 = sb("hT2", [128, KC, TT], BF16)
        b_hT2 = bufs(NB, "hT2")
        NWQ = 3
        wr = [sb(f"wq{i}", [128, 4096], BF16) for i in range(NWQ)]
        b_wr = bufs(NWQ, "wq")
        sg = sb("sg", [128, 32 * TT], BF16)
        b_sg = bufs(32, "sg")
        sg3 = sg[:].rearrange("p (j t) -> p j t", j=32)
        outsb = sg[:].bitcast(F32).rearrange("p (b n) -> p b n", b=NB)
        araw2 = [sb(f"araw{i}", [128, 4, 1536], BF16) for i in range(2)]
        b_araw2 = bufs(2, "araw")
        aT = sb("aT", [128, 48, TT], BF16)
        b_aT = bufs(NB, "aT")
        mT = sb("mT", [128, KC, TT], BF16)
        b_mT = bufs(KC, "mT")
        tmpm = [sb(f"tmpm{i}", [128, TT]) for i in range(2)]
        b_tmpm = bufs(2, "tmpm")
        ysb = sb("ysb", [128, D])
        b_ysb = Buf("ysb")

        xg = nc.dram_tensor("xg", [8192, 1536], BF16).ap()
        b_xg = Buf("xg")
        for i4 in range(4):
            def dyn1(e, i4=i4):
                pid = e.partition_id()
                g = pid % 4
                return e.dma_start(out=xg[i4 * 2048:(i4 + 1) * 2048, :], in_=xdst[bass.ds(g * 8192 + i4 * 2048, 2048), :])
            S.dma("pool", "xg", dyn1, L["b_ag"], [b_xg])

        wctr = [0]

        def load_w(src_ap, n, pattern, dims, src_bufs):
            sl = wctr[0] % NWQ
            wctr[0] += 1
            view = wr[sl][:, 0:n].rearrange(pattern, **dims)
            O.dma("sp", f"wq{sl}", view, src_ap, src_bufs, [b_wr[sl]])
            return sl, view

        for t2 in range(DEBUG.get("p2_tiles", 4)):
            for blk in range(NB):
                row0 = t2 * TT + blk * 128
                xb, b_xb, xslot = (xo, b_xo, "xo") if blk % 2 == 0 else (ysb, b_ysb, "xo2")
                so = 3 * (blk % 2)
                O.dma("sp", xslot, xb[:], x_own[row0:row0 + 128, :], (), [b_xb])
                O.act(junk[:], xb[:], AF.Square, [b_xb], [b_junk, b_st], accum=st[:, so:so + 1])
                O.act(st[:, so + 1:so + 2], st[:, so:so + 1], AF.Sqrt, [b_st], [b_st], bias=EPS, scale=1.0 / D)
                O.rec(st[:, so + 2:so + 3], st[:, so + 1:so + 2], [b_st], [b_st])
                O.ts("dve", xb[:], xb[:], st[:, so + 2:so + 3], ALU.mult, [b_xb, b_st], [b_xb])
                for kq in range(4):
                    J = job()
                    for s_ in range(4):
                        k = kq * 4 + s_
                        O.tr(J.s4()[:, s_, :], xb[:, k * 128:(k + 1) * 128], ident, [b_xb, b_cst], J.B)
                    for s_ in range(4):
                        k = kq * 4 + s_
                        if kq % 2 == 0:
                            O.act(hT2[:, k, blk * 128:(blk + 1) * 128], J.s4()[:, s_, :], AF.Identity, [b_prm], [b_hT2[blk]] + J.B,
                                  bias=shift[:, k:k + 1], scale=Gs[:, k:k + 1])
                        else:
                            O.ts("dve", hT2[:, k, blk * 128:(blk + 1) * 128], J.s4()[:, s_, :], Gs[:, k:k + 1], ALU.mult, [b_prm], [b_hT2[blk]] + J.B,
                                 s2=shift[:, k:k + 1], op1=ALU.add)
            for jp in range(16):
                sl, wv = load_w(wgate_b[2 * jp:2 * jp + 2].rearrange("j p k c -> p j k c"), 4096, "p (j k c) -> p j k c", dict(j=2, k=KC), b_wgate[2 * jp:2 * jp + 2])
                for jj in range(2):
                    j = 2 * jp + jj
                    J = job()
                    O.mmg(J.t[:], [(wv[:, jj, k, :], hT2[:, k, :]) for k in range(KC)], [b_wr[sl]] + b_hT2, J.B)
                    O.act(sg3[:, j, :], J.t[:], AF.Sigmoid, [], [b_sg[j]] + J.B)
            for blk in range(NB):
                row0 = t2 * TT + blk * 128
                araw, b_araw = araw2[blk % 2], b_araw2[blk % 2]
                for r in range(4):
                    xrow = (row0 // 256) * 1024 + r * 256 + (row0 % 256)
                    O.dma("sp", f"araw{blk % 2}", araw[:, r, :], xg[xrow:xrow + 128, :], [b_xg], [b_araw])
                for r in range(4):
                    for grp in range(3):
                        fc0 = (r * 8 + grp * 4) if grp < 2 else (32 + r * 4)
                        J = job()
                        jb = J.t[:].bitcast(BF16).rearrange("p (s d) -> p s d", s=8)
                        for s_ in range(4):
                            cc = grp * 4 + s_
                            O.tr(jb[:, s_, :], araw[:, r, cc * 128:(cc + 1) * 128], identb[:], [b_araw, b_identb], J.B)
                        if (r * 3 + grp) % 2 == 0:
                            O.act(aT[:, fc0:fc0 + 4, blk * 128:(blk + 1) * 128], jb[:, 0:4, :], AF.Copy, [], [b_aT[blk]] + J.B)
                        else:
                            O.cp("dve", aT[:, fc0:fc0 + 4, blk * 128:(blk + 1) * 128], jb[:, 0:4, :], [], [b_aT[blk]] + J.B)
            for cb in range(16):
                Ja, Jb = job(), job()
                sl, wv = load_w(wpa_b[cb], 4096, "p (k c) -> p k c", dict(k=32), [b_wpa[cb]])
                O.mmg(Ja.t[:], [(wv[:, k, :], aT[:, k, :]) for k in range(32)], [b_wr[sl]] + b_aT, Ja.B)
                sl2, wv2 = load_w(wpb_b[cb], 2048, "p (k c) -> p k c", dict(k=16), [b_wpb[cb]])
                O.mmg(Jb.t[:], [(wv2[:, k, :], aT[:, 32 + k, :]) for k in range(16)], [b_wr[sl2]] + b_aT, Jb.B)
                O.tt("dve", tmpm[0][:], Ja.t[:], sg3[:, cb, :], ALU.mult, [b_sg[cb]], [b_tmpm[0]] + Ja.B)
                O.tt("dve", tmpm[1][:], Jb.t[:], sg3[:, 16 + cb, :], ALU.mult, [b_sg[16 + cb]], [b_tmpm[1]] + Jb.B)
                O.tt("pool", mT[:, cb, :], tmpm[0][:], tmpm[1][:], ALU.add, b_tmpm, [b_mT[cb]])
            b_out = bufs(NB, "outsb")
            for b_ in b_out:
                for bs_ in b_sg:
                    for ag, c in list(bs_.r.items()) + ([bs_.w] if bs_.w is not None else []):
                        if b_.r.get(ag, 0) < c:
                            b_.r[ag] = c
            for pcs in range(8):
                sl, wv = load_w(wout_b[pcs], 4096, "p (k c) -> p k c", dict(k=KC), [b_wout[pcs]])
                for blk in range(NB):
                    J = job()
                    O.mmg(J.t[:, 0:256], [(mT[:, k, blk * 128:(blk + 1) * 128], wv[:, k, :]) for k in range(KC)], [b_wr[sl]] + b_mT, J.B)
                    dstv = outsb[:, blk, pcs * 256:(pcs + 1) * 256]
                    if (pcs + blk) % 2 == 0:
                        O.act(dstv, J.t[:, 0:256], AF.Copy, [], [b_out[blk]] + J.B)
                    else:
                        O.cp("dve", dstv, J.t[:, 0:256], [], [b_out[blk]] + J.B)
            for blk in range(NB):
                row0 = t2 * TT + blk * 128
                O.dma("sp", "xo", xo[:], x_own[row0:row0 + 128, :], (), [b_xo])
                O.act(junk[:], outsb[:, blk, :], AF.Square, [b_out[blk]], [b_junk, b_st], accum=st[:, 8:9])
                O.act(st[:, 9:10], st[:, 8:9], AF.Sqrt, [b_st], [b_st], bias=EPS, scale=1.0 / D)
                O.rec(st[:, 10:11], st[:, 9:10], [b_st], [b_st])
                O.stt(ysb[:], outsb[:, blk, :], st[:, 10:11], GP[:], ALU.mult, ALU.mult, [b_out[blk], b_st, b_GP], [b_ysb])
                O.tt("pool", ysb[:], ysb[:], xo[:], ALU.add, [b_ysb, b_xo], [b_ysb])
                O.dma("sp", "yo", y_out[row0:row0 + 128, :], ysb[:], [b_ysb], [Buf()])
            for b_ in b_sg:
                for bo in b_out:
                    for ag, c in bo.r.items():
                        if b_.r.get(ag, 0) < c:
                            b_.r[ag] = c
                    if bo.w is not None:
                        ag, c = bo.w
                        if b_.r.get(ag, 0) < c:
                            b_.r[ag] = c


def host_inputs(inp):
    f = np.float32
    x = np.asarray(inp["x"], f)
    c = np.asarray(inp["c"], f)
    w_ada = np.ascontiguousarray(np.asarray(inp["w_ada"], f)[0])
    b_ada = np.asarray(inp["b_ada"], f)[0]
    w_in = np.asarray(inp["w_in"], f)[0]
    gconv = np.asarray(inp["gdn_conv_w"], f)[0]
    mconv = np.asarray(inp["mlstm_conv_w"], f)[0]

    def fm_layout(cols):
        n = cols.shape[1] // 128
        return np.ascontiguousarray(cols.reshape(KC, 128, n, 128).transpose(2, 1, 0, 3))

    def tm_layout(cols, w):
        n = cols.shape[1] // w
        return np.ascontiguousarray(cols.reshape(KC, 128, n, w).transpose(2, 1, 0, 3))

    w_gate = fm_layout(w_in[:, 20560:24656])
    wpa_full = np.asarray(inp["w_proj_gdn"], f)[0]
    wpb_full = np.asarray(inp["w_proj_mlstm"], f)[0]
    wout_full = np.asarray(inp["w_out"], f)[0]
    wpa = np.ascontiguousarray(wpa_full.reshape(32, 128, 16, 128).transpose(2, 1, 0, 3))
    wpb = np.ascontiguousarray(wpb_full.reshape(16, 128, 16, 128).transpose(2, 1, 0, 3))
    wout = tm_layout(wout_full, 256)
    consts = make_consts()
    npw = np.asarray(inp["norm_post_w"], f)[0][None, :]
    normw_t = np.ascontiguousarray(np.asarray(inp["norm_pre_w"], f)[0].reshape(KC, 128).T)
    b_ada_t = np.ascontiguousarray(b_ada[:4096].reshape(32, 128).T)
    b_gate_row = np.ascontiguousarray(b_ada[4096:][None, :])
    maps = []
    for core in range(8):
        b, g = core // 4, core % 4
        gq = w_in[:, g * 512:(g + 1) * 512]
        gk = w_in[:, 2048 + g * 512:2048 + (g + 1) * 512]
        gv = w_in[:, 4096 + g * 1024:4096 + (g + 1) * 1024]
        mqc = w_in[:, 12352 + g * 256:12352 + (g + 1) * 256]
        mkc = w_in[:, 13376 + g * 256:13376 + (g + 1) * 256]
        w_fm = fm_layout(np.concatenate([gq, gk, gv, mqc, mkc], axis=1))
        gz = w_in[:, 8256 + g * 1024:8256 + (g + 1) * 1024]
        mv = w_in[:, 14400 + g * 512:14400 + (g + 1) * 512]
        mo = w_in[:, 16464 + g * 512:16464 + (g + 1) * 512]
        mz = w_in[:, 18512 + g * 512:18512 + (g + 1) * 512]
        w_tm = tm_layout(np.concatenate([gz, mv, mo, mz], axis=1), 256)
        sm = np.concatenate([w_in[:, 8192 + g * 8:8192 + (g + 1) * 8], w_in[:, 8224 + g * 8:8224 + (g + 1) * 8],
                             w_in[:, 16448 + g * 2:16448 + (g + 1) * 2], w_in[:, 16456 + g * 2:16456 + (g + 1) * 2]], axis=1)
        w_sm = np.ascontiguousarray(sm.reshape(KC, 128, 20).transpose(1, 0, 2))
        cv = np.concatenate([gconv[:, g * 512:(g + 1) * 512], gconv[:, 2048 + g * 512:2048 + (g + 1) * 512],
                             gconv[:, 4096 + g * 1024:4096 + (g + 1) * 1024],
                             mconv[:, g * 256:(g + 1) * 256], mconv[:, 1024 + g * 256:1024 + (g + 1) * 256]], axis=1)
        convw = np.ascontiguousarray(cv.reshape(4, 20, 128).transpose(2, 1, 0))
        m = {
            "x_full": x[b], "x_own": np.ascontiguousarray(x[b, g * 2048:(g + 1) * 2048]),
            "c_t": np.ascontiguousarray(c[b].reshape(KC, 128).T),
            "w_ada": w_ada, "b_ada_t": b_ada_t, "b_gate_row": b_gate_row, "normw_t": normw_t,
            "w_fm": w_fm, "w_tm": w_tm, "w_sm": w_sm, "w_gate": w_gate, "convw": convw,
            "alog": np.asarray(inp["gdn_A_log"], f)[0][None, g * 8:(g + 1) * 8].copy(),
            "dtb": np.asarray(inp["gdn_dt_bias"], f)[0][None, g * 8:(g + 1) * 8].copy(),
            "gnw": np.asarray(inp["gdn_norm_w"], f)[0][None, :].copy(),
            "mbi": np.asarray(inp["mlstm_b_i"], f)[0][None, g * 2:(g + 1) * 2].copy(),
            "mbf": np.asarray(inp["mlstm_b_f"], f)[0][None, g * 2:(g + 1) * 2].copy(),
            "mnw": np.asarray(inp["mlstm_norm_w"], f)[0][None, g * 512:(g + 1) * 512].copy(),
            "wpa": wpa, "wpb": wpb, "wout": wout, "npw": npw, "consts": consts,
        }
        maps.append(m)
    return maps


def kernel(**inputs):
    maps = host_inputs(inputs)
    nc = build_program()
    res = run_bass_kernel_spmd(nc, maps, core_ids=list(range(8)))
    out = np.zeros((2, SEQ, D), np.float32)
    for core in range(8):
        b, g = core // 4, core % 4
        out[b, g * 2048:(g + 1) * 2048] = res.results[core]["y_out"]
    return out
```

```python
import math
from contextlib import ExitStack

import numpy as np
import concourse.bass as bass
import concourse.mybir as mybir
from concourse.bass_utils import run_bass_kernel_spmd

F32 = mybir.dt.float32
BF16 = mybir.dt.bfloat16
F32R = mybir.dt.float32r
AF = mybir.ActivationFunctionType
ALU = mybir.AluOpType
AX = mybir.AxisListType

D = 2048
KC = 16
SEQ = 8192
TT = 512
NB = 4
IN_W = 24656
EPS = 1e-6
NEG = -30000.0
GROUPS = [[0, 1, 2, 3], [4, 5, 6, 7]]

DEBUG = {"on": False, "nt": 16, "taps": [], "phase2": True}


class Buf:
    __slots__ = ("name", "w", "r", "psum")

    def __init__(self, name="", psum=False):
        self.name = name
        self.w = None
        self.r = {}
        self.psum = psum


def bufs(n, name=""):
    return [Buf(f"{name}{i}") for i in range(n)]


class Sched:
    ENG = ("pe", "act", "dve", "pool", "sp")

    def __init__(self, nc):
        self.nc = nc
        self.q = {e: [] for e in self.ENG}
        self.sem = {}
        self.cnt = {}
        self.waited = {}
        self.epoch = 0

    def new_epoch(self):
        self.epoch += 1

    def _sem(self, agent):
        if agent not in self.sem:
            self.sem[agent] = self.nc.alloc_semaphore(name="s_" + agent.replace("@", "_").replace(":", "_"))
            self.cnt[agent] = 0
        return self.sem[agent]

    def _deps(self, eng, reads, writes):
        deps = {}
        pre = eng + "@"
        for b in reads:
            if b.w is not None:
                a, c = b.w
                if b.psum and a.startswith(pre):
                    continue
                if deps.get(a, 0) < c:
                    deps[a] = c
        for b in writes:
            if b.w is not None:
                a, c = b.w
                if not (b.psum and a.startswith(pre)) and deps.get(a, 0) < c:
                    deps[a] = c
            for a, c in b.r.items():
                if b.psum and a.startswith(pre):
                    continue
                if deps.get(a, 0) < c:
                    deps[a] = c
        waits = []
        for a, c in deps.items():
            if eng == "pe" and a.startswith("pe@"):
                continue
            if self.waited.get((eng, a), 0) >= c:
                continue
            self.waited[(eng, a)] = c
            waits.append((self.sem[a], c))
        return waits

    def _post(self, agent, c, reads, writes):
        for b in reads:
            if b.r.get(agent, 0) < c:
                b.r[agent] = c
        for b in writes:
            b.w = (agent, c)
            b.r = {}

    def op(self, eng, fn, reads=(), writes=()):
        waits = self._deps(eng, reads, writes)
        agent = f"{eng}@{self.epoch}"
        sem = self._sem(agent)
        self.cnt[agent] += 1
        self._post(agent, self.cnt[agent], reads, writes)
        self.q[eng].append((waits, fn, sem, 1))

    def dma(self, eng, slot, fn, reads=(), writes=(), inc=16):
        waits = self._deps(eng, reads, writes)
        agent = "dma:" + slot
        sem = self._sem(agent)
        self.cnt[agent] += inc
        self._post(agent, self.cnt[agent], reads, writes)
        self.q[eng].append((waits, fn, sem, inc))

    def group_done(self, slot, bl):
        agent = "dma:" + slot
        for b in bl:
            b.w = (agent, self.cnt[agent])

    def barrier(self, exclude=()):
        snap = [(a, c) for a, c in self.cnt.items() if c > 0 and a not in exclude]
        for e in self.ENG:
            waits = []
            for a, c in snap:
                if self.waited.get((e, a), 0) >= c:
                    continue
                self.waited[(e, a)] = c
                waits.append((self.sem[a], c))
            if waits:
                self.q[e].append((waits, None, None, 0))

    def wait_all_dma(self, eng):
        waits = [(self.sem[a], c) for a, c in self.cnt.items() if a.startswith("dma:")]
        self.q[eng].append((waits, None, None, 0))

    def emit(self):
        names = {"pe": "tensor", "act": "scalar", "dve": "vector", "pool": "gpsimd", "sp": "sync"}
        with self.nc.Block() as block:
            for e in self.ENG:
                items = self.q[e]
                if not items:
                    continue

                def body(eng, items=items):
                    for waits, fn, sem, inc in items:
                        for s, v in waits:
                            eng.wait_ge(s, v)
                        if fn is not None:
                            fn(eng).then_inc(sem, inc)

                getattr(block, names[e])(body)


def make_consts():
    idx = np.arange(128)
    s = idx[:, None]
    t = idx[None, :]
    same = (s // 64) == (t // 64)
    c = np.zeros((128, 8, 128), np.float32)
    c[:, 0] = np.eye(128)
    c[:, 1] = ((s <= t) & same)
    c[:, 2] = same
    c[:, 3] = (s <= t)
    c[:, 4] = 1.0
    c[:, 5] = np.where((t < s) & same, 0.0, -NEG)
    c[:, 6] = np.where((t >= s) & same, 0.0, NEG)
    c[:, 7] = np.where(t >= s, 0.0, NEG)
    return c


C_ID, C_TRI64, C_BLK64, C_TRI128, C_ONES, C_MASKL, C_MASKU, C_MASKU128 = range(8)


class Ops:
    def __init__(self, S):
        self.S = S

    def tt(self, eng, out, in0, in1, op, reads, writes):
        self.S.op(eng, lambda e: e.tensor_tensor(out=out, in0=in0, in1=in1, op=op), reads, writes)

    def ts(self, eng, out, in0, s1, op0, reads, writes, s2=None, op1=None):
        if op1 is None:
            self.S.op(eng, lambda e: e.tensor_scalar(out=out, in0=in0, scalar1=s1, scalar2=None, op0=op0), reads, writes)
        else:
            self.S.op(eng, lambda e: e.tensor_scalar(out=out, in0=in0, scalar1=s1, scalar2=s2, op0=op0, op1=op1), reads, writes)

    def stt(self, out, in0, scalar, in1, op0, op1, reads, writes):
        self.S.op("dve", lambda e: e.scalar_tensor_tensor(out=out, in0=in0, scalar=scalar, in1=in1, op0=op0, op1=op1), reads, writes)

    def act(self, out, in_, func, reads, writes, bias=None, scale=None, accum=None):
        kw = {}
        if bias is not None:
            kw["bias"] = bias
        if scale is not None:
            kw["scale"] = scale
        if accum is not None:
            kw["accum_out"] = accum
        self.S.op("act", lambda e: e.activation(out=out, in_=in_, func=func, **kw), reads, writes)

    def cp(self, eng, out, in_, reads, writes):
        self.S.op(eng, lambda e: e.tensor_copy(out=out, in_=in_), reads, writes)

    def rec(self, out, in_, reads, writes):
        self.S.op("dve", lambda e: e.reciprocal(out=out, in_=in_), reads, writes)

    def red(self, out, in_, reads, writes):
        self.S.op("dve", lambda e: e.tensor_reduce(out=out, in_=in_, axis=AX.X, op=ALU.add), reads, writes)

    def memset(self, eng, out, val, writes):
        self.S.op(eng, lambda e: e.memset(out, val), (), writes)

    def mm(self, out, lhsT, rhs, reads, writes):
        self.S.op("pe", lambda e: e.matmul(out, lhsT=lhsT, rhs=rhs, start=True, stop=True), reads, writes)

    def mmg(self, out, pairs, reads, writes):
        pairs = list(pairs)

        def fn(e):
            n = len(pairs)
            for i, (l, r) in enumerate(pairs):
                ins = e.matmul(out, lhsT=l, rhs=r, start=(i == 0), stop=(i == n - 1))
            return ins
        self.S.op("pe", fn, reads, writes)

    def tr(self, out, in_, ident, reads, writes):
        self.S.op("pe", lambda e: e.transpose(out=out, in_=in_, identity=ident), reads, writes)

    def dma(self, eng, slot, out, in_, reads, writes):
        self.S.dma(eng, slot, lambda e: e.dma_start(out=out, in_=in_), reads, writes)


def RR(ap):
    return ap


def bc_mid(ap, n):
    return ap.unsqueeze(1).broadcast_to([128, n, ap.shape[-1]])


def bc_last(ap, n):
    return ap.unsqueeze(2).broadcast_to([128, ap.shape[-1], n])


class Bank:
    def __init__(self, t, i):
        self.t = t
        self.b = Buf(f"bank{i}", psum=True)
        self.B = [self.b]

    def s4(self):
        return self.t[:].rearrange("p (s d) -> p s d", s=4)

    def flat(self):
        return self.t[:]


def build_program():
    nt_run = DEBUG["nt"]
    nc = bass.Bass("TRN2", target_bir_lowering=False)
    S = Sched(nc)
    O = Ops(S)

    def din(name, shape, dt=F32):
        return nc.dram_tensor(name, list(shape), dt, kind="ExternalInput").ap()

    x_full = din("x_full", [SEQ, D])
    x_own = din("x_own", [2048, D])
    c_t = din("c_t", [128, KC])
    w_ada = din("w_ada", [D, 3 * D])
    b_ada_t = din("b_ada_t", [128, 32])
    b_gate_row = din("b_gate_row", [1, D])
    normw_t = din("normw_t", [128, KC])
    w_fm = din("w_fm", [20, 128, KC, 128])
    w_tm = din("w_tm", [10, 128, KC, 256])
    w_sm = din("w_sm", [128, KC, 20])
    w_gate = din("w_gate", [32, 128, KC, 128])
    convw_d = din("convw", [128, 20, 4])
    alog_d = din("alog", [1, 8])
    dtb_d = din("dtb", [1, 8])
    gnw_d = din("gnw", [1, 128])
    mbi_d = din("mbi", [1, 2])
    mbf_d = din("mbf", [1, 2])
    mnw_d = din("mnw", [1, 512])
    wpa = din("wpa", [16, 128, 32, 128])
    wpb = din("wpb", [16, 128, 16, 128])
    wout = din("wout", [8, 128, KC, 256])
    npw_d = din("npw", [1, D])
    consts_d = din("consts", [128, 8, 128])
    y_out = nc.dram_tensor("y_out", [2048, D], F32, kind="ExternalOutput").ap()

    wfm_b = nc.dram_tensor("wfm_b", [20, 128, KC, 128], BF16).ap()
    wtm_b = nc.dram_tensor("wtm_b", [10, 128, KC, 256], BF16).ap()
    wsm_b = nc.dram_tensor("wsm_b", [128, KC, 20], BF16).ap()
    wgate_b = nc.dram_tensor("wgate_b", [32, 128, KC, 128], BF16).ap()
    wpa_b = nc.dram_tensor("wpa_b", [16, 128, 32, 128], BF16).ap()
    wpb_b = nc.dram_tensor("wpb_b", [16, 128, 16, 128], BF16).ap()
    wout_b = nc.dram_tensor("wout_b", [8, 128, KC, 256], BF16).ap()
    gp_d = nc.dram_tensor("gp_d", [128, D], F32).ap()
    xsrc = nc.dram_tensor("xsrc", [SEQ, 1536], BF16).ap()
    xdst = nc.dram_tensor("xdst", [4 * SEQ, 1536], BF16).ap()

    taps = DEBUG["taps"]

    def tap(name, ap, shape, reads):
        if not DEBUG["on"]:
            return
        o = nc.dram_tensor("tap_" + name, list(shape), F32, kind="ExternalOutput").ap()
        O.dma("sp", "tap_" + name, o, ap, reads, [Buf()])
        taps.append(name)

    es = ExitStack()
    with es:
        def sb(name, shape, dt=F32):
            return es.enter_context(nc.sbuf_tensor(name, list(shape), dt))

        banks = [Bank(es.enter_context(nc.psum_tensor(f"bank{i}", [128, 512], F32)), i) for i in range(8)]
        bctr = [0]

        def job():
            bk = banks[bctr[0] % 8]
            bctr[0] += 1
            return bk

        b_wfm = bufs(20, "wfm")
        b_wtm = bufs(10, "wtm")
        b_wsm = Buf("wsm")
        b_wgate = bufs(32, "wg")
        b_wpa = bufs(16, "wpa")
        b_wpb = bufs(16, "wpb")
        b_wout = bufs(8, "wo")

        def cast(dst, src, lo, hi, bl, slot):
            O.dma("pool", slot, dst[lo:hi], src[lo:hi], (), bl[lo:hi])

        cast_jobs = [(f"c1f{i}", wfm_b, w_fm, i, i + 4, b_wfm) for i in range(0, 20, 4)]
        cast_jobs += [("c1s", wsm_b, w_sm, None, None, [b_wsm])]
        cast_jobs += [(f"c1t{i}", wtm_b, w_tm, i, i + 2, b_wtm) for i in range(0, 10, 2)]
        CAST_AGENTS = tuple("dma:" + j[0] for j in cast_jobs)

        def issue_cast(after):
            if not cast_jobs:
                return
            slot, dst, src_, lo, hi, bl = cast_jobs.pop(0)
            if lo is None:
                O.dma("pool", slot, dst, src_, after, bl)
            else:
                O.dma("pool", slot, dst[lo:hi], src_[lo:hi], after, bl[lo:hi])
        cst = sb("cst", [128, 8, 128])
        b_cst = Buf("cst")
        O.dma("sp", "cst", cst[:], consts_d, (), [b_cst])
        ident = cst[:, C_ID, :]
        ones = cst[:, C_ONES, :]
        identb = sb("identb", [128, 128], BF16)
        b_identb = Buf("identb")
        O.cp("dve", identb[:], ident, [b_cst], [b_identb])

        prm = sb("prm", [128, 256])
        b_prm = Buf("prm")
        o_ct, o_bada, o_nw, o_alog, o_dtb, o_bi, o_bf = 0, 16, 48, 64, 72, 80, 82
        o_csil, o_ada, o_G, o_negA = 96, 112, 160, 176
        O.dma("sp", "prm", prm[:, o_ct:o_ct + 16], c_t, (), [b_prm])
        O.dma("sp", "prm", prm[:, o_bada:o_bada + 32], b_ada_t, (), [b_prm])
        O.dma("sp", "prm", prm[:, o_nw:o_nw + 16], normw_t, (), [b_prm])
        O.dma("sp", "prm", prm[:, o_alog:o_alog + 8], alog_d.broadcast_to([128, 8]), (), [b_prm])
        O.dma("sp", "prm", prm[:, o_dtb:o_dtb + 8], dtb_d.broadcast_to([128, 8]), (), [b_prm])
        O.dma("sp", "prm", prm[:, o_bi:o_bi + 2], mbi_d.broadcast_to([128, 2]), (), [b_prm])
        O.dma("sp", "prm", prm[:, o_bf:o_bf + 2], mbf_d.broadcast_to([128, 2]), (), [b_prm])
        convw = sb("convw_sb", [128, 20, 4])
        b_convw = Buf("convw")
        O.dma("sp", "prm", convw[:], convw_d, (), [b_convw])
        gnw = sb("gnw_sb", [128, 128])
        mnw = sb("mnw_sb", [128, 512])
        b_gnw, b_mnw = Buf(), Buf()
        O.dma("sp", "prm", gnw[:], gnw_d.broadcast_to([128, 128]), (), [b_gnw])
        O.dma("sp", "prm", mnw[:], mnw_d.broadcast_to([128, 512]), (), [b_mnw])

        S.group_done("prm", [b_prm, b_convw, b_gnw, b_mnw])
        dtb = prm[:, o_dtb:o_dtb + 8]
        negA = prm[:, o_negA:o_negA + 8]
        csil = prm[:, o_csil:o_csil + 16]
        Gs = prm[:, o_G:o_G + 16]
        shift = prm[:, o_ada:o_ada + 16]

        O.act(csil, prm[:, o_ct:o_ct + 16], AF.Silu, [b_prm], [b_prm])
        O.act(negA, prm[:, o_alog:o_alog + 8], AF.Exp, [b_prm], [b_prm])
        O.ts("dve", negA, negA, -1.0, ALU.mult, [b_prm], [b_prm])
        with ExitStack() as es0:
            wada_t = [es0.enter_context(nc.sbuf_tensor(f"wada{i}", [128, KC, 512], F32)) for i in range(2)]
            b_wada = bufs(2, "wada")
            w_ada_v = w_ada.rearrange("(k p) n -> p k n", p=128)
            J = job()
            for pc in range(8):
                sl = pc % 2
                O.dma("sp", f"wada{sl}", wada_t[sl][:], w_ada_v[:, :, pc * 512:(pc + 1) * 512], (), [b_wada[sl]])
                issue_cast([b_wada[sl]])
                if pc >= 5:
                    issue_cast([b_wada[sl]])
                for jj in range(4):
                    j = pc * 4 + jj
                    O.mmg(J.t[:, j:j + 1], [(wada_t[sl][:, k, jj * 128:(jj + 1) * 128], csil[:, k:k + 1]) for k in range(KC)],
                          [b_wada[sl], b_prm], J.B)
            O.tt("dve", prm[:, o_ada:o_ada + 32], J.t[:, 0:32], prm[:, o_bada:o_bada + 32], ALU.add, [b_prm], [b_prm] + J.B)
            O.stt(Gs, prm[:, o_ada + 16:o_ada + 32], 1.0, prm[:, o_nw:o_nw + 16], ALU.add, ALU.mult, [b_prm], [b_prm])
            b_gpd = Buf("gpd")
            if DEBUG["phase2"]:
                csr = es0.enter_context(nc.sbuf_tensor("csr", [128, KC, 128], F32))
                rows = es0.enter_context(nc.sbuf_tensor("rows", [128, 2, D], F32))
                GPt = es0.enter_context(nc.sbuf_tensor("GPt", [128, D], F32))
                b_csr, b_rows, b_GPt = Buf(), Buf(), Buf()
                O.dma("sp", "rw", rows[:, 0, :], b_gate_row.broadcast_to([128, D]), (), [b_rows])
                O.dma("sp", "rw", rows[:, 1, :], npw_d.broadcast_to([128, D]), (), [b_rows])
                O.cp("dve", csr[:], bc_last(csil, 128), [b_prm], [b_csr])
                for pc in range(4):
                    sl = pc % 2
                    O.dma("sp", f"wada{sl}", wada_t[sl][:], w_ada_v[:, :, 4096 + pc * 512:4096 + (pc + 1) * 512], (), [b_wada[sl]])
                    Jg = job()
                    O.mmg(Jg.t[:], [(csr[:, k, :], wada_t[sl][:, k, :]) for k in range(KC)], [b_csr, b_wada[sl]], Jg.B)
                    gsl = GPt[:, pc * 512:(pc + 1) * 512]
                    O.tt("dve", gsl, Jg.t[:], rows[:, 0, pc * 512:(pc + 1) * 512], ALU.add, [b_rows], [b_GPt] + Jg.B)
                    O.tt("dve", gsl, gsl, rows[:, 1, pc * 512:(pc + 1) * 512], ALU.mult, [b_GPt, b_rows], [b_GPt])
                O.dma("sp", "gpd", gp_d, GPt[:], [b_GPt], [b_gpd])
        while cast_jobs:
            issue_cast(())
        tap("G", Gs, [128, 16], [b_prm])
        tap("shift", shift, [128, 16], [b_prm])
        S.barrier(exclude=CAST_AGENTS)

        es1 = ExitStack()
        es1.__enter__()

        def sb1(name, shape, dt=F32):
            return es1.enter_context(nc.sbuf_tensor(name, list(shape), dt))

        xt = [sb1(f"xt{i}", [128, D]) for i in range(1)]
        b_xt = bufs(1, "xt")
        junk = sb1("junk", [128, 1024])
        b_junk = Buf("junk")
        junk_bf = junk[:].bitcast(BF16)
        st = sb1("st", [128, 64])
        b_st = Buf("st")
        hT = sb1("hT", [128, KC, TT], BF16)
        b_hT = bufs(NB, "hT")
        NWR = 3
        wr = [sb1(f"wr{i}", [128, 4096], BF16) for i in range(NWR)]
        b_wr = bufs(NWR, "wr")
        qn = sb1("qn", [128, 4, TT])
        kn = sb1("kn", [128, 4, TT])
        b_qn = bufs(4, "qn")
        b_kn = bufs(4, "kn")
        mq = sb1("mq", [128, 2, TT])
        mk = sb1("mk", [128, 2, TT])
        b_mq = bufs(2, "mq")
        b_mk = bufs(2, "mk")
        cpool = sb1("cpool", [128, 2 * TT + 3 * (TT + 4)])
        cacc = [cpool[:, i * TT:(i + 1) * TT] for i in range(2)]
        cw = [cpool[:, 2 * TT + i * (TT + 4):2 * TT + i * (TT + 4) + TT + 3] for i in range(3)]
        b_cw = bufs(3, "cw")
        b_cacc = bufs(2, "cacc")
        xt2 = cpool[:, 0:D]
        b_xt2 = b_cacc + b_cw[0:2]
        halo = sb1("halo", [128, 20, 3])
        b_halo = bufs(20, "halo")
        vtm = sb1("vtm", [128, NB, 8, 128])
        b_vtm = [[Buf() for _ in range(8)] for _ in range(NB)]
        ktm = [sb1(f"ktm{i}", [128, 4, 128]) for i in range(2)]
        b_ktm = bufs(2, "ktm")
        mktm = sb1("mktm", [128, 2, 128])
        b_mktm = Buf("mktm")
        zsg = sb1("zsg", [128, NB, 1024], BF16)
        b_zsg = [[Buf() for _ in range(4)] for _ in range(NB)]
        Vp = sb1("Vp", [128, NB, 2, 257])
        b_Vp = [[Buf() for _ in range(2)] for _ in range(NB)]
        mgw = sb1("mgw", [128, NB, 512], BF16)
        b_mgw = [[Buf() for _ in range(2)] for _ in range(NB)]
        gt = sb1("gt", [128, NB, 112])
        b_gt = Buf("gt")

        class GS:
            def __init__(self, i):
                def t(n, dt=F32):
                    return sb1(f"g{n}{i}", [128, 2, 128], dt), Buf(f"g{n}{i}")
                self.A, self.bA = t("A")
                X0 = sb1(f"gX0{i}", [128, 2, 2, 128])
                X1 = sb1(f"gX1{i}", [128, 2, 2, 128])
                self.X = [X0, X1]
                self.Gp1, self.bGp1 = X0[:, :, 0, :], Buf(f"gGp1{i}")
                self.Ap0, self.bAp0 = X0[:, :, 1, :], Buf(f"gAp0{i}")
                self.Gp0, self.bGp0 = X1[:, :, 0, :], Buf(f"gGp0{i}")
                self.Ap1, self.bAp1 = X1[:, :, 1, :], Buf(f"gAp1{i}")
                self.Lp1, self.bLp1 = t("Lp1")
                self.WTb, self.bWTb = t("WTb", BF16)
                self.qdTb, self.bqdTb = t("qdTb", BF16)
                self.aTb, self.baTb = t("aTb", BF16)
                self.kttb, self.bkttb = t("kttb", BF16)
                self.vnb, self.bvnb = t("vnb", BF16)
                self.egs = sb1(f"gegs{i}", [128, 4])
                self.begs = Buf(f"gegs{i}")
        NS = 4
        gss = [GS(i) for i in range(NS)]
        Sst = sb1("Sst", [128, 8, 128])
        b_S = bufs(8, "S")
        Cst = sb1("Cst", [128, 2, 257])
        b_C = bufs(2, "C")
        otm = [sb1(f"otm{i}", [128, 8, 128]) for i in range(2)]
        b_otm = [bufs(4, f"otm{i}") for i in range(2)]
        og = [sb1(f"og{i}", [128, 1536], BF16) for i in range(2)]
        b_og = [[Buf(), Buf()] for _ in range(2)]
        gamT = [sb1(f"gamT{i}", [8, 128]) for i in range(2)]
        b_gamT = bufs(2, "gamT")
        bTt = [sb1(f"bTt{i}", [2, 128]) for i in range(2)]
        b_bTt = bufs(2, "bTt")
        T1 = sb1("T1s", [128, 2, 128])
        T2 = sb1("T2s", [128, 2, 128])
        T3 = sb1("T3s", [128, 2, 128])
        T4 = sb1("T4s", [128, 2, 128])
        b_T1, b_T2, b_T3, b_T4 = Buf("T1"), Buf("T2"), Buf("T3"), Buf("T4")
        Sb = sb1("Sb", [128, 8, 128], BF16)
        b_Sb = bufs(8, "Sb")
        mtmpU, b_mtmpU_alias = T2, b_T2
        sTm = sb1("sTm", [128, 2, 128])
        Dm = sb1("Dm", [128, 2, 128])
        Eb = sb1("Eb", [128, 2, 128])
        qbT = sb1("qbT", [128, 2, 128])
        kwm = sb1("kwm", [128, 2, 128])
        hbuf = sb1("hbuf", [128, 1, 256])
        b_sTm, b_Dm, b_Eb, b_qbT, b_kwm, b_hbuf = bufs(6, "mw")
        b_mtmpU = b_T2

        b_xsrc = bufs(SEQ // 128, "xsrc")
        b_ag = bufs(32, "ag")
        O.memset("pool", Sst[:], 0.0, b_S)
        O.memset("pool", Sb[:], 0.0, b_Sb)
        O.memset("pool", Cst[:], 0.0, b_C)
        O.memset("pool", halo[:], 0.0, b_halo)
        O.memset("pool", Vp[:], 1.0, [b for r in b_Vp for b in r])
        O.memset("pool", gt[:], 0.0, [b_gt])
        if DEBUG["on"] and nt_run < 16 and DEBUG["phase2"]:
            O.memset("pool", og[0][:], 0.0, b_og[0])
            for r_ in range(nt_run * NB, SEQ // 128):
                O.dma("sp", "zf", xsrc[r_ * 128:(r_ + 1) * 128, :], og[0][:], b_og[0], [b_xsrc[r_]])
            S.group_done("zf", b_xsrc[nt_run * NB:])

        LN_SQ = math.log(128.0 ** -0.5)
        wctr = [0]
        def issue_ag(c):
            S.dma("pool", f"cc{c % 4}", lambda e: e.collective_compute(
                "AllGather", ALU.bypass, replica_groups=GROUPS, ins=[xsrc[c * 256:(c + 1) * 256, :]], outs=[xdst[c * 1024:(c + 1) * 1024, :]]),
                [b_xsrc[2 * c], b_xsrc[2 * c + 1]], [b_ag[c]], inc=1)

        def load_w(src_ap, n, pattern, dims, src_bufs):
            sl = wctr[0] % NWR
            wctr[0] += 1
            view = wr[sl][:, 0:n].rearrange(pattern, **dims)
            O.dma("sp", f"wr{sl}", view, src_ap, src_bufs, [b_wr[sl]])
            return sl, view

        def stage_a(src, row0, blk, xslot):
            if blk % 2 == 0:
                xtile, bxl = xt[0][:], [b_xt[0]]
            else:
                xtile, bxl = xt2, b_xt2
            so = 3 * (blk % 2)
            O.dma("sp", f"xt{blk % 2}", xtile, src[row0:row0 + 128, :], (), bxl)
            O.act(junk_bf, xtile, AF.Square, bxl, [b_junk, b_st], accum=st[:, so:so + 1])
            O.act(st[:, so + 1:so + 2], st[:, so:so + 1], AF.Sqrt, [b_st], [b_st], bias=EPS, scale=1.0 / D)
            O.rec(st[:, so + 2:so + 3], st[:, so + 1:so + 2], [b_st], [b_st])
            O.ts("dve", xtile, xtile, st[:, so + 2:so + 3], ALU.mult, bxl + [b_st], bxl)
            for kq in range(4):
                J = job()
                for s_ in range(4):
                    k = kq * 4 + s_
                    O.tr(J.s4()[:, s_, :], xtile[:, k * 128:(k + 1) * 128], ident, bxl + [b_cst], J.B)
                for s_ in range(4):
                    k = kq * 4 + s_
                    if kq % 2 == 0:
                        O.act(hT[:, k, blk * 128:(blk + 1) * 128], J.s4()[:, s_, :], AF.Identity, [b_prm], [b_hT[blk]] + J.B,
                              bias=shift[:, k:k + 1], scale=Gs[:, k:k + 1])
                    else:
                        O.ts("dve", hT[:, k, blk * 128:(blk + 1) * 128], J.s4()[:, s_, :], Gs[:, k:k + 1], ALU.mult, [b_prm], [b_hT[blk]] + J.B,
                             s2=shift[:, k:k + 1], op1=ALU.add)

        for ti in range(nt_run):
            if ti % 4 == 0:
                S.new_epoch()
            tok0 = ti * TT
            for blk in range(NB):
                stage_a(x_full, tok0 + blk * 128, blk, 0)

            pending = []

            def flush(upto=10 ** 9):
                while pending and pending[0][0] <= upto:
                    pending.pop(0)[1]()

            part2 = [None]

            for jp in range(10):
                sl, wv = load_w(wfm_b[2 * jp:2 * jp + 2].rearrange("j p k c -> p j k c"), 4096, "p (j k c) -> p j k c", dict(j=2, k=KC), b_wfm[2 * jp:2 * jp + 2])
                for jj in range(2):
                    j = 2 * jp + jj
                    J = job()
                    O.mmg(J.t[:], [(wv[:, jj, k, :], hT[:, k, :]) for k in range(KC)], [b_wr[sl]] + b_hT, J.B)
                    flush(j - 3)
                    c_, bc = cw[j % 3], b_cw[j % 3]
                    ac, bac = cacc[j % 2], b_cacc[j % 2]
                    O.act(c_[:, 3:TT + 3], J.t[:], AF.Copy, [], [bc] + J.B)
                    O.cp("pool", c_[:, 0:3], halo[:, j, :], [b_halo[j]], [bc])
                    O.ts("dve", ac[:], c_[:, 3:TT + 3], convw[:, j, 3:4], ALU.mult, [bc, b_convw], [bac])
                    for d_ in (1, 2, 3):
                        O.stt(ac[:], c_[:, 3 - d_:TT + 3 - d_], convw[:, j, 3 - d_:4 - d_], ac[:], ALU.mult, ALU.add, [bc, b_convw, bac], [bac])
                    O.cp("pool", halo[:, j, :], c_[:, TT:TT + 3], [bc], [b_halo[j]])
                    if part2[0] is not None:
                        part2[0]()
                        part2[0] = None

                    def p2(j=j, c_=c_, bc=bc, ac=ac, bac=bac):
                        if j < 8:
                            isq = j < 4
                            hq = j % 4
                            dst, bd = (qn, b_qn[hq]) if isq else (kn, b_kn[hq])
                            O.act(dst[:, hq, :], ac[:], AF.Silu, [bac], [bd])
                            O.act(c_[:, 0:TT], dst[:, hq, :], AF.Square, [bd], [bc])

                            def tail():
                                J2 = job()
                                O.mm(J2.t[:], ones, c_[:, 0:TT], [bc, b_cst], J2.B)
                                bx_ = Buf()
                                O.act(J2.t[:], J2.t[:], AF.Ln, [], J2.B + [bx_], bias=EPS, scale=1.0)
                                O.act(J2.t[:], J2.t[:], AF.Exp, [bx_], J2.B, bias=(LN_SQ if isq else 0.0), scale=-0.5)
                                O.tt("dve", dst[:, hq, :], dst[:, hq, :], J2.t[:], ALU.mult, [], [bd] + J2.B)
                            pending.append((j, tail))
                        elif j < 16:
                            hv = j - 8
                            O.act(c_[:, 0:TT], ac[:], AF.Silu, [bac], [bc])

                            def tail():
                                J2 = job()
                                for blk in range(NB):
                                    O.tr(J2.s4()[:, blk, :], c_[:, blk * 128:(blk + 1) * 128], ident, [bc, b_cst], J2.B)
                                O.cp("dve", vtm[:, :, hv, :], J2.s4(), [], [b_vtm[blk][hv] for blk in range(NB)] + J2.B)
                            pending.append((j, tail))
                        else:
                            h_ = j % 2
                            dst, bd = (mq, b_mq[h_]) if j < 18 else (mk, b_mk[h_])
                            O.act(dst[:, h_, :], ac[:], AF.Silu, [bac], [bd])
                    part2[0] = p2
            part2[0]()

            sl, wv = load_w(wsm_b, KC * 20, "p (k c) -> p k c", dict(k=KC), [b_wsm])
            J = job()
            for blk in range(NB):
                O.mmg(J.t[:, blk * 20:(blk + 1) * 20], [(hT[:, k, blk * 128:(blk + 1) * 128], wv[:, k, :]) for k in range(KC)],
                      [b_wr[sl], b_hT[blk]], J.B)
            flush()

            def G_(a, b_):
                return gt[:, :, a:b_]
            RG, WG = [b_gt], [b_gt]
            O.act(G_(0, 20), J.t[:, 0:80].rearrange("p (b c) -> p b c", b=NB), AF.Copy, [], WG + J.B)
            O.tt("dve", G_(20, 28), G_(0, 8), bc_mid(dtb, NB), ALU.add, [b_gt, b_prm], WG)
            O.act(G_(20, 28), G_(20, 28), AF.Exp, RG, WG)
            O.act(G_(20, 28), G_(20, 28), AF.Ln, RG, WG, bias=1.0, scale=1.0)
            O.tt("dve", G_(20, 28), G_(20, 28), bc_mid(negA, NB), ALU.mult, [b_gt, b_prm], WG)
            O.act(G_(28, 36), G_(8, 16), AF.Sigmoid, RG, WG)
            O.tt("dve", G_(36, 38), G_(16, 18), bc_mid(prm[:, o_bi:o_bi + 2], NB), ALU.add, [b_gt, b_prm], WG)
            O.tt("dve", G_(38, 40), G_(18, 20), bc_mid(prm[:, o_bf:o_bf + 2], NB), ALU.add, [b_gt, b_prm], WG)
            O.act(G_(38, 40), G_(38, 40), AF.Exp, RG, WG, scale=-1.0)
            O.act(G_(38, 40), G_(38, 40), AF.Ln, RG, WG, bias=1.0, scale=1.0)
            O.ts("dve", G_(38, 40), G_(38, 40), -1.0, ALU.mult, RG, WG)

            def stage_d2():
                J = job()
                for blk in range(NB):
                    for (cid, a, b_, o_) in ((C_TRI64, 20, 28, 0), (C_BLK64, 20, 28, 8), (C_TRI128, 38, 40, 16), (C_ONES, 38, 40, 18)):
                        base = blk * 20 + o_
                        O.mm(J.t[:, base:base + (b_ - a)], cst[:, cid, :], gt[:, blk, a:b_], [b_gt, b_cst], J.B)
                O.act(G_(40, 60), J.t[:, 0:80].rearrange("p (b c) -> p b c", b=NB), AF.Copy, [], WG + J.B)
                O.ts("dve", G_(60, 68), G_(40, 48), -1.0, ALU.mult, RG, WG)
                O.act(G_(96, 104), G_(40, 48), AF.Exp, RG, WG)
                O.tt("dve", G_(68, 76), G_(96, 104), G_(28, 36), ALU.mult, RG, WG)
                O.tt("dve", G_(76, 84), G_(48, 56), G_(40, 48), ALU.subtract, RG, WG)
                O.act(G_(76, 84), G_(76, 84), AF.Exp, RG, WG)
                O.tt("dve", G_(104, 106), G_(36, 38), G_(56, 58), ALU.subtract, RG, WG)
                O.ts("dve", G_(84, 86), G_(104, 106), LN_SQ, ALU.add, RG, WG)
                O.tt("dve", G_(86, 88), G_(104, 106), G_(58, 60), ALU.add, RG, WG)
                O.act(G_(86, 88), G_(86, 88), AF.Exp, RG, WG)
                O.act(G_(88, 90), G_(58, 60), AF.Exp, RG, WG)

            for pc in range(10):
                sl, wv = load_w(wtm_b[pc], 4096, "p (k c) -> p k c", dict(k=KC), [b_wtm[pc]])
                tg, hf = pc // 2, pc % 2
                for blk in range(NB):
                    J = job()
                    pa = J.t[:, 0:256]
                    O.mmg(pa, [(hT[:, k, blk * 128:(blk + 1) * 128], wv[:, k, :]) for k in range(KC)], [b_wr[sl], b_hT[blk]], J.B)
                    if pc == 1 and blk == 0:
                        stage_d2()
                    if tg < 2:
                        c0 = pc * 256
                        bz = b_zsg[blk][pc]
                        zv = zsg[:, blk, c0:c0 + 256]
                        O.act(zv, pa, AF.Silu, [], [bz] + J.B)
                        zv3 = zv.rearrange("p (h d) -> p h d", h=2)
                        O.tt("pool", zv3, zv3, bc_mid(gnw[:], 2), ALU.mult, [bz, b_gnw], [bz])
                    elif tg == 2:
                        O.act(Vp[:, blk, hf, 0:256], pa, AF.Copy, [], [b_Vp[blk][hf]] + J.B)
                    elif tg == 3:
                        mv_ = mgw[:, blk, hf * 256:(hf + 1) * 256]
                        O.act(mv_, pa, AF.Sigmoid, [], [b_mgw[blk][hf]] + J.B)
                        O.tt("pool", mv_, mv_, mnw[:, hf * 256:(hf + 1) * 256], ALU.mult, [b_mgw[blk][hf], b_mnw], [b_mgw[blk][hf]])
                    else:
                        ac, bac = cacc[blk % 2], b_cacc[blk % 2]
                        mv_ = mgw[:, blk, hf * 256:(hf + 1) * 256]
                        O.act(ac[:, 0:256], pa, AF.Silu, [], [bac] + J.B)
                        O.tt("dve", mv_, mv_, ac[:, 0:256], ALU.mult, [bac, b_mgw[blk][hf]], [b_mgw[blk][hf]])
            if ti >= 1 and DEBUG["phase2"]:
                issue_ag(2 * (ti - 1))
                issue_ag(2 * (ti - 1) + 1)
            if ti == min(1, nt_run - 1) and DEBUG["phase2"]:
                for i in range(0, 32, 8):
                    cast(wgate_b, w_gate, i, i + 8, b_wgate, "c2")
                for i in range(0, 16, 4):
                    cast(wpa_b, wpa, i, i + 4, b_wpa, "c2")
                for i in range(0, 16, 8):
                    cast(wpb_b, wpb, i, i + 8, b_wpb, "c2")
                for i in range(0, 8, 4):
                    cast(wout_b, wout, i, i + 4, b_wout, "c2")
                S.group_done("c2", b_wgate + b_wpa + b_wpb + b_wout)
            if ti == 0:
                tap("gt", gt[:].rearrange("p b c -> p (b c)"), [128, NB * 112], [b_gt])
                tap("qn", qn[:].rearrange("p h t -> p (h t)"), [128, 4 * TT], b_qn)
                tap("kn", kn[:].rearrange("p h t -> p (h t)"), [128, 4 * TT], b_kn)
                tap("mq", mq[:].rearrange("p h t -> p (h t)"), [128, 2 * TT], b_mq)
                tap("vtm", vtm[:].rearrange("p b h d -> p (b h d)"), [128, NB * 8 * 128], [b for r in b_vtm for b in r])

            def emit_gamT(blk):
                J = job()
                O.mm(J.t[0:8, 0:128], gt[:, blk, 20:28], cst[:, C_TRI64, :], [b_gt, b_cst], J.B)
                O.act(gamT[blk % 2][:], J.t[0:8, 0:128], AF.Copy, [], [b_gamT[blk % 2]] + J.B)

            def emit_bT(blk):
                J = job()
                O.mm(J.t[0:2, 0:128], gt[:, blk, 38:40], cst[:, C_TRI128, :], [b_gt, b_cst], J.B)
                O.act(bTt[blk % 2][:], J.t[0:2, 0:128], AF.Copy, [], [b_bTt[blk % 2]] + J.B)

            emit_gamT(0)
            emit_bT(0)

            def pair_gen(blk, hq, g):
                tsl = slice(blk * 128, (blk + 1) * 128)
                hv0 = 2 * hq
                kt_, bkt = ktm[blk % 2], b_ktm[blk % 2]
                ot_, bot = otm[blk % 2], b_otm[blk % 2][hq]

                def gc(a):
                    return gt[:, blk, a + hv0:a + hv0 + 2]
                if hq == 0:
                    J = job()
                    for h4 in range(4):
                        O.tr(J.s4()[:, h4, :], kn[:, h4, tsl], ident, [b_kn[h4], b_cst], J.B)
                    O.act(kt_[:], J.s4(), AF.Copy, [], [bkt] + J.B)
                Jk = job()
                O.mm(Jk.s4()[:, 0, :], kn[:, hq, tsl], kn[:, hq, tsl], [b_kn[hq]], Jk.B)
                O.mm(Jk.s4()[:, 1, :], kn[:, hq, tsl], qn[:, hq, tsl], [b_kn[hq], b_qn[hq]], Jk.B)
                if hq == 0 and blk + 1 < NB:
                    emit_gamT(blk + 1)
                Jr = job()
                prow = Jr.s4()[:, 0:2, :]
                for e_ in range(2):
                    hv = hv0 + e_
                    O.mm(Jr.s4()[:, e_, :], cst[0:8, C_ID, hv:hv + 1].broadcast_to([8, 128]), gamT[blk % 2][:], [b_gamT[blk % 2], b_cst], Jr.B)
                O.tt("dve", T1[:], prow, bc_mid(cst[:, C_MASKL, :], 2), ALU.add, [b_cst], [b_T1] + Jr.B)
                O.tt("dve", T2[:], prow, bc_mid(cst[:, C_MASKU, :], 2), ALU.add, [b_cst], [b_T2] + Jr.B)
                O.act(T4[:], prow, AF.Exp, [], [b_T4] + Jr.B)
                for e_ in range(2):
                    hv = hv0 + e_
                    O.act(g.A[:, e_, :], T1[:, e_, :], AF.Exp, [b_gt, b_T1], [g.bA], bias=gt[:, blk, 40 + hv:41 + hv], scale=-1.0)
                    O.act(T3[:, e_, :], T2[:, e_, :], AF.Exp, [b_gt, b_T2], [b_T3], bias=gt[:, blk, 60 + hv:61 + hv], scale=1.0)
                for e_ in range(2):
                    hv = hv0 + e_
                    O.stt(g.A[:, e_, :], Jk.s4()[:, 0, :], gt[:, blk, 28 + hv:29 + hv], g.A[:, e_, :], ALU.mult, ALU.mult, [b_gt], [g.bA] + Jk.B)
                O.tt("dve", g.aTb[:], bc_mid(Jk.s4()[:, 1, :], 2), T3[:], ALU.mult, [b_T3], [g.baTb] + Jk.B)
                O.cp("pool", g.egs[:, 0:2], T4[:, :, 63], [b_T4], [g.begs])
                O.cp("pool", g.egs[:, 2:4], T4[:, :, 127], [b_T4], [g.begs])
                O.tt("dve", g.qdTb[:], bc_mid(qn[:, hq, tsl], 2), T4[:], ALU.mult, [b_qn[hq], b_T4], [g.bqdTb])
                O.tt("pool", g.kttb[:], bc_mid(kt_[:, hq, :], 2), bc_last(gc(76), 128), ALU.mult, [bkt, b_gt], [g.bkttb])
                yield
                Ja = job()
                for e_ in range(2):
                    O.tr(Ja.s4()[:, e_, :], g.A[:, e_, :], ident, [g.bA, b_cst], Ja.B)
                O.act(g.Ap0[:], Ja.s4()[:, 0:2, :], AF.Copy, [], [g.bAp0] + Ja.B)
                O.tt("dve", g.Gp0[:], bc_mid(ident, 2), Ja.s4()[:, 0:2, :], ALU.subtract, [b_cst], [g.bGp0] + Ja.B)
                yield
                Lp = [(g.A, g.bA), (g.Lp1, g.bLp1)]
                Ap = [(g.Ap0, g.bAp0), (g.Ap1, g.bAp1)]
                Gp = [(g.Gp0, g.bGp0), (g.Gp1, g.bGp1)]
                cur = 0
                for rnd in range(6):
                    nxt = 1 - cur
                    (Lc, bLc), (Ac, bAc) = Lp[cur], Ap[cur]
                    (Ln, bLn), (An, bAn) = Lp[nxt], Ap[nxt]
                    (Gs_, bGs), (Gd, bGd) = Gp[nxt], Gp[cur]
                    comb = 1 <= rnd <= 3
                    if comb:
                        JX = job()
                        jx = JX.t[:].rearrange("p (e a d) -> p e a d", e=2, a=2)
                        for e_ in range(2):
                            O.mm(jx[:, e_, :, :], Lc[:, e_, :], g.X[cur][:, e_, :, :], [bLc, bGs, bAc], JX.B)
                    elif rnd >= 1:
                        J3 = job()
                        for e_ in range(2):
                            O.mm(J3.s4()[:, e_, :], Lc[:, e_, :], Gs_[:, e_, :], [bLc, bGs], J3.B)
                    if rnd <= 4:
                        J1 = job()
                        for e_ in range(2):
                            O.mm(J1.s4()[:, e_, :], Ac[:, e_, :], Lc[:, e_, :], [bAc, bLc], J1.B)
                    if rnd == 0:
                        J2 = job()
                        for e_ in range(2):
                            O.mm(J2.s4()[:, e_, :], Lc[:, e_, :], Ac[:, e_, :], [bAc, bLc], J2.B)
                    if comb:
                        O.tt("dve", Gd[:], jx[:, :, 0, :], Gs_[:], ALU.add, [bGs], [bGd] + JX.B)
                    elif rnd >= 1:
                        O.tt("dve", Gd[:], J3.s4()[:, 0:2, :], Gs_[:], ALU.add, [bGs], [bGd] + J3.B)
                    if rnd <= 4:
                        O.act(Ln[:], J1.s4()[:, 0:2, :], AF.Copy, [], [bLn] + J1.B)
                    if comb:
                        O.act(An[:], jx[:, :, 1, :], AF.Copy, [], [bAn] + JX.B)
                    elif rnd == 0:
                        O.act(An[:], J2.s4()[:, 0:2, :], AF.Copy, [], [bAn] + J2.B)
                    yield
                    cur = nxt
                TTt, b_TT = Gp[1]
                kbg, bkbg = g.A, g.bA
                vb, bvb = g.Lp1, g.bLp1
                U, bU = g.Ap1, g.bAp1
                O.tt("pool", kbg[:], bc_mid(kt_[:, hq, :], 2), bc_last(gc(68), 128), ALU.mult, [bkt, b_gt], [bkbg])
                O.tt("pool", vb[:], vtm[:, blk, hv0:hv0 + 2, :], bc_last(gc(28), 128), ALU.mult, [b_vtm[blk][hv0], b_vtm[blk][hv0 + 1], b_gt], [bvb])
                yield
                Jw = job()
                for e_ in range(2):
                    O.mm(Jw.s4()[:, e_, :], kbg[:, e_, :], TTt[:, e_, :], [bkbg, b_TT], Jw.B)
                O.act(g.WTb[:], Jw.s4()[:, 0:2, :], AF.Copy, [], [g.bWTb] + Jw.B)
                Ju = job()
                for e_ in range(2):
                    O.mm(Ju.s4()[:, e_, :], TTt[:, e_, :], vb[:, e_, :], [bvb, b_TT], Ju.B)
                O.act(U[:], Ju.s4()[:, 0:2, :], AF.Copy, [], [bU] + Ju.B)
                yield
                for c in range(2):
                    cs = slice(64 * c, 64 * c + 64)
                    J1 = job()
                    for e_ in range(2):
                        O.mm(J1.s4()[:, e_, :], g.WTb[:, e_, :], Sb[:, hv0 + e_, :], [g.bWTb, b_Sb[hv0 + e_]], J1.B)
                    O.tt("dve", g.vnb[cs, :, :], U[cs, :, :], J1.s4()[cs, 0:2, :], ALU.subtract, [bU], [g.bvnb] + J1.B)
                    yield
                    J2 = job()
                    for e_ in range(2):
                        O.mmg(J2.s4()[:, e_, :], [(g.qdTb[:, e_, :], Sb[:, hv0 + e_, :]), (g.aTb[cs, e_, :], g.vnb[cs, e_, :])],
                              [g.bqdTb, b_Sb[hv0 + e_], g.baTb, g.bvnb], J2.B)
                    J3 = job()
                    for e_ in range(2):
                        O.mm(J3.s4()[:, e_, :], g.kttb[cs, e_, :], g.vnb[cs, e_, :], [g.bkttb, g.bvnb], J3.B)
                    for e_ in range(2):
                        O.stt(Sst[:, hv0 + e_, :], Sst[:, hv0 + e_, :], g.egs[:, 2 * c + e_:2 * c + e_ + 1], J3.s4()[:, e_, :], ALU.mult, ALU.add,
                              [g.begs], [b_S[hv0 + e_]] + J3.B)
                    O.act(Sb[:, hv0:hv0 + 2, :], Sst[:, hv0:hv0 + 2, :], AF.Copy, [b_S[hv0], b_S[hv0 + 1]], [b_Sb[hv0], b_Sb[hv0 + 1]])
                    O.act(ot_[cs, hv0:hv0 + 2, :], J2.s4()[cs, 0:2, :], AF.Copy, [], [bot] + J2.B)
                    yield

            def gdn_epilogue(blk):
                ogs = (ti * NB + blk) % 2
                ogt = og[ogs]
                ot_, bot = otm[blk % 2], b_otm[blk % 2]
                if ti == 0 and blk == 0:
                    tap("otm", ot_[:].rearrange("p h d -> p (h d)"), [128, 1024], bot)
                    tap("S0", Sst[:].rearrange("p h d -> p (h d)"), [128, 1024], b_S)
                j3 = junk[:].rearrange("p (h d) -> p h d", h=8)
                O.tt("dve", j3, ot_[:], ot_[:], ALU.mult, bot, [b_junk])
                O.red(st[:, 8:16], j3, [b_junk], [b_st])
                O.act(st[:, 8:16], st[:, 8:16], AF.Sqrt, [b_st], [b_st], bias=EPS, scale=1.0 / 128)
                O.rec(st[:, 16:24], st[:, 8:16], [b_st], [b_st])
                O.tt("dve", ot_[:], ot_[:], bc_last(st[:, 16:24], 128), ALU.mult, bot + [b_st], bot)
                O.tt("dve", ogt[:, 0:1024], ot_[:].rearrange("p h d -> p (h d)"), zsg[:, blk, :], ALU.mult, bot + b_zsg[blk], [b_og[ogs][0]])

            def mlstm_gen(blk):
                tsl = slice(blk * 128, (blk + 1) * 128)
                ogs = (ti * NB + blk) % 2
                ogt = og[ogs]
                J = job()
                for h_ in range(2):
                    O.tr(J.s4()[:, h_, :], mk[:, h_, tsl], ident, [b_mk[h_], b_cst], J.B)
                O.act(mktm[:], J.s4()[:, 0:2, :], AF.Copy, [], [b_mktm] + J.B)
                if blk + 1 < NB:
                    emit_bT(blk + 1)
                Jr = job()
                prow = Jr.s4()[:, 0:2, :]
                for h_ in range(2):
                    O.mm(Jr.s4()[:, h_, :], cst[0:2, C_ID, h_:h_ + 1].broadcast_to([2, 128]), bTt[blk % 2][:], [b_bTt[blk % 2], b_cst], Jr.B)
                Jq = job()
                for h_ in range(2):
                    O.mm(Jq.s4()[:, h_, :], mk[:, h_, tsl], mq[:, h_, tsl], [b_mk[h_], b_mq[h_]], Jq.B)
                O.tt("dve", mtmpU[:], prow, bc_mid(cst[:, C_MASKU128, :], 2), ALU.add, [b_cst], [b_mtmpU] + Jr.B)
                for h_ in range(2):
                    O.act(Dm[:, h_, :], mtmpU[:, h_, :], AF.Exp, [b_mtmpU, b_gt], [b_Dm], bias=gt[:, blk, 84 + h_:85 + h_], scale=1.0)
                O.act(Eb[:], prow, AF.Exp, [], [b_Eb] + Jr.B, bias=LN_SQ, scale=1.0)
                O.tt("dve", sTm[:], Jq.s4()[:, 0:2, :], Dm[:], ALU.mult, [b_Dm], [b_sTm] + Jq.B)
                O.tt("dve", qbT[:], mq[:, :, tsl], Eb[:], ALU.mult, b_mq + [b_Eb], [b_qbT])
                O.tt("pool", kwm[:], mktm[:], bc_last(gt[:, blk, 86:88], 128), ALU.mult, [b_mktm, b_gt], [b_kwm])
                yield
                for h_ in range(2):
                    Jn = job()
                    pnv = Jn.t[:]
                    O.mmg(pnv[:, 0:257], [(qbT[:, h_, :], Cst[:, h_, :]), (sTm[:, h_, :], Vp[:, blk, h_, :])],
                          [b_qbT, b_C[h_], b_sTm, b_Vp[blk][h_]], Jn.B)
                    Jc = job()
                    O.mm(Jc.t[:, 0:257], kwm[:, h_, :], Vp[:, blk, h_, :], [b_kwm, b_Vp[blk][h_]], Jc.B)
                    O.act(st[:, 23:24], pnv[:, 256:257], AF.Copy, [], [b_st] + Jn.B)
                    O.stt(st[:, 24:25], st[:, 23:24], -1.0, st[:, 23:24], ALU.mult, ALU.max, [b_st], [b_st])
                    O.ts("dve", st[:, 24:25], st[:, 24:25], 1.0, ALU.max, [b_st], [b_st])
                    O.rec(st[:, 25:26], st[:, 24:25], [b_st], [b_st])
                    O.act(hbuf[:, 0, :], pnv[:, 0:256], AF.Copy, [b_st], [b_hbuf] + Jn.B, scale=st[:, 25:26])
                    O.stt(Cst[:, h_, :], Cst[:, h_, :], gt[:, blk, 88 + h_:89 + h_], Jc.t[:, 0:257], ALU.mult, ALU.add, [b_gt], [b_C[h_]] + Jc.B)
                    O.act(junk[:, 0:256], hbuf[:, 0, :], AF.Square, [b_hbuf], [b_junk, b_st], accum=st[:, 26:27])
                    O.act(st[:, 27:28], st[:, 26:27], AF.Sqrt, [b_st], [b_st], bias=EPS, scale=1.0 / 256)
                    O.rec(st[:, 28:29], st[:, 27:28], [b_st], [b_st])
                    O.stt(ogt[:, 1024 + h_ * 256:1024 + (h_ + 1) * 256], hbuf[:, 0, :], st[:, 28:29], mgw[:, blk, h_ * 256:(h_ + 1) * 256], ALU.mult, ALU.mult,
                          [b_hbuf, b_st, b_mgw[blk][h_]], [b_og[ogs][1]])
                    yield

            def og_store(blk):
                ogs = (ti * NB + blk) % 2
                r0 = tok0 + blk * 128
                O.dma("sp", f"og{ogs}", xsrc[r0:r0 + 128, :], og[ogs][:], b_og[ogs], [b_xsrc[r0 // 128]])

            pair_q = [(blk, hq) for blk in range(NB) for hq in range(4)]
            free_sets = list(range(NS))
            active = []
            done_pairs = [0] * NB
            ml_q = list(range(NB))
            ml_active = None
            ml_done = [False] * NB
            epi_done = [False] * NB
            stored = [False] * NB

            def try_store():
                for b_ in range(NB):
                    if (not stored[b_]) and epi_done[b_] and ml_done[b_]:
                        og_store(b_)
                        stored[b_] = True

            def drain_ml(upto):
                nonlocal ml_active
                while True:
                    if ml_active is None:
                        if ml_q and ml_q[0] <= upto:
                            b_ = ml_q.pop(0)
                            ml_active = (mlstm_gen(b_), b_)
                        else:
                            return
                    try:
                        while True:
                            next(ml_active[0])
                    except StopIteration:
                        ml_done[ml_active[1]] = True
                        ml_active = None

            while pair_q or active or ml_q or ml_active is not None:
                while pair_q and free_sets:
                    blk_, hq_ = pair_q.pop(0)
                    si = free_sets.pop(0)
                    active.append([pair_gen(blk_, hq_, gss[si]), si, blk_])
                for item in list(active):
                    try:
                        next(item[0])
                    except StopIteration:
                        active.remove(item)
                        free_sets.append(item[1])
                        done_pairs[item[2]] += 1
                        if done_pairs[item[2]] == 4:
                            bb = item[2]
                            if bb >= 2:
                                drain_ml(bb - 2)
                                try_store()
                            gdn_epilogue(bb)
                            epi_done[bb] = True
                            try_store()
                if ml_active is None and ml_q and (ml_q[0] < 2 or stored[ml_q[0] - 2]):
                    b_ = ml_q.pop(0)
                    ml_active = (mlstm_gen(b_), b_)
                if ml_active is not None:
                    try:
                        next(ml_active[0])
                    except StopIteration:
                        ml_done[ml_active[1]] = True
                        ml_active = None
                        try_store()
                if not active and not pair_q and ml_active is None and ml_q and not (ml_q[0] < 2 or stored[ml_q[0] - 2]):
                    try_store()
            try_store()
            assert all(stored), stored

        es1.__exit__(None, None, None)

        if DEBUG["phase2"]:
            last_c = 2 * (nt_run - 1)
            n_c = 32 if (DEBUG["on"] and nt_run < 16) else 2 * nt_run
            for c in range(last_c, n_c):
                issue_ag(c)
            phase2(nc, S, O, job, locals())
        else:
            if DEBUG["on"]:
                o = nc.dram_tensor("tap_xsrc", [nt_run * TT, 1536], BF16, kind="ExternalOutput").ap()
                O.dma("pool", "tapx", o, xsrc[0:nt_run * TT, :], b_xsrc[0:nt_run * NB], [Buf()])
                taps.append("xsrc")
            S.barrier()
            zz = sb("zz", [128, D])
            bz = Buf()
            O.memset("pool", zz[:], 0.0, [bz])
            O.dma("sp", "yz", y_out[0:128, :], zz[:], [bz], [Buf()])
        S.wait_all_dma("sp")
        S.wait_all_dma("pool")
        S.emit()
    return nc


def phase2(nc, S, O, job, L):
    x_own, y_out, xdst = L["x_own"], L["y_out"], L["xdst"]
    wgate_b, wpa_b, wpb_b, wout_b = L["wgate_b"], L["wpa_b"], L["wpb_b"], L["wout_b"]
    b_wgate, b_wpa, b_wpb, b_wout = L["b_wgate"], L["b_wpa"], L["b_wpb"], L["b_wout"]
    cst, b_cst, b_prm = L["cst"], L["b_cst"], L["b_prm"]
    identb, b_identb = L["identb"], L["b_identb"]
    gp_d, b_gpd = L["gp_d"], L["b_gpd"]
    ident = cst[:, C_ID, :]
    Gs, shift = L["Gs"], L["shift"]
    S.new_epoch()
    S.barrier(exclude=("dma:cc0", "dma:cc1", "dma:cc2", "dma:cc3", "dma:c2", "dma:gpd") + L["CAST_AGENTS"])
    with ExitStack() as es:
        def sb(name, shape, dt=F32):
            return es.enter_context(nc.sbuf_tensor(name, list(shape), dt))
        GP = sb("GP", [128, D])
        b_GP = Buf("GP")
        O.dma("sp", "gpl", GP[:], gp_d, [b_gpd], [b_GP])
        xo = sb("xo", [128, D])
        b_xo = Buf("xo")
        junk = sb("junk2", [128, D], BF16)
        b_junk = Buf()
        st = sb("st2", [128, 16])
        b_st = Buf()
        hT2 = sb("hT2", [128, KC, TT], BF16)
        b_hT2 = bufs(NB, "hT2")
        NWQ = 3
        wr = [sb(f"wq{i}", [128, 4096], BF16) for i in range(NWQ)]
        b_wr = bufs(NWQ, "wq")
        sg = sb("sg", [128, 32 * TT], BF16)
        b_sg = bufs(32, "sg")
        sg3 = sg[:].rearrange("p (j t) -> p j t", j=32)
        outsb = sg[:].bitcast(F32).rearrange("p (b n) -> p b n", b=NB)
        araw2 = [sb(f"araw{i}", [128, 4, 1536], BF16) for i in range(2)]
        b_araw2 = bufs(2, "araw")
        aT = sb("aT", [128, 48, TT], BF16)
        b_aT = bufs(NB, "aT")
        mT = sb("mT", [128, KC, TT], BF16)
        b_mT = bufs(KC, "mT")
        tmpm = [sb(f"tmpm{i}", [128, TT]) for i in range(2)]
        b_tmpm = bufs(2, "tmpm")
        ysb = sb("ysb", [128, D])
        b_ysb = Buf("ysb")

        xg = nc.dram_tensor("xg", [8192, 1536], BF16).ap()
        b_xg = Buf("xg")
        for i4 in range(4):
            def dyn1(e, i4=i4):
                pid = e.partition_id()
                g = pid % 4
                return e.dma_start(out=xg[i4 * 2048:(i4 + 1) * 2048, :], in_=xdst[bass.ds(g * 8192 + i4 * 2048, 2048), :])
            S.dma("pool", "xg", dyn1, L["b_ag"], [b_xg])

        wctr = [0]

        def load_w(src_ap, n, pattern, dims, src_bufs):
            sl = wctr[0] % NWQ
            wctr[0] += 1
            view = wr[sl][:, 0:n].rearrange(pattern, **dims)
            O.dma("sp", f"wq{sl}", view, src_ap, src_bufs, [b_wr[sl]])
            return sl, view

        for t2 in range(DEBUG.get("p2_tiles", 4)):
            for blk in range(NB):
                row0 = t2 * TT + blk * 128
                xb, b_xb, xslot = (xo, b_xo, "xo") if blk % 2 == 0 else (ysb, b_ysb, "xo2")
                so = 3 * (blk % 2)
                O.dma("sp", xslot, xb[:], x_own[row0:row0 + 128, :], (), [b_xb])
                O.act(junk[:], xb[:], AF.Square, [b_xb], [b_junk, b_st], accum=st[:, so:so + 1])
                O.act(st[:, so + 1:so + 2], st[:, so:so + 1], AF.Sqrt, [b_st], [b_st], bias=EPS, scale=1.0 / D)
                O.rec(st[:, so + 2:so + 3], st[:, so + 1:so + 2], [b_st], [b_st])
                O.ts("dve", xb[:], xb[:], st[:, so + 2:so + 3], ALU.mult, [b_xb, b_st], [b_xb])
                for kq in range(4):
                    J = job()
                    for s_ in range(4):
                        k = kq * 4 + s_
                        O.tr(J.s4()[:, s_, :], xb[:, k * 128:(k + 1) * 128], ident, [b_xb, b_cst], J.B)
                    for s_ in range(4):
                        k = kq * 4 + s_
                        if kq % 2 == 0:
                            O.act(hT2[:, k, blk * 128:(blk + 1) * 128], J.s4()[:, s_, :], AF.Identity, [b_prm], [b_hT2[blk]] + J.B,
                                  bias=shift[:, k:k + 1], scale=Gs[:, k:k + 1])
                        else:
                            O.ts("dve", hT2[:, k, blk * 128:(blk + 1) * 128], J.s4()[:, s_, :], Gs[:, k:k + 1], ALU.mult, [b_prm], [b_hT2[blk]] + J.B,
                                 s2=shift[:, k:k + 1], op1=ALU.add)
            for jp in range(16):
                sl, wv = load_w(wgate_b[2 * jp:2 * jp + 2].rearrange("j p k c -> p j k c"), 4096, "p (j k c) -> p j k c", dict(j=2, k=KC), b_wgate[2 * jp:2 * jp + 2])
                for jj in range(2):
                    j = 2 * jp + jj
                    J = job()
                    O.mmg(J.t[:], [(wv[:, jj, k, :], hT2[:, k, :]) for k in range(KC)], [b_wr[sl]] + b_hT2, J.B)
                    O.act(sg3[:, j, :], J.t[:], AF.Sigmoid, [], [b_sg[j]] + J.B)
            for blk in range(NB):
                row0 = t2 * TT + blk * 128
                araw, b_araw = araw2[blk % 2], b_araw2[blk % 2]
                for r in range(4):
                    xrow = (row0 // 256) * 1024 + r * 256 + (row0 % 256)
                    O.dma("sp", f"araw{blk % 2}", araw[:, r, :], xg[xrow:xrow + 128, :], [b_xg], [b_araw])
                for r in range(4):
                    for grp in range(3):
                        fc0 = (r * 8 + grp * 4) if grp < 2 else (32 + r * 4)
                        J = job()
                        jb = J.t[:].bitcast(BF16).rearrange("p (s d) -> p s d", s=8)
                        for s_ in range(4):
                            cc = grp * 4 + s_
                            O.tr(jb[:, s_, :], araw[:, r, cc * 128:(cc + 1) * 128], identb[:], [b_araw, b_identb], J.B)
                        if (r * 3 + grp) % 2 == 0:
                            O.act(aT[:, fc0:fc0 + 4, blk * 128:(blk + 1) * 128], jb[:, 0:4, :], AF.Copy, [], [b_aT[blk]] + J.B)
                        else:
                            O.cp("dve", aT[:, fc0:fc0 + 4, blk * 128:(blk + 1) * 128], jb[:, 0:4, :], [], [b_aT[blk]] + J.B)
            for cb in range(16):
                Ja, Jb = job(), job()
                sl, wv = load_w(wpa_b[cb], 4096, "p (k c) -> p k c", dict(k=32), [b_wpa[cb]])
                O.mmg(Ja.t[:], [(wv[:, k, :], aT[:, k, :]) for k in range(32)], [b_wr[sl]] + b_aT, Ja.B)
                sl2, wv2 = load_w(wpb_b[cb], 2048, "p (k c) -> p k c", dict(k=16), [b_wpb[cb]])
                O.mmg(Jb.t[:], [(wv2[:, k, :], aT[:, 32 + k, :]) for k in range(16)], [b_wr[sl2]] + b_aT, Jb.B)
                O.tt("dve", tmpm[0][:], Ja.t[:], sg3[:, cb, :], ALU.mult, [b_sg[cb]], [b_tmpm[0]] + Ja.B)
                O.tt("dve", tmpm[1][:], Jb.t[:], sg3[:, 16 + cb, :], ALU.mult, [b_sg[16 + cb]], [b_tmpm[1]] + Jb.B)
                O.tt("pool", mT[:, cb, :], tmpm[0][:], tmpm[1][:], ALU.add, b_tmpm, [b_mT[cb]])
            b_out = bufs(NB, "outsb")
            for b_ in b_out:
                for bs_ in b_sg:
                    for ag, c in list(bs_.r.items()) + ([bs_.w] if bs_.w is not None else []):
                        if b_.r.get(ag, 0) < c:
                            b_.r[ag] = c
            for pcs in range(8):
                sl, wv = load_w(wout_b[pcs], 4096, "p (k c) -> p k c", dict(k=KC), [b_wout[pcs]])
                for blk in range(NB):
                    J = job()
                    O.mmg(J.t[:, 0:256], [(mT[:, k, blk * 128:(blk + 1) * 128], wv[:, k, :]) for k in range(KC)], [b_wr[sl]] + b_mT, J.B)
                    dstv = outsb[:, blk, pcs * 256:(pcs + 1) * 256]
                    if (pcs + blk) % 2 == 0:
                        O.act(dstv, J.t[:, 0:256], AF.Copy, [], [b_out[blk]] + J.B)
                    else:
                        O.cp("dve", dstv, J.t[:, 0:256], [], [b_out[blk]] + J.B)
            for blk in range(NB):
                row0 = t2 * TT + blk * 128
                O.dma("sp", "xo", xo[:], x_own[row0:row0 + 128, :], (), [b_xo])
                O.act(junk[:], outsb[:, blk, :], AF.Square, [b_out[blk]], [b_junk, b_st], accum=st[:, 8:9])
                O.act(st[:, 9:10], st[:, 8:9], AF.Sqrt, [b_st], [b_st], bias=EPS, scale=1.0 / D)
                O.rec(st[:, 10:11], st[:, 9:10], [b_st], [b_st])
                O.stt(ysb[:], outsb[:, blk, :], st[:, 10:11], GP[:], ALU.mult, ALU.mult, [b_out[blk], b_st, b_GP], [b_ysb])
                O.tt("pool", ysb[:], ysb[:], xo[:], ALU.add, [b_ysb, b_xo], [b_ysb])
                O.dma("sp", "yo", y_out[row0:row0 + 128, :], ysb[:], [b_ysb], [Buf()])
            for b_ in b_sg:
                for bo in b_out:
                    for ag, c in bo.r.items():
                        if b_.r.get(ag, 0) < c:
                            b_.r[ag] = c
                    if bo.w is not None:
                        ag, c = bo.w
                        if b_.r.get(ag, 0) < c:
                            b_.r[ag] = c


def host_inputs(inp):
    f = np.float32
    x = np.asarray(inp["x"], f)
    c = np.asarray(inp["c"], f)
    w_ada = np.ascontiguousarray(np.asarray(inp["w_ada"], f)[0])
    b_ada = np.asarray(inp["b_ada"], f)[0]
    w_in = np.asarray(inp["w_in"], f)[0]
    gconv = np.asarray(inp["gdn_conv_w"], f)[0]
    mconv = np.asarray(inp["mlstm_conv_w"], f)[0]

    def fm_layout(cols):
        n = cols.shape[1] // 128
        return np.ascontiguousarray(cols.reshape(KC, 128, n, 128).transpose(2, 1, 0, 3))

    def tm_layout(cols, w):
        n = cols.shape[1] // w
        return np.ascontiguousarray(cols.reshape(KC, 128, n, w).transpose(2, 1, 0, 3))

    w_gate = fm_layout(w_in[:, 20560:24656])
    wpa_full = np.asarray(inp["w_proj_gdn"], f)[0]
    wpb_full = np.asarray(inp["w_proj_mlstm"], f)[0]
    wout_full = np.asarray(inp["w_out"], f)[0]
    wpa = np.ascontiguousarray(wpa_full.reshape(32, 128, 16, 128).transpose(2, 1, 0, 3))
    wpb = np.ascontiguousarray(wpb_full.reshape(16, 128, 16, 128).transpose(2, 1, 0, 3))
    wout = tm_layout(wout_full, 256)
    consts = make_consts()
    npw = np.asarray(inp["norm_post_w"], f)[0][None, :]
    normw_t = np.ascontiguousarray(np.asarray(inp["norm_pre_w"], f)[0].reshape(KC, 128).T)
    b_ada_t = np.ascontiguousarray(b_ada[:4096].reshape(32, 128).T)
    b_gate_row = np.ascontiguousarray(b_ada[4096:][None, :])
    maps = []
    for core in range(8):
        b, g = core // 4, core % 4
        gq = w_in[:, g * 512:(g + 1) * 512]
        gk = w_in[:, 2048 + g * 512:2048 + (g + 1) * 512]
        gv = w_in[:, 4096 + g * 1024:4096 + (g + 1) * 1024]
        mqc = w_in[:, 12352 + g * 256:12352 + (g + 1) * 256]
        mkc = w_in[:, 13376 + g * 256:13376 + (g + 1) * 256]
        w_fm = fm_layout(np.concatenate([gq, gk, gv, mqc, mkc], axis=1))
        gz = w_in[:, 8256 + g * 1024:8256 + (g + 1) * 1024]
        mv = w_in[:, 14400 + g * 512:14400 + (g + 1) * 512]
        mo = w_in[:, 16464 + g * 512:16464 + (g + 1) * 512]
        mz = w_in[:, 18512 + g * 512:18512 + (g + 1) * 512]
        w_tm = tm_layout(np.concatenate([gz, mv, mo, mz], axis=1), 256)
        sm = np.concatenate([w_in[:, 8192 + g * 8:8192 + (g + 1) * 8], w_in[:, 8224 + g * 8:8224 + (g + 1) * 8],
                             w_in[:, 16448 + g * 2:16448 + (g + 1) * 2], w_in[:, 16456 + g * 2:16456 + (g + 1) * 2]], axis=1)
        w_sm = np.ascontiguousarray(sm.reshape(KC, 128, 20).transpose(1, 0, 2))
        cv = np.concatenate([gconv[:, g * 512:(g + 1) * 512], gconv[:, 2048 + g * 512:2048 + (g + 1) * 512],
                             gconv[:, 4096 + g * 1024:4096 + (g + 1) * 1024],
                             mconv[:, g * 256:(g + 1) * 256], mconv[:, 1024 + g * 256:1024 + (g + 1) * 256]], axis=1)
        convw = np.ascontiguousarray(cv.reshape(4, 20, 128).transpose(2, 1, 0))
        m = {
            "x_full": x[b], "x_own": np.ascontiguousarray(x[b, g * 2048:(g + 1) * 2048]),
            "c_t": np.ascontiguousarray(c[b].reshape(KC, 128).T),
            "w_ada": w_ada, "b_ada_t": b_ada_t, "b_gate_row": b_gate_row, "normw_t": normw_t,
            "w_fm": w_fm, "w_tm": w_tm, "w_sm": w_sm, "w_gate": w_gate, "convw": convw,
            "alog": np.asarray(inp["gdn_A_log"], f)[0][None, g * 8:(g + 1) * 8].copy(),
            "dtb": np.asarray(inp["gdn_dt_bias"], f)[0][None, g * 8:(g + 1) * 8].copy(),
            "gnw": np.asarray(inp["gdn_norm_w"], f)[0][None, :].copy(),
            "mbi": np.asarray(inp["mlstm_b_i"], f)[0][None, g * 2:(g + 1) * 2].copy(),
            "mbf": np.asarray(inp["mlstm_b_f"], f)[0][None, g * 2:(g + 1) * 2].copy(),
            "mnw": np.asarray(inp["mlstm_norm_w"], f)[0][None, g * 512:(g + 1) * 512].copy(),
            "wpa": wpa, "wpb": wpb, "wout": wout, "npw": npw, "consts": consts,
        }
        maps.append(m)
    return maps


def kernel(**inputs):
    maps = host_inputs(inputs)
    nc = build_program()
    res = run_bass_kernel_spmd(nc, maps, core_ids=list(range(8)))
    out = np.zeros((2, SEQ, D), np.float32)
    for core in range(8):
        b, g = core // 4, core % 4
        out[b, g * 2048:(g + 1) * 2048] = res.results[core]["y_out"]
    return out
```

```python
import math
from contextlib import ExitStack

import numpy as np
import concourse.bass as bass
import concourse.mybir as mybir
from concourse.bass_utils import run_bass_kernel_spmd

F32 = mybir.dt.float32
BF16 = mybir.dt.bfloat16
F32R = mybir.dt.float32r
AF = mybir.ActivationFunctionType
ALU = mybir.AluOpType
AX = mybir.AxisListType

D = 2048
KC = 16
SEQ = 8192
TT = 512
NB = 4
IN_W = 24656
EPS = 1e-6
NEG = -30000.0
GROUPS = [[0, 1, 2, 3], [4, 5, 6, 7]]

DEBUG = {"on": False, "nt": 16, "taps": [], "phase2": True}


class Buf:
    __slots__ = ("name", "w", "r", "psum")

    def __init__(self, name="", psum=False):
        self.name = name
        self.w = None
        self.r = {}
        self.psum = psum


def bufs(n, name=""):
    return [Buf(f"{name}{i}") for i in range(n)]


class Sched:
    ENG = ("pe", "act", "dve", "pool", "sp")

    def __init__(self, nc):
        self.nc = nc
        self.q = {e: [] for e in self.ENG}
        self.sem = {}
        self.cnt = {}
        self.waited = {}
        self.epoch = 0

    def new_epoch(self):
        self.epoch += 1

    def _sem(self, agent):
        if agent not in self.sem:
            self.sem[agent] = self.nc.alloc_semaphore(name="s_" + agent.replace("@", "_").replace(":", "_"))
            self.cnt[agent] = 0
        return self.sem[agent]

    def _deps(self, eng, reads, writes):
        deps = {}
        pre = eng + "@"
        for b in reads:
            if b.w is not None:
                a, c = b.w
                if b.psum and a.startswith(pre):
                    continue
                if deps.get(a, 0) < c:
                    deps[a] = c
        for b in writes:
            if b.w is not None:
                a, c = b.w
                if not (b.psum and a.startswith(pre)) and deps.get(a, 0) < c:
                    deps[a] = c
            for a, c in b.r.items():
                if b.psum and a.startswith(pre):
                    continue
                if deps.get(a, 0) < c:
                    deps[a] = c
        waits = []
        for a, c in deps.items():
            if eng == "pe" and a.startswith("pe@"):
                continue
            if self.waited.get((eng, a), 0) >= c:
                continue
            self.waited[(eng, a)] = c
            waits.append((self.sem[a], c))
        return waits

    def _post(self, agent, c, reads, writes):
        for b in reads:
            if b.r.get(agent, 0) < c:
                b.r[agent] = c
        for b in writes:
            b.w = (agent, c)
            b.r = {}

    def op(self, eng, fn, reads=(), writes=()):
        waits = self._deps(eng, reads, writes)
        agent = f"{eng}@{self.epoch}"
        sem = self._sem(agent)
        self.cnt[agent] += 1
        self._post(agent, self.cnt[agent], reads, writes)
        self.q[eng].append((waits, fn, sem, 1))

    def dma(self, eng, slot, fn, reads=(), writes=(), inc=16):
        waits = self._deps(eng, reads, writes)
        agent = "dma:" + slot
        sem = self._sem(agent)
        self.cnt[agent] += inc
        self._post(agent, self.cnt[agent], reads, writes)
        self.q[eng].append((waits, fn, sem, inc))

    def group_done(self, slot, bl):
        agent = "dma:" + slot
        for b in bl:
            b.w = (agent, self.cnt[agent])

    def barrier(self, exclude=()):
        snap = [(a, c) for a, c in self.cnt.items() if c > 0 and a not in exclude]
        for e in self.ENG:
            waits = []
            for a, c in snap:
                if self.waited.get((e, a), 0) >= c:
                    continue
                self.waited[(e, a)] = c
                waits.append((self.sem[a], c))
            if waits:
                self.q[e].append((waits, None, None, 0))

    def wait_all_dma(self, eng):
        waits = [(self.sem[a], c) for a, c in self.cnt.items() if a.startswith("dma:")]
        self.q[eng].append((waits, None, None, 0))

    def emit(self):
        names = {"pe": "tensor", "act": "scalar", "dve": "vector", "pool": "gpsimd", "sp": "sync"}
        with self.nc.Block() as block:
            for e in self.ENG:
                items = self.q[e]
                if not items:
                    continue

                def body(eng, items=items):
                    for waits, fn, sem, inc in items:
                        for s, v in waits:
                            eng.wait_ge(s, v)
                        if fn is not None:
                            fn(eng).then_inc(sem, inc)

                getattr(block, names[e])(body)


def make_consts():
    idx = np.arange(128)
    s = idx[:, None]
    t = idx[None, :]
    same = (s // 64) == (t // 64)
    c = np.zeros((128, 8, 128), np.float32)
    c[:, 0] = np.eye(128)
    c[:, 1] = ((s <= t) & same)
    c[:, 2] = same
    c[:, 3] = (s <= t)
    c[:, 4] = 1.0
    c[:, 5] = np.where((t < s) & same, 0.0, -NEG)
    c[:, 6] = np.where((t >= s) & same, 0.0, NEG)
    c[:, 7] = np.where(t >= s, 0.0, NEG)
    return c


C_ID, C_TRI64, C_BLK64, C_TRI128, C_ONES, C_MASKL, C_MASKU, C_MASKU128 = range(8)


class Ops:
    def __init__(self, S):
        self.S = S

    def tt(self, eng, out, in0, in1, op, reads, writes):
        self.S.op(eng, lambda e: e.tensor_tensor(out=out, in0=in0, in1=in1, op=op), reads, writes)

    def ts(self, eng, out, in0, s1, op0, reads, writes, s2=None, op1=None):
        if op1 is None:
            self.S.op(eng, lambda e: e.tensor_scalar(out=out, in0=in0, scalar1=s1, scalar2=None, op0=op0), reads, writes)
        else:
            self.S.op(eng, lambda e: e.tensor_scalar(out=out, in0=in0, scalar1=s1, scalar2=s2, op0=op0, op1=op1), reads, writes)

    def stt(self, out, in0, scalar, in1, op0, op1, reads, writes):
        self.S.op("dve", lambda e: e.scalar_tensor_tensor(out=out, in0=in0, scalar=scalar, in1=in1, op0=op0, op1=op1), reads, writes)

    def act(self, out, in_, func, reads, writes, bias=None, scale=None, accum=None):
        kw = {}
        if bias is not None:
            kw["bias"] = bias
        if scale is not None:
            kw["scale"] = scale
        if accum is not None:
            kw["accum_out"] = accum
        self.S.op("act", lambda e: e.activation(out=out, in_=in_, func=func, **kw), reads, writes)

    def cp(self, eng, out, in_, reads, writes):
        self.S.op(eng, lambda e: e.tensor_copy(out=out, in_=in_), reads, writes)

    def rec(self, out, in_, reads, writes):
        self.S.op("dve", lambda e: e.reciprocal(out=out, in_=in_), reads, writes)

    def red(self, out, in_, reads, writes):
        self.S.op("dve", lambda e: e.tensor_reduce(out=out, in_=in_, axis=AX.X, op=ALU.add), reads, writes)

    def memset(self, eng, out, val, writes):
        self.S.op(eng, lambda e: e.memset(out, val), (), writes)

    def mm(self, out, lhsT, rhs, reads, writes):
        self.S.op("pe", lambda e: e.matmul(out, lhsT=lhsT, rhs=rhs, start=True, stop=True), reads, writes)

    def mmg(self, out, pairs, reads, writes):
        pairs = list(pairs)

        def fn(e):
            n = len(pairs)
            for i, (l, r) in enumerate(pairs):
                ins = e.matmul(out, lhsT=l, rhs=r, start=(i == 0), stop=(i == n - 1))
            return ins
        self.S.op("pe", fn, reads, writes)

    def tr(self, out, in_, ident, reads, writes):
        self.S.op("pe", lambda e: e.transpose(out=out, in_=in_, identity=ident), reads, writes)

    def dma(self, eng, slot, out, in_, reads, writes):
        self.S.dma(eng, slot, lambda e: e.dma_start(out=out, in_=in_), reads, writes)


def RR(ap):
    return ap


def bc_mid(ap, n):
    return ap.unsqueeze(1).broadcast_to([128, n, ap.shape[-1]])


def bc_last(ap, n):
    return ap.unsqueeze(2).broadcast_to([128, ap.shape[-1], n])


class Bank:
    def __init__(self, t, i):
        self.t = t
        self.b = Buf(f"bank{i}", psum=True)
        self.B = [self.b]

    def s4(self):
        return self.t[:].rearrange("p (s d) -> p s d", s=4)

    def flat(self):
        return self.t[:]


def build_program():
    nt_run = DEBUG["nt"]
    nc = bass.Bass("TRN2", target_bir_lowering=False)
    S = Sched(nc)
    O = Ops(S)

    def din(name, shape, dt=F32):
        return nc.dram_tensor(name, list(shape), dt, kind="ExternalInput").ap()

    x_full = din("x_full", [SEQ, D])
    x_own = din("x_own", [2048, D])
    c_t = din("c_t", [128, KC])
    w_ada = din("w_ada", [D, 3 * D])
    b_ada_t = din("b_ada_t", [128, 32])
    b_gate_row = din("b_gate_row", [1, D])
    normw_t = din("normw_t", [128, KC])
    w_fm = din("w_fm", [20, 128, KC, 128])
    w_tm = din("w_tm", [10, 128, KC, 256])
    w_sm = din("w_sm", [128, KC, 20])
    w_gate = din("w_gate", [32, 128, KC, 128])
    convw_d = din("convw", [128, 20, 4])
    alog_d = din("alog", [1, 8])
    dtb_d = din("dtb", [1, 8])
    gnw_d = din("gnw", [1, 128])
    mbi_d = din("mbi", [1, 2])
    mbf_d = din("mbf", [1, 2])
    mnw_d = din("mnw", [1, 512])
    wpa = din("wpa", [16, 128, 32, 128])
    wpb = din("wpb", [16, 128, 16, 128])
    wout = din("wout", [8, 128, KC, 256])
    npw_d = din("npw", [1, D])
    consts_d = din("consts", [128, 8, 128])
    y_out = nc.dram_tensor("y_out", [2048, D], F32, kind="ExternalOutput").ap()

    wfm_b = nc.dram_tensor("wfm_b", [20, 128, KC, 128], BF16).ap()
    wtm_b = nc.dram_tensor("wtm_b", [10, 128, KC, 256], BF16).ap()
    wsm_b = nc.dram_tensor("wsm_b", [128, KC, 20], BF16).ap()
    wgate_b = nc.dram_tensor("wgate_b", [32, 128, KC, 128], BF16).ap()
    wpa_b = nc.dram_tensor("wpa_b", [16, 128, 32, 128], BF16).ap()
    wpb_b = nc.dram_tensor("wpb_b", [16, 128, 16, 128], BF16).ap()
    wout_b = nc.dram_tensor("wout_b", [8, 128, KC, 256], BF16).ap()
    gp_d = nc.dram_tensor("gp_d", [128, D], F32).ap()
    xsrc = nc.dram_tensor("xsrc", [SEQ, 1536], BF16).ap()
    xdst = nc.dram_tensor("xdst", [4 * SEQ, 1536], BF16).ap()

    taps = DEBUG["taps"]

    def tap(name, ap, shape, reads):
        if not DEBUG["on"]:
            return
        o = nc.dram_tensor("tap_" + name, list(shape), F32, kind="ExternalOutput").ap()
        O.dma("sp", "tap_" + name, o, ap, reads, [Buf()])
        taps.append(name)

    es = ExitStack()
    with es:
        def sb(name, shape, dt=F32):
            return es.enter_context(nc.sbuf_tensor(name, list(shape), dt))

        banks = [Bank(es.enter_context(nc.psum_tensor(f"bank{i}", [128, 512], F32)), i) for i in range(8)]
        bctr = [0]

        def job():
            bk = banks[bctr[0] % 8]
            bctr[0] += 1
            return bk

        b_wfm = bufs(20, "wfm")
        b_wtm = bufs(10, "wtm")
        b_wsm = Buf("wsm")
        b_wgate = bufs(32, "wg")
        b_wpa = bufs(16, "wpa")
        b_wpb = bufs(16, "wpb")
        b_wout = bufs(8, "wo")

        def cast(dst, src, lo, hi, bl, slot):
            O.dma("pool", slot, dst[lo:hi], src[lo:hi], (), bl[lo:hi])

        cast_jobs = [(f"c1f{i}", wfm_b, w_fm, i, i + 4, b_wfm) for i in range(0, 20, 4)]
        cast_jobs += [("c1s", wsm_b, w_sm, None, None, [b_wsm])]
        cast_jobs += [(f"c1t{i}", wtm_b, w_tm, i, i + 2, b_wtm) for i in range(0, 10, 2)]
        CAST_AGENTS = tuple("dma:" + j[0] for j in cast_jobs)

        def issue_cast(after):
            if not cast_jobs:
                return
            slot, dst, src_, lo, hi, bl = cast_jobs.pop(0)
            if lo is None:
                O.dma("pool", slot, dst, src_, after, bl)
            else:
                O.dma("pool", slot, dst[lo:hi], src_[lo:hi], after, bl[lo:hi])
        cst = sb("cst", [128, 8, 128])
        b_cst = Buf("cst")
        O.dma("sp", "cst", cst[:], consts_d, (), [b_cst])
        ident = cst[:, C_ID, :]
        ones = cst[:, C_ONES, :]
        identb = sb("identb", [128, 128], BF16)
        b_identb = Buf("identb")
        O.cp("dve", identb[:], ident, [b_cst], [b_identb])

        prm = sb("prm", [128, 256])
        b_prm = Buf("prm")
        o_ct, o_bada, o_nw, o_alog, o_dtb, o_bi, o_bf = 0, 16, 48, 64, 72, 80, 82
        o_csil, o_ada, o_G, o_negA = 96, 112, 160, 176
        O.dma("sp", "prm", prm[:, o_ct:o_ct + 16], c_t, (), [b_prm])
        O.dma("sp", "prm", prm[:, o_bada:o_bada + 32], b_ada_t, (), [b_prm])
        O.dma("sp", "prm", prm[:, o_nw:o_nw + 16], normw_t, (), [b_prm])
        O.dma("sp", "prm", prm[:, o_alog:o_alog + 8], alog_d.broadcast_to([128, 8]), (), [b_prm])
        O.dma("sp", "prm", prm[:, o_dtb:o_dtb + 8], dtb_d.broadcast_to([128, 8]), (), [b_prm])
        O.dma("sp", "prm", prm[:, o_bi:o_bi + 2], mbi_d.broadcast_to([128, 2]), (), [b_prm])
        O.dma("sp", "prm", prm[:, o_bf:o_bf + 2], mbf_d.broadcast_to([128, 2]), (), [b_prm])
        convw = sb("convw_sb", [128, 20, 4])
        b_convw = Buf("convw")
        O.dma("sp", "prm", convw[:], convw_d, (), [b_convw])
        gnw = sb("gnw_sb", [128, 128])
        mnw = sb("mnw_sb", [128, 512])
        b_gnw, b_mnw = Buf(), Buf()
        O.dma("sp", "prm", gnw[:], gnw_d.broadcast_to([128, 128]), (), [b_gnw])
        O.dma("sp", "prm", mnw[:], mnw_d.broadcast_to([128, 512]), (), [b_mnw])

        S.group_done("prm", [b_prm, b_convw, b_gnw, b_mnw])
        dtb = prm[:, o_dtb:o_dtb + 8]
        negA = prm[:, o_negA:o_negA + 8]
        csil = prm[:, o_csil:o_csil + 16]
        Gs = prm[:, o_G:o_G + 16]
        shift = prm[:, o_ada:o_ada + 16]

        O.act(csil, prm[:, o_ct:o_ct + 16], AF.Silu, [b_prm], [b_prm])
        O.act(negA, prm[:, o_alog:o_alog + 8], AF.Exp, [b_prm], [b_prm])
        O.ts("dve", negA, negA, -1.0, ALU.mult, [b_prm], [b_prm])
        with ExitStack() as es0:
            wada_t = [es0.enter_context(nc.sbuf_tensor(f"wada{i}", [128, KC, 512], F32)) for i in range(2)]
            b_wada = bufs(2, "wada")
            w_ada_v = w_ada.rearrange("(k p) n -> p k n", p=128)
            J = job()
            for pc in range(8):
                sl = pc % 2
                O.dma("sp", f"wada{sl}", wada_t[sl][:], w_ada_v[:, :, pc * 512:(pc + 1) * 512], (), [b_wada[sl]])
                issue_cast([b_wada[sl]])
                if pc >= 5:
                    issue_cast([b_wada[sl]])
                for jj in range(4):
                    j = pc * 4 + jj
                    O.mmg(J.t[:, j:j + 1], [(wada_t[sl][:, k, jj * 128:(jj + 1) * 128], csil[:, k:k + 1]) for k in range(KC)],
                          [b_wada[sl], b_prm], J.B)
            O.tt("dve", prm[:, o_ada:o_ada + 32], J.t[:, 0:32], prm[:, o_bada:o_bada + 32], ALU.add, [b_prm], [b_prm] + J.B)
            O.stt(Gs, prm[:, o_ada + 16:o_ada + 32], 1.0, prm[:, o_nw:o_nw + 16], ALU.add, ALU.mult, [b_prm], [b_prm])
            b_gpd = Buf("gpd")
            if DEBUG["phase2"]:
                csr = es0.enter_context(nc.sbuf_tensor("csr", [128, KC, 128], F32))
                rows = es0.enter_context(nc.sbuf_tensor("rows", [128, 2, D], F32))
                GPt = es0.enter_context(nc.sbuf_tensor("GPt", [128, D], F32))
                b_csr, b_rows, b_GPt = Buf(), Buf(), Buf()
                O.dma("sp", "rw", rows[:, 0, :], b_gate_row.broadcast_to([128, D]), (), [b_rows])
                O.dma("sp", "rw", rows[:, 1, :], npw_d.broadcast_to([128, D]), (), [b_rows])
                O.cp("dve", csr[:], bc_last(csil, 128), [b_prm], [b_csr])
                for pc in range(4):
                    sl = pc % 2
                    O.dma("sp", f"wada{sl}", wada_t[sl][:], w_ada_v[:, :, 4096 + pc * 512:4096 + (pc + 1) * 512], (), [b_wada[sl]])
                    Jg = job()
                    O.mmg(Jg.t[:], [(csr[:, k, :], wada_t[sl][:, k, :]) for k in range(KC)], [b_csr, b_wada[sl]], Jg.B)
                    gsl = GPt[:, pc * 512:(pc + 1) * 512]
                    O.tt("dve", gsl, Jg.t[:], rows[:, 0, pc * 512:(pc + 1) * 512], ALU.add, [b_rows], [b_GPt] + Jg.B)
                    O.tt("dve", gsl, gsl, rows[:, 1, pc * 512:(pc + 1) * 512], ALU.mult, [b_GPt, b_rows], [b_GPt])
                O.dma("sp", "gpd", gp_d, GPt[:], [b_GPt], [b_gpd])
        while cast_jobs:
            issue_cast(())
        tap("G", Gs, [128, 16], [b_prm])
        tap("shift", shift, [128, 16], [b_prm])
        S.barrier(exclude=CAST_AGENTS)

        es1 = ExitStack()
        es1.__enter__()

        def sb1(name, shape, dt=F32):
            return es1.enter_context(nc.sbuf_tensor(name, list(shape), dt))

        xt = [sb1(f"xt{i}", [128, D]) for i in range(1)]
        b_xt = bufs(1, "xt")
        junk = sb1("junk", [128, 1024])
        b_junk = Buf("junk")
        junk_bf = junk[:].bitcast(BF16)
        st = sb1("st", [128, 64])
        b_st = Buf("st")
        hT = sb1("hT", [128, KC, TT], BF16)
        b_hT = bufs(NB, "hT")
        NWR = 3
        wr = [sb1(f"wr{i}", [128, 4096], BF16) for i in range(NWR)]
        b_wr = bufs(NWR, "wr")
        qn = sb1("qn", [128, 4, TT])
        kn = sb1("kn", [128, 4, TT])
        b_qn = bufs(4, "qn")
        b_kn = bufs(4, "kn")
        mq = sb1("mq", [128, 2, TT])
        mk = sb1("mk", [128, 2, TT])
        b_mq = bufs(2, "mq")
        b_mk = bufs(2, "mk")
        cpool = sb1("cpool", [128, 2 * TT + 3 * (TT + 4)])
        cacc = [cpool[:, i * TT:(i + 1) * TT] for i in range(2)]
        cw = [cpool[:, 2 * TT + i * (TT + 4):2 * TT + i * (TT + 4) + TT + 3] for i in range(3)]
        b_cw = bufs(3, "cw")
        b_cacc = bufs(2, "cacc")
        xt2 = cpool[:, 0:D]
        b_xt2 = b_cacc + b_cw[0:2]
        halo = sb1("halo", [128, 20, 3])
        b_halo = bufs(20, "halo")
        vtm = sb1("vtm", [128, NB, 8, 128])
        b_vtm = [[Buf() for _ in range(8)] for _ in range(NB)]
        ktm = [sb1(f"ktm{i}", [128, 4, 128]) for i in range(2)]
        b_ktm = bufs(2, "ktm")
        mktm = sb1("mktm", [128, 2, 128])
        b_mktm = Buf("mktm")
        zsg = sb1("zsg", [128, NB, 1024], BF16)
        b_zsg = [[Buf() for _ in range(4)] for _ in range(NB)]
        Vp = sb1("Vp", [128, NB, 2, 257])
        b_Vp = [[Buf() for _ in range(2)] for _ in range(NB)]
        mgw = sb1("mgw", [128, NB, 512], BF16)
        b_mgw = [[Buf() for _ in range(2)] for _ in range(NB)]
        gt = sb1("gt", [128, NB, 112])
        b_gt = Buf("gt")

        class GS:
            def __init__(self, i):
                def t(n, dt=F32):
                    return sb1(f"g{n}{i}", [128, 2, 128], dt), Buf(f"g{n}{i}")
                self.A, self.bA = t("A")
                X0 = sb1(f"gX0{i}", [128, 2, 2, 128])
                X1 = sb1(f"gX1{i}", [128, 2, 2, 128])
                self.X = [X0, X1]
                self.Gp1, self.bGp1 = X0[:, :, 0, :], Buf(f"gGp1{i}")
                self.Ap0, self.bAp0 = X0[:, :, 1, :], Buf(f"gAp0{i}")
                self.Gp0, self.bGp0 = X1[:, :, 0, :], Buf(f"gGp0{i}")
                self.Ap1, self.bAp1 = X1[:, :, 1, :], Buf(f"gAp1{i}")
                self.Lp1, self.bLp1 = t("Lp1")
                self.WTb, self.bWTb = t("WTb", BF16)
                self.qdTb, self.bqdTb = t("qdTb", BF16)
                self.aTb, self.baTb = t("aTb", BF16)
                self.kttb, self.bkttb = t("kttb", BF16)
                self.vnb, self.bvnb = t("vnb", BF16)
                self.egs = sb1(f"gegs{i}", [128, 4])
                self.begs = Buf(f"gegs{i}")
        NS = 4
        gss = [GS(i) for i in range(NS)]
        Sst = sb1("Sst", [128, 8, 128])
        b_S = bufs(8, "S")
        Cst = sb1("Cst", [128, 2, 257])
        b_C = bufs(2, "C")
        otm = [sb1(f"otm{i}", [128, 8, 128]) for i in range(2)]
        b_otm = [bufs(4, f"otm{i}") for i in range(2)]
        og = [sb1(f"og{i}", [128, 1536], BF16) for i in range(2)]
        b_og = [[Buf(), Buf()] for _ in range(2)]
        gamT = [sb1(f"gamT{i}", [8, 128]) for i in range(2)]
        b_gamT = bufs(2, "gamT")
        bTt = [sb1(f"bTt{i}", [2, 128]) for i in range(2)]
        b_bTt = bufs(2, "bTt")
        T1 = sb1("T1s", [128, 2, 128])
        T2 = sb1("T2s", [128, 2, 128])
        T3 = sb1("T3s", [128, 2, 128])
        T4 = sb1("T4s", [128, 2, 128])
        b_T1, b_T2, b_T3, b_T4 = Buf("T1"), Buf("T2"), Buf("T3"), Buf("T4")
        Sb = sb1("Sb", [128, 8, 128], BF16)
        b_Sb = bufs(8, "Sb")
        mtmpU, b_mtmpU_alias = T2, b_T2
        sTm = sb1("sTm", [128, 2, 128])
        Dm = sb1("Dm", [128, 2, 128])
        Eb = sb1("Eb", [128, 2, 128])
        qbT = sb1("qbT", [128, 2, 128])
        kwm = sb1("kwm", [128, 2, 128])
        hbuf = sb1("hbuf", [128, 1, 256])
        b_sTm, b_Dm, b_Eb, b_qbT, b_kwm, b_hbuf = bufs(6, "mw")
        b_mtmpU = b_T2

        b_xsrc = bufs(SEQ // 128, "xsrc")
        b_ag = bufs(32, "ag")
        O.memset("pool", Sst[:], 0.0, b_S)
        O.memset("pool", Sb[:], 0.0, b_Sb)
        O.memset("pool", Cst[:], 0.0, b_C)
        O.memset("pool", halo[:], 0.0, b_halo)
        O.memset("pool", Vp[:], 1.0, [b for r in b_Vp for b in r])
        O.memset("pool", gt[:], 0.0, [b_gt])
        if DEBUG["on"] and nt_run < 16 and DEBUG["phase2"]:
            O.memset("pool", og[0][:], 0.0, b_og[0])
            for r_ in range(nt_run * NB, SEQ // 128):
                O.dma("sp", "zf", xsrc[r_ * 128:(r_ + 1) * 128, :], og[0][:], b_og[0], [b_xsrc[r_]])
            S.group_done("zf", b_xsrc[nt_run * NB:])

        LN_SQ = math.log(128.0 ** -0.5)
        wctr = [0]
        def issue_ag(c):
            S.dma("pool", f"cc{c % 4}", lambda e: e.collective_compute(
                "AllGather", ALU.bypass, replica_groups=GROUPS, ins=[xsrc[c * 256:(c + 1) * 256, :]], outs=[xdst[c * 1024:(c + 1) * 1024, :]]),
                [b_xsrc[2 * c], b_xsrc[2 * c + 1]], [b_ag[c]], inc=1)

        def load_w(src_ap, n, pattern, dims, src_bufs):
            sl = wctr[0] % NWR
            wctr[0] += 1
            view = wr[sl][:, 0:n].rearrange(pattern, **dims)
            O.dma("sp", f"wr{sl}", view, src_ap, src_bufs, [b_wr[sl]])
            return sl, view

        def stage_a(src, row0, blk, xslot):
            if blk % 2 == 0:
                xtile, bxl = xt[0][:], [b_xt[0]]
            else:
                xtile, bxl = xt2, b_xt2
            so = 3 * (blk % 2)
            O.dma("sp", f"xt{blk % 2}", xtile, src[row0:row0 + 128, :], (), bxl)
            O.act(junk_bf, xtile, AF.Square, bxl, [b_junk, b_st], accum=st[:, so:so + 1])
            O.act(st[:, so + 1:so + 2], st[:, so:so + 1], AF.Sqrt, [b_st], [b_st], bias=EPS, scale=1.0 / D)
            O.rec(st[:, so + 2:so + 3], st[:, so + 1:so + 2], [b_st], [b_st])
            O.ts("dve", xtile, xtile, st[:, so + 2:so + 3], ALU.mult, bxl + [b_st], bxl)
            for kq in range(4):
                J = job()
                for s_ in range(4):
                    k = kq * 4 + s_
                    O.tr(J.s4()[:, s_, :], xtile[:, k * 128:(k + 1) * 128], ident, bxl + [b_cst], J.B)
                for s_ in range(4):
                    k = kq * 4 + s_
                    if kq % 2 == 0:
                        O.act(hT[:, k, blk * 128:(blk + 1) * 128], J.s4()[:, s_, :], AF.Identity, [b_prm], [b_hT[blk]] + J.B,
                              bias=shift[:, k:k + 1], scale=Gs[:, k:k + 1])
                    else:
                        O.ts("dve", hT[:, k, blk * 128:(blk + 1) * 128], J.s4()[:, s_, :], Gs[:, k:k + 1], ALU.mult, [b_prm], [b_hT[blk]] + J.B,
                             s2=shift[:, k:k + 1], op1=ALU.add)

        for ti in range(nt_run):
            if ti % 4 == 0:
                S.new_epoch()
            tok0 = ti * TT
            for blk in range(NB):
                stage_a(x_full, tok0 + blk * 128, blk, 0)

            pending = []

            def flush(upto=10 ** 9):
                while pending and pending[0][0] <= upto:
                    pending.pop(0)[1]()

            part2 = [None]

            for jp in range(10):
                sl, wv = load_w(wfm_b[2 * jp:2 * jp + 2].rearrange("j p k c -> p j k c"), 4096, "p (j k c) -> p j k c", dict(j=2, k=KC), b_wfm[2 * jp:2 * jp + 2])
                for jj in range(2):
                    j = 2 * jp + jj
                    J = job()
                    O.mmg(J.t[:], [(wv[:, jj, k, :], hT[:, k, :]) for k in range(KC)], [b_wr[sl]] + b_hT, J.B)
                    flush(j - 3)
                    c_, bc = cw[j % 3], b_cw[j % 3]
                    ac, bac = cacc[j % 2], b_cacc[j % 2]
                    O.act(c_[:, 3:TT + 3], J.t[:], AF.Copy, [], [bc] + J.B)
                    O.cp("pool", c_[:, 0:3], halo[:, j, :], [b_halo[j]], [bc])
                    O.ts("dve", ac[:], c_[:, 3:TT + 3], convw[:, j, 3:4], ALU.mult, [bc, b_convw], [bac])
                    for d_ in (1, 2, 3):
                        O.stt(ac[:], c_[:, 3 - d_:TT + 3 - d_], convw[:, j, 3 - d_:4 - d_], ac[:], ALU.mult, ALU.add, [bc, b_convw, bac], [bac])
                    O.cp("pool", halo[:, j, :], c_[:, TT:TT + 3], [bc], [b_halo[j]])
                    if part2[0] is not None:
                        part2[0]()
                        part2[0] = None

                    def p2(j=j, c_=c_, bc=bc, ac=ac, bac=bac):
                        if j < 8:
                            isq = j < 4
                            hq = j % 4
                            dst, bd = (qn, b_qn[hq]) if isq else (kn, b_kn[hq])
                            O.act(dst[:, hq, :], ac[:], AF.Silu, [bac], [bd])
                            O.act(c_[:, 0:TT], dst[:, hq, :], AF.Square, [bd], [bc])

                            def tail():
                                J2 = job()
                                O.mm(J2.t[:], ones, c_[:, 0:TT], [bc, b_cst], J2.B)
                                bx_ = Buf()
                                O.act(J2.t[:], J2.t[:], AF.Ln, [], J2.B + [bx_], bias=EPS, scale=1.0)
                                O.act(J2.t[:], J2.t[:], AF.Exp, [bx_], J2.B, bias=(LN_SQ if isq else 0.0), scale=-0.5)
                                O.tt("dve", dst[:, hq, :], dst[:, hq, :], J2.t[:], ALU.mult, [], [bd] + J2.B)
                            pending.append((j, tail))
                        elif j < 16:
                            hv = j - 8
                            O.act(c_[:, 0:TT], ac[:], AF.Silu, [bac], [bc])

                            def tail():
                                J2 = job()
                                for blk in range(NB):
                                    O.tr(J2.s4()[:, blk, :], c_[:, blk * 128:(blk + 1) * 128], ident, [bc, b_cst], J2.B)
                                O.cp("dve", vtm[:, :, hv, :], J2.s4(), [], [b_vtm[blk][hv] for blk in range(NB)] + J2.B)
                            pending.append((j, tail))
                        else:
                            h_ = j % 2
                            dst, bd = (mq, b_mq[h_]) if j < 18 else (mk, b_mk[h_])
                            O.act(dst[:, h_, :], ac[:], AF.Silu, [bac], [bd])
                    part2[0] = p2
            part2[0]()

            sl, wv = load_w(wsm_b, KC * 20, "p (k c) -> p k c", dict(k=KC), [b_wsm])
            J = job()
            for blk in range(NB):
                O.mmg(J.t[:, blk * 20:(blk + 1) * 20], [(hT[:, k, blk * 128:(blk + 1) * 128], wv[:, k, :]) for k in range(KC)],
                      [b_wr[sl], b_hT[blk]], J.B)
            flush()

            def G_(a, b_):
                return gt[:, :, a:b_]
            RG, WG = [b_gt], [b_gt]
            O.act(G_(0, 20), J.t[:, 0:80].rearrange("p (b c) -> p b c", b=NB), AF.Copy, [], WG + J.B)
            O.tt("dve", G_(20, 28), G_(0, 8), bc_mid(dtb, NB), ALU.add, [b_gt, b_prm], WG)
            O.act(G_(20, 28), G_(20, 28), AF.Exp, RG, WG)
            O.act(G_(20, 28), G_(20, 28), AF.Ln, RG, WG, bias=1.0, scale=1.0)
            O.tt("dve", G_(20, 28), G_(20, 28), bc_mid(negA, NB), ALU.mult, [b_gt, b_prm], WG)
            O.act(G_(28, 36), G_(8, 16), AF.Sigmoid, RG, WG)
            O.tt("dve", G_(36, 38), G_(16, 18), bc_mid(prm[:, o_bi:o_bi + 2], NB), ALU.add, [b_gt, b_prm], WG)
            O.tt("dve", G_(38, 40), G_(18, 20), bc_mid(prm[:, o_bf:o_bf + 2], NB), ALU.add, [b_gt, b_prm], WG)
            O.act(G_(38, 40), G_(38, 40), AF.Exp, RG, WG, scale=-1.0)
            O.act(G_(38, 40), G_(38, 40), AF.Ln, RG, WG, bias=1.0, scale=1.0)
            O.ts("dve", G_(38, 40), G_(38, 40), -1.0, ALU.mult, RG, WG)

            def stage_d2():
                J = job()
                for blk in range(NB):
                    for (cid, a, b_, o_) in ((C_TRI64, 20, 28, 0), (C_BLK64, 20, 28, 8), (C_TRI128, 38, 40, 16), (C_ONES, 38, 40, 18)):
                        base = blk * 20 + o_
                        O.mm(J.t[:, base:base + (b_ - a)], cst[:, cid, :], gt[:, blk, a:b_], [b_gt, b_cst], J.B)
                O.act(G_(40, 60), J.t[:, 0:80].rearrange("p (b c) -> p b c", b=NB), AF.Copy, [], WG + J.B)
                O.ts("dve", G_(60, 68), G_(40, 48), -1.0, ALU.mult, RG, WG)
                O.act(G_(96, 104), G_(40, 48), AF.Exp, RG, WG)
                O.tt("dve", G_(68, 76), G_(96, 104), G_(28, 36), ALU.mult, RG, WG)
                O.tt("dve", G_(76, 84), G_(48, 56), G_(40, 48), ALU.subtract, RG, WG)
                O.act(G_(76, 84), G_(76, 84), AF.Exp, RG, WG)
                O.tt("dve", G_(104, 106), G_(36, 38), G_(56, 58), ALU.subtract, RG, WG)
                O.ts("dve", G_(84, 86), G_(104, 106), LN_SQ, ALU.add, RG, WG)
                O.tt("dve", G_(86, 88), G_(104, 106), G_(58, 60), ALU.add, RG, WG)
                O.act(G_(86, 88), G_(86, 88), AF.Exp, RG, WG)
                O.act(G_(88, 90), G_(58, 60), AF.Exp, RG, WG)

            for pc in range(10):
                sl, wv = load_w(wtm_b[pc], 4096, "p (k c) -> p k c", dict(k=KC), [b_wtm[pc]])
                tg, hf = pc // 2, pc % 2
                for blk in range(NB):
                    J = job()
                    pa = J.t[:, 0:256]
                    O.mmg(pa, [(hT[:, k, blk * 128:(blk + 1) * 128], wv[:, k, :]) for k in range(KC)], [b_wr[sl], b_hT[blk]], J.B)
                    if pc == 3 and blk == 0:
                        stage_d2()
                    if tg < 2:
                        c0 = pc * 256
                        bz = b_zsg[blk][pc]
                        zv = zsg[:, blk, c0:c0 + 256]
                        O.act(zv, pa, AF.Silu, [], [bz] + J.B)
                        zv3 = zv.rearrange("p (h d) -> p h d", h=2)
                        O.tt("pool", zv3, zv3, bc_mid(gnw[:], 2), ALU.mult, [bz, b_gnw], [bz])
                    elif tg == 2:
                        O.act(Vp[:, blk, hf, 0:256], pa, AF.Copy, [], [b_Vp[blk][hf]] + J.B)
                    elif tg == 3:
                        mv_ = mgw[:, blk, hf * 256:(hf + 1) * 256]
                        O.act(mv_, pa, AF.Sigmoid, [], [b_mgw[blk][hf]] + J.B)
                        O.tt("pool", mv_, mv_, mnw[:, hf * 256:(hf + 1) * 256], ALU.mult, [b_mgw[blk][hf], b_mnw], [b_mgw[blk][hf]])
                    else:
                        ac, bac = cacc[blk % 2], b_cacc[blk % 2]
                        mv_ = mgw[:, blk, hf * 256:(hf + 1) * 256]
                        O.act(ac[:, 0:256], pa, AF.Silu, [], [bac] + J.B)
                        O.tt("dve", mv_, mv_, ac[:, 0:256], ALU.mult, [bac, b_mgw[blk][hf]], [b_mgw[blk][hf]])
            if ti >= 1 and DEBUG["phase2"]:
                issue_ag(2 * (ti - 1))
                issue_ag(2 * (ti - 1) + 1)
            if ti == min(1, nt_run - 1) and DEBUG["phase2"]:
                for i in range(0, 32, 8):
                    cast(wgate_b, w_gate, i, i + 8, b_wgate, "c2")
                for i in range(0, 16, 4):
                    cast(wpa_b, wpa, i, i + 4, b_wpa, "c2")
                for i in range(0, 16, 8):
                    cast(wpb_b, wpb, i, i + 8, b_wpb, "c2")
                for i in range(0, 8, 4):
                    cast(wout_b, wout, i, i + 4, b_wout, "c2")
                S.group_done("c2", b_wgate + b_wpa + b_wpb + b_wout)
            if ti == 0:
                tap("gt", gt[:].rearrange("p b c -> p (b c)"), [128, NB * 112], [b_gt])
                tap("qn", qn[:].rearrange("p h t -> p (h t)"), [128, 4 * TT], b_qn)
                tap("kn", kn[:].rearrange("p h t -> p (h t)"), [128, 4 * TT], b_kn)
                tap("mq", mq[:].rearrange("p h t -> p (h t)"), [128, 2 * TT], b_mq)
                tap("vtm", vtm[:].rearrange("p b h d -> p (b h d)"), [128, NB * 8 * 128], [b for r in b_vtm for b in r])

            def emit_gamT(blk):
                J = job()
                O.mm(J.t[0:8, 0:128], gt[:, blk, 20:28], cst[:, C_TRI64, :], [b_gt, b_cst], J.B)
                O.act(gamT[blk % 2][:], J.t[0:8, 0:128], AF.Copy, [], [b_gamT[blk % 2]] + J.B)

            def emit_bT(blk):
                J = job()
                O.mm(J.t[0:2, 0:128], gt[:, blk, 38:40], cst[:, C_TRI128, :], [b_gt, b_cst], J.B)
                O.act(bTt[blk % 2][:], J.t[0:2, 0:128], AF.Copy, [], [b_bTt[blk % 2]] + J.B)

            emit_gamT(0)
            emit_bT(0)

            def pair_gen(blk, hq, g):
                tsl = slice(blk * 128, (blk + 1) * 128)
                hv0 = 2 * hq
                kt_, bkt = ktm[blk % 2], b_ktm[blk % 2]
                ot_, bot = otm[blk % 2], b_otm[blk % 2][hq]

                def gc(a):
                    return gt[:, blk, a + hv0:a + hv0 + 2]
                if hq == 0:
                    J = job()
                    for h4 in range(4):
                        O.tr(J.s4()[:, h4, :], kn[:, h4, tsl], ident, [b_kn[h4], b_cst], J.B)
                    O.act(kt_[:], J.s4(), AF.Copy, [], [bkt] + J.B)
                Jk = job()
                O.mm(Jk.s4()[:, 0, :], kn[:, hq, tsl], kn[:, hq, tsl], [b_kn[hq]], Jk.B)
                O.mm(Jk.s4()[:, 1, :], kn[:, hq, tsl], qn[:, hq, tsl], [b_kn[hq], b_qn[hq]], Jk.B)
                if hq == 0 and blk + 1 < NB:
                    emit_gamT(blk + 1)
                Jr = job()
                prow = Jr.s4()[:, 0:2, :]
                for e_ in range(2):
                    hv = hv0 + e_
                    O.mm(Jr.s4()[:, e_, :], cst[0:8, C_ID, hv:hv + 1].broadcast_to([8, 128]), gamT[blk % 2][:], [b_gamT[blk % 2], b_cst], Jr.B)
                O.tt("dve", T1[:], prow, bc_mid(cst[:, C_MASKL, :], 2), ALU.add, [b_cst], [b_T1] + Jr.B)
                O.tt("dve", T2[:], prow, bc_mid(cst[:, C_MASKU, :], 2), ALU.add, [b_cst], [b_T2] + Jr.B)
                O.act(T4[:], prow, AF.Exp, [], [b_T4] + Jr.B)
                for e_ in range(2):
                    hv = hv0 + e_
                    O.act(g.A[:, e_, :], T1[:, e_, :], AF.Exp, [b_gt, b_T1], [g.bA], bias=gt[:, blk, 40 + hv:41 + hv], scale=-1.0)
                    O.act(T3[:, e_, :], T2[:, e_, :], AF.Exp, [b_gt, b_T2], [b_T3], bias=gt[:, blk, 60 + hv:61 + hv], scale=1.0)
                for e_ in range(2):
                    hv = hv0 + e_
                    O.stt(g.A[:, e_, :], Jk.s4()[:, 0, :], gt[:, blk, 28 + hv:29 + hv], g.A[:, e_, :], ALU.mult, ALU.mult, [b_gt], [g.bA] + Jk.B)
                O.tt("dve", g.aTb[:], bc_mid(Jk.s4()[:, 1, :], 2), T3[:], ALU.mult, [b_T3], [g.baTb] + Jk.B)
                O.cp("pool", g.egs[:, 0:2], T4[:, :, 63], [b_T4], [g.begs])
                O.cp("pool", g.egs[:, 2:4], T4[:, :, 127], [b_T4], [g.begs])
                O.tt("dve", g.qdTb[:], bc_mid(qn[:, hq, tsl], 2), T4[:], ALU.mult, [b_qn[hq], b_T4], [g.bqdTb])
                O.tt("pool", g.kttb[:], bc_mid(kt_[:, hq, :], 2), bc_last(gc(76), 128), ALU.mult, [bkt, b_gt], [g.bkttb])
                yield
                Ja = job()
                for e_ in range(2):
                    O.tr(Ja.s4()[:, e_, :], g.A[:, e_, :], ident, [g.bA, b_cst], Ja.B)
                O.act(g.Ap0[:], Ja.s4()[:, 0:2, :], AF.Copy, [], [g.bAp0] + Ja.B)
                O.tt("dve", g.Gp0[:], bc_mid(ident, 2), Ja.s4()[:, 0:2, :], ALU.subtract, [b_cst], [g.bGp0] + Ja.B)
                yield
                Lp = [(g.A, g.bA), (g.Lp1, g.bLp1)]
                Ap = [(g.Ap0, g.bAp0), (g.Ap1, g.bAp1)]
                Gp = [(g.Gp0, g.bGp0), (g.Gp1, g.bGp1)]
                cur = 0
                for rnd in range(6):
                    nxt = 1 - cur
                    (Lc, bLc), (Ac, bAc) = Lp[cur], Ap[cur]
                    (Ln, bLn), (An, bAn) = Lp[nxt], Ap[nxt]
                    (Gs_, bGs), (Gd, bGd) = Gp[nxt], Gp[cur]
                    comb = 1 <= rnd <= 3
                    if comb:
                        JX = job()
                        jx = JX.t[:].rearrange("p (e a d) -> p e a d", e=2, a=2)
                        for e_ in range(2):
                            O.mm(jx[:, e_, :, :], Lc[:, e_, :], g.X[cur][:, e_, :, :], [bLc, bGs, bAc], JX.B)
                    elif rnd >= 1:
                        J3 = job()
                        for e_ in range(2):
                            O.mm(J3.s4()[:, e_, :], Lc[:, e_, :], Gs_[:, e_, :], [bLc, bGs], J3.B)
                    if rnd <= 4:
                        J1 = job()
                        for e_ in range(2):
                            O.mm(J1.s4()[:, e_, :], Ac[:, e_, :], Lc[:, e_, :], [bAc, bLc], J1.B)
                    if rnd == 0:
                        J2 = job()
                        for e_ in range(2):
                            O.mm(J2.s4()[:, e_, :], Lc[:, e_, :], Ac[:, e_, :], [bAc, bLc], J2.B)
                    if comb:
                        O.tt("dve", Gd[:], jx[:, :, 0, :], Gs_[:], ALU.add, [bGs], [bGd] + JX.B)
                    elif rnd >= 1:
                        O.tt("dve", Gd[:], J3.s4()[:, 0:2, :], Gs_[:], ALU.add, [bGs], [bGd] + J3.B)
                    if rnd <= 4:
                        O.act(Ln[:], J1.s4()[:, 0:2, :], AF.Copy, [], [bLn] + J1.B)
                    if comb:
                        O.act(An[:], jx[:, :, 1, :], AF.Copy, [], [bAn] + JX.B)
                    elif rnd == 0:
                        O.act(An[:], J2.s4()[:, 0:2, :], AF.Copy, [], [bAn] + J2.B)
                    yield
                    cur = nxt
                TTt, b_TT = Gp[1]
                kbg, bkbg = g.A, g.bA
                vb, bvb = g.Lp1, g.bLp1
                U, bU = g.Ap1, g.bAp1
                O.tt("pool", kbg[:], bc_mid(kt_[:, hq, :], 2), bc_last(gc(68), 128), ALU.mult, [bkt, b_gt], [bkbg])
                O.tt("pool", vb[:], vtm[:, blk, hv0:hv0 + 2, :], bc_last(gc(28), 128), ALU.mult, [b_vtm[blk][hv0], b_vtm[blk][hv0 + 1], b_gt], [bvb])
                yield
                Jw = job()
                for e_ in range(2):
                    O.mm(Jw.s4()[:, e_, :], kbg[:, e_, :], TTt[:, e_, :], [bkbg, b_TT], Jw.B)
                O.act(g.WTb[:], Jw.s4()[:, 0:2, :], AF.Copy, [], [g.bWTb] + Jw.B)
                Ju = job()
                for e_ in range(2):
                    O.mm(Ju.s4()[:, e_, :], TTt[:, e_, :], vb[:, e_, :], [bvb, b_TT], Ju.B)
                O.act(U[:], Ju.s4()[:, 0:2, :], AF.Copy, [], [bU] + Ju.B)
                yield
                for c in range(2):
                    cs = slice(64 * c, 64 * c + 64)
                    J1 = job()
                    for e_ in range(2):
                        O.mm(J1.s4()[:, e_, :], g.WTb[:, e_, :], Sb[:, hv0 + e_, :], [g.bWTb, b_Sb[hv0 + e_]], J1.B)
                    O.tt("dve", g.vnb[cs, :, :], U[cs, :, :], J1.s4()[cs, 0:2, :], ALU.subtract, [bU], [g.bvnb] + J1.B)
                    yield
                    J2 = job()
                    for e_ in range(2):
                        O.mmg(J2.s4()[:, e_, :], [(g.qdTb[:, e_, :], Sb[:, hv0 + e_, :]), (g.aTb[cs, e_, :], g.vnb[cs, e_, :])],
                              [g.bqdTb, b_Sb[hv0 + e_], g.baTb, g.bvnb], J2.B)
                    J3 = job()
                    for e_ in range(2):
                        O.mm(J3.s4()[:, e_, :], g.kttb[cs, e_, :], g.vnb[cs, e_, :], [g.bkttb, g.bvnb], J3.B)
                    for e_ in range(2):
                        O.stt(Sst[:, hv0 + e_, :], Sst[:, hv0 + e_, :], g.egs[:, 2 * c + e_:2 * c + e_ + 1], J3.s4()[:, e_, :], ALU.mult, ALU.add,
                              [g.begs], [b_S[hv0 + e_]] + J3.B)
                    O.act(Sb[:, hv0:hv0 + 2, :], Sst[:, hv0:hv0 + 2, :], AF.Copy, [b_S[hv0], b_S[hv0 + 1]], [b_Sb[hv0], b_Sb[hv0 + 1]])
                    O.act(ot_[cs, hv0:hv0 + 2, :], J2.s4()[cs, 0:2, :], AF.Copy, [], [bot] + J2.B)
                    yield

            def gdn_epilogue(blk):
                ogs = (ti * NB + blk) % 2
                ogt = og[ogs]
                ot_, bot = otm[blk % 2], b_otm[blk % 2]
                if ti == 0 and blk == 0:
                    tap("otm", ot_[:].rearrange("p h d -> p (h d)"), [128, 1024], bot)
                    tap("S0", Sst[:].rearrange("p h d -> p (h d)"), [128, 1024], b_S)
                j3 = junk[:].rearrange("p (h d) -> p h d", h=8)
                O.tt("dve", j3, ot_[:], ot_[:], ALU.mult, bot, [b_junk])
                O.red(st[:, 8:16], j3, [b_junk], [b_st])
                O.act(st[:, 8:16], st[:, 8:16], AF.Sqrt, [b_st], [b_st], bias=EPS, scale=1.0 / 128)
                O.rec(st[:, 16:24], st[:, 8:16], [b_st], [b_st])
                O.tt("dve", ot_[:], ot_[:], bc_last(st[:, 16:24], 128), ALU.mult, bot + [b_st], bot)
                O.tt("dve", ogt[:, 0:1024], ot_[:].rearrange("p h d -> p (h d)"), zsg[:, blk, :], ALU.mult, bot + b_zsg[blk], [b_og[ogs][0]])

            def mlstm_gen(blk):
                tsl = slice(blk * 128, (blk + 1) * 128)
                ogs = (ti * NB + blk) % 2
                ogt = og[ogs]
                J = job()
                for h_ in range(2):
                    O.tr(J.s4()[:, h_, :], mk[:, h_, tsl], ident, [b_mk[h_], b_cst], J.B)
                O.act(mktm[:], J.s4()[:, 0:2, :], AF.Copy, [], [b_mktm] + J.B)
                if blk + 1 < NB:
                    emit_bT(blk + 1)
                Jr = job()
                prow = Jr.s4()[:, 0:2, :]
                for h_ in range(2):
                    O.mm(Jr.s4()[:, h_, :], cst[0:2, C_ID, h_:h_ + 1].broadcast_to([2, 128]), bTt[blk % 2][:], [b_bTt[blk % 2], b_cst], Jr.B)
                Jq = job()
                for h_ in range(2):
                    O.mm(Jq.s4()[:, h_, :], mk[:, h_, tsl], mq[:, h_, tsl], [b_mk[h_], b_mq[h_]], Jq.B)
                O.tt("dve", mtmpU[:], prow, bc_mid(cst[:, C_MASKU128, :], 2), ALU.add, [b_cst], [b_mtmpU] + Jr.B)
                for h_ in range(2):
                    O.act(Dm[:, h_, :], mtmpU[:, h_, :], AF.Exp, [b_mtmpU, b_gt], [b_Dm], bias=gt[:, blk, 84 + h_:85 + h_], scale=1.0)
                O.act(Eb[:], prow, AF.Exp, [], [b_Eb] + Jr.B, bias=LN_SQ, scale=1.0)
                O.tt("dve", sTm[:], Jq.s4()[:, 0:2, :], Dm[:], ALU.mult, [b_Dm], [b_sTm] + Jq.B)
                O.tt("dve", qbT[:], mq[:, :, tsl], Eb[:], ALU.mult, b_mq + [b_Eb], [b_qbT])
                O.tt("pool", kwm[:], mktm[:], bc_last(gt[:, blk, 86:88], 128), ALU.mult, [b_mktm, b_gt], [b_kwm])
                yield
                for h_ in range(2):
                    Jn = job()
                    pnv = Jn.t[:]
                    O.mmg(pnv[:, 0:257], [(qbT[:, h_, :], Cst[:, h_, :]), (sTm[:, h_, :], Vp[:, blk, h_, :])],
                          [b_qbT, b_C[h_], b_sTm, b_Vp[blk][h_]], Jn.B)
                    Jc = job()
                    O.mm(Jc.t[:, 0:257], kwm[:, h_, :], Vp[:, blk, h_, :], [b_kwm, b_Vp[blk][h_]], Jc.B)
                    O.act(st[:, 23:24], pnv[:, 256:257], AF.Copy, [], [b_st] + Jn.B)
                    O.stt(st[:, 24:25], st[:, 23:24], -1.0, st[:, 23:24], ALU.mult, ALU.max, [b_st], [b_st])
                    O.ts("dve", st[:, 24:25], st[:, 24:25], 1.0, ALU.max, [b_st], [b_st])
                    O.rec(st[:, 25:26], st[:, 24:25], [b_st], [b_st])
                    O.act(hbuf[:, 0, :], pnv[:, 0:256], AF.Copy, [b_st], [b_hbuf] + Jn.B, scale=st[:, 25:26])
                    O.stt(Cst[:, h_, :], Cst[:, h_, :], gt[:, blk, 88 + h_:89 + h_], Jc.t[:, 0:257], ALU.mult, ALU.add, [b_gt], [b_C[h_]] + Jc.B)
                    O.act(junk[:, 0:256], hbuf[:, 0, :], AF.Square, [b_hbuf], [b_junk, b_st], accum=st[:, 26:27])
                    O.act(st[:, 27:28], st[:, 26:27], AF.Sqrt, [b_st], [b_st], bias=EPS, scale=1.0 / 256)
                    O.rec(st[:, 28:29], st[:, 27:28], [b_st], [b_st])
                    O.stt(ogt[:, 1024 + h_ * 256:1024 + (h_ + 1) * 256], hbuf[:, 0, :], st[:, 28:29], mgw[:, blk, h_ * 256:(h_ + 1) * 256], ALU.mult, ALU.mult,
                          [b_hbuf, b_st, b_mgw[blk][h_]], [b_og[ogs][1]])
                    yield

            def og_store(blk):
                ogs = (ti * NB + blk) % 2
                r0 = tok0 + blk * 128
                O.dma("sp", f"og{ogs}", xsrc[r0:r0 + 128, :], og[ogs][:], b_og[ogs], [b_xsrc[r0 // 128]])

            pair_q = [(blk, hq) for blk in range(NB) for hq in range(4)]
            free_sets = list(range(NS))
            active = []
            done_pairs = [0] * NB
            ml_q = list(range(NB))
            ml_active = None
            ml_done = [False] * NB
            epi_done = [False] * NB
            stored = [False] * NB

            def try_store():
                for b_ in range(NB):
                    if (not stored[b_]) and epi_done[b_] and ml_done[b_]:
                        og_store(b_)
                        stored[b_] = True

            def drain_ml(upto):
                nonlocal ml_active
                while True:
                    if ml_active is None:
                        if ml_q and ml_q[0] <= upto:
                            b_ = ml_q.pop(0)
                            ml_active = (mlstm_gen(b_), b_)
                        else:
                            return
                    try:
                        while True:
                            next(ml_active[0])
                    except StopIteration:
                        ml_done[ml_active[1]] = True
                        ml_active = None

            while pair_q or active or ml_q or ml_active is not None:
                while pair_q and free_sets:
                    blk_, hq_ = pair_q.pop(0)
                    si = free_sets.pop(0)
                    active.append([pair_gen(blk_, hq_, gss[si]), si, blk_])
                for item in list(active):
                    try:
                        next(item[0])
                    except StopIteration:
                        active.remove(item)
                        free_sets.append(item[1])
                        done_pairs[item[2]] += 1
                        if done_pairs[item[2]] == 4:
                            bb = item[2]
                            if bb >= 2:
                                drain_ml(bb - 2)
                                try_store()
                            gdn_epilogue(bb)
                            epi_done[bb] = True
                            try_store()
                if ml_active is None and ml_q and (ml_q[0] < 2 or stored[ml_q[0] - 2]):
                    b_ = ml_q.pop(0)
                    ml_active = (mlstm_gen(b_), b_)
                if ml_active is not None:
                    try:
                        next(ml_active[0])
                    except StopIteration:
                        ml_done[ml_active[1]] = True
                        ml_active = None
                        try_store()
                if not active and not pair_q and ml_active is None and ml_q and not (ml_q[0] < 2 or stored[ml_q[0] - 2]):
                    try_store()
            try_store()
            assert all(stored), stored

        es1.__exit__(None, None, None)

        if DEBUG["phase2"]:
            last_c = 2 * (nt_run - 1)
            n_c = 32 if (DEBUG["on"] and nt_run < 16) else 2 * nt_run
            for c in range(last_c, n_c):
                issue_ag(c)
            phase2(nc, S, O, job, locals())
        else:
            if DEBUG["on"]:
                o = nc.dram_tensor("tap_xsrc", [nt_run * TT, 1536], BF16, kind="ExternalOutput").ap()
                O.dma("pool", "tapx", o, xsrc[0:nt_run * TT, :], b_xsrc[0:nt_run * NB], [Buf()])
                taps.append("xsrc")
            S.barrier()
            zz = sb("zz", [128, D])
            bz = Buf()
            O.memset("pool", zz[:], 0.0, [bz])
            O.dma("sp", "yz", y_out[0:128, :], zz[:], [bz], [Buf()])
        S.wait_all_dma("sp")
        S.wait_all_dma("pool")
        S.emit()
    return nc


def phase2(nc, S, O, job, L):
    x_own, y_out, xdst = L["x_own"], L["y_out"], L["xdst"]
    wgate_b, wpa_b, wpb_b, wout_b = L["wgate_b"], L["wpa_b"], L["wpb_b"], L["wout_b"]
    b_wgate, b_wpa, b_wpb, b_wout = L["b_wgate"], L["b_wpa"], L["b_wpb"], L["b_wout"]
    cst, b_cst, b_prm = L["cst"], L["b_cst"], L["b_prm"]
    identb, b_identb = L["identb"], L["b_identb"]
    gp_d, b_gpd = L["gp_d"], L["b_gpd"]
    ident = cst[:, C_ID, :]
    Gs, shift = L["Gs"], L["shift"]
    S.new_epoch()
    S.barrier(exclude=("dma:cc0", "dma:cc1", "dma:cc2", "dma:cc3", "dma:c2", "dma:gpd") + L["CAST_AGENTS"])
    with ExitStack() as es:
        def sb(name, shape, dt=F32):
            return es.enter_context(nc.sbuf_tensor(name, list(shape), dt))
        GP = sb("GP", [128, D])
        b_GP = Buf("GP")
        O.dma("sp", "gpl", GP[:], gp_d, [b_gpd], [b_GP])
        xo = sb("xo", [128, D])
        b_xo = Buf("xo")
        junk = sb("junk2", [128, D], BF16)
        b_junk = Buf()
        st = sb("st2", [128, 16])
        b_st = Buf()
        hT2 = sb("hT2", [128, KC, TT], BF16)
        b_hT2 = bufs(NB, "hT2")
        NWQ = 3
        wr = [sb(f"wq{i}", [128, 4096], BF16) for i in range(NWQ)]
        b_wr = bufs(NWQ, "wq")
        sg = sb("sg", [128, 32 * TT], BF16)
        b_sg = bufs(32, "sg")
        sg3 = sg[:].rearrange("p (j t) -> p j t", j=32)
        outsb = sg[:].bitcast(F32).rearrange("p (b n) -> p b n", b=NB)
        araw2 = [sb(f"araw{i}", [128, 4, 1536], BF16) for i in range(2)]
        b_araw2 = bufs(2, "araw")
        aT = sb("aT", [128, 48, TT], BF16)
        b_aT = bufs(NB, "aT")
        mT = sb("mT", [128, KC, TT], BF16)
        b_mT = bufs(KC, "mT")
        tmpm = [sb(f"tmpm{i}", [128, TT]) for i in range(2)]
        b_tmpm = bufs(2, "tmpm")
        ysb = sb("ysb", [128, D])
        b_ysb = Buf("ysb")

        xg = nc.dram_tensor("xg", [8192, 1536], BF16).ap()
        b_xg = Buf("xg")
        for i4 in range(4):
            def dyn1(e, i4=i4):
                pid = e.partition_id()
                g = pid % 4
                return e.dma_start(out=xg[i4 * 2048:(i4 + 1) * 2048, :], in_=xdst[bass.ds(g * 8192 + i4 * 2048, 2048), :])
            S.dma("pool", "xg", dyn1, L["b_ag"], [b_xg])

        wctr = [0]

        def load_w(src_ap, n, pattern, dims, src_bufs):
            sl = wctr[0] % NWQ
            wctr[0] += 1
            view = wr[sl][:, 0:n].rearrange(pattern, **dims)
            O.dma("sp", f"wq{sl}", view, src_ap, src_bufs, [b_wr[sl]])
            return sl, view

        for t2 in range(DEBUG.get("p2_tiles", 4)):
            for blk in range(NB):
                row0 = t2 * TT + blk * 128
                xb, b_xb, xslot = (xo, b_xo, "xo") if blk % 2 == 0 else (ysb, b_ysb, "xo2")
                so = 3 * (blk % 2)
                O.dma("sp", xslot, xb[:], x_own[row0:row0 + 128, :], (), [b_xb])
                O.act(junk[:], xb[:], AF.Square, [b_xb], [b_junk, b_st], accum=st[:, so:so + 1])
                O.act(st[:, so + 1:so + 2], st[:, so:so + 1], AF.Sqrt, [b_st], [b_st], bias=EPS, scale=1.0 / D)
                O.rec(st[:, so + 2:so + 3], st[:, so + 1:so + 2], [b_st], [b_st])
                O.ts("dve", xb[:], xb[:], st[:, so + 2:so + 3], ALU.mult, [b_xb, b_st], [b_xb])
                for kq in range(4):
                    J = job()
                    for s_ in range(4):
                        k = kq * 4 + s_
                        O.tr(J.s4()[:, s_, :], xb[:, k * 128:(k + 1) * 128], ident, [b_xb, b_cst], J.B)
                    for s_ in range(4):
                        k = kq * 4 + s_
                        if kq % 2 == 0:
                            O.act(hT2[:, k, blk * 128:(blk + 1) * 128], J.s4()[:, s_, :], AF.Identity, [b_prm], [b_hT2[blk]] + J.B,
                                  bias=shift[:, k:k + 1], scale=Gs[:, k:k + 1])
                        else:
                            O.ts("dve", hT2[:, k, blk * 128:(blk + 1) * 128], J.s4()[:, s_, :], Gs[:, k:k + 1], ALU.mult, [b_prm], [b_hT2[blk]] + J.B,
                                 s2=shift[:, k:k + 1], op1=ALU.add)
            for jp in range(16):
                sl, wv = load_w(wgate_b[2 * jp:2 * jp + 2].rearrange("j p k c -> p j k c"), 4096, "p (j k c) -> p j k c", dict(j=2, k=KC), b_wgate[2 * jp:2 * jp + 2])
                for jj in range(2):
                    j = 2 * jp + jj
                    J = job()
                    O.mmg(J.t[:], [(wv[:, jj, k, :], hT2[:, k, :]) for k in range(KC)], [b_wr[sl]] + b_hT2, J.B)
                    O.act(sg3[:, j, :], J.t[:], AF.Sigmoid, [], [b_sg[j]] + J.B)
            for blk in range(NB):
                row0 = t2 * TT + blk * 128
                araw, b_araw = araw2[blk % 2], b_araw2[blk % 2]
                for r in range(4):
                    xrow = (row0 // 256) * 1024 + r * 256 + (row0 % 256)
                    O.dma("sp", f"araw{blk % 2}", araw[:, r, :], xg[xrow:xrow + 128, :], [b_xg], [b_araw])
                for r in range(4):
                    for grp in range(3):
                        fc0 = (r * 8 + grp * 4) if grp < 2 else (32 + r * 4)
                        J = job()
                        jb = J.t[:].bitcast(BF16).rearrange("p (s d) -> p s d", s=8)
                        for s_ in range(4):
                            cc = grp * 4 + s_
                            O.tr(jb[:, s_, :], araw[:, r, cc * 128:(cc + 1) * 128], identb[:], [b_araw, b_identb], J.B)
                        if (r * 3 + grp) % 2 == 0:
                            O.act(aT[:, fc0:fc0 + 4, blk * 128:(blk + 1) * 128], jb[:, 0:4, :], AF.Copy, [], [b_aT[blk]] + J.B)
                        else:
                            O.cp("dve", aT[:, fc0:fc0 + 4, blk * 128:(blk + 1) * 128], jb[:, 0:4, :], [], [b_aT[blk]] + J.B)
            for cb in range(16):
                Ja, Jb = job(), job()
                sl, wv = load_w(wpa_b[cb], 4096, "p (k c) -> p k c", dict(k=32), [b_wpa[cb]])
                O.mmg(Ja.t[:], [(wv[:, k, :], aT[:, k, :]) for k in range(32)], [b_wr[sl]] + b_aT, Ja.B)
                sl2, wv2 = load_w(wpb_b[cb], 2048, "p (k c) -> p k c", dict(k=16), [b_wpb[cb]])
                O.mmg(Jb.t[:], [(wv2[:, k, :], aT[:, 32 + k, :]) for k in range(16)], [b_wr[sl2]] + b_aT, Jb.B)
                O.tt("dve", tmpm[0][:], Ja.t[:], sg3[:, cb, :], ALU.mult, [b_sg[cb]], [b_tmpm[0]] + Ja.B)
                O.tt("dve", tmpm[1][:], Jb.t[:], sg3[:, 16 + cb, :], ALU.mult, [b_sg[16 + cb]], [b_tmpm[1]] + Jb.B)
                O.tt("pool", mT[:, cb, :], tmpm[0][:], tmpm[1][:], ALU.add, b_tmpm, [b_mT[cb]])
            b_out = bufs(NB, "outsb")
            for b_ in b_out:
                for bs_ in b_sg:
                    for ag, c in list(bs_.r.items()) + ([bs_.w] if bs_.w is not None else []):
                        if b_.r.get(ag, 0) < c:
                            b_.r[ag] = c
            for pcs in range(8):
                sl, wv = load_w(wout_b[pcs], 4096, "p (k c) -> p k c", dict(k=KC), [b_wout[pcs]])
                for blk in range(NB):
                    J = job()
                    O.mmg(J.t[:, 0:256], [(mT[:, k, blk * 128:(blk + 1) * 128], wv[:, k, :]) for k in range(KC)], [b_wr[sl]] + b_mT, J.B)
                    dstv = outsb[:, blk, pcs * 256:(pcs + 1) * 256]
                    if (pcs + blk) % 2 == 0:
                        O.act(dstv, J.t[:, 0:256], AF.Copy, [], [b_out[blk]] + J.B)
                    else:
                        O.cp("dve", dstv, J.t[:, 0:256], [], [b_out[blk]] + J.B)
            for blk in range(NB):
                row0 = t2 * TT + blk * 128
                O.dma("sp", "xo", xo[:], x_own[row0:row0 + 128, :], (), [b_xo])
                O.act(junk[:], outsb[:, blk, :], AF.Square, [b_out[blk]], [b_junk, b_st], accum=st[:, 8:9])
                O.act(st[:, 9:10], st[:, 8:9], AF.Sqrt, [b_st], [b_st], bias=EPS, scale=1.0 / D)
                O.rec(st[:, 10:11], st[:, 9:10], [b_st], [b_st])
                O.stt(ysb[:], outsb[:, blk, :], st[:, 10:11], GP[:], ALU.mult, ALU.mult, [b_out[blk], b_st, b_GP], [b_ysb])
                O.tt("pool", ysb[:], ysb[:], xo[:], ALU.add, [b_ysb, b_xo], [b_ysb])
                O.dma("sp", "yo", y_out[row0:row0 + 128, :], ysb[:], [b_ysb], [Buf()])
            for b_ in b_sg:
                for bo in b_out:
                    for ag, c in bo.r.items():
                        if b_.r.get(ag, 0) < c:
                            b_.r[ag] = c
                    if bo.w is not None:
                        ag, c = bo.w
                        if b_.r.get(ag, 0) < c:
                            b_.r[ag] = c


def host_inputs(inp):
    f = np.float32
    x = np.asarray(inp["x"], f)
    c = np.asarray(inp["c"], f)
    w_ada = np.ascontiguousarray(np.asarray(inp["w_ada"], f)[0])
    b_ada = np.asarray(inp["b_ada"], f)[0]
    w_in = np.asarray(inp["w_in"], f)[0]
    gconv = np.asarray(inp["gdn_conv_w"], f)[0]
    mconv = np.asarray(inp["mlstm_conv_w"], f)[0]

    def fm_layout(cols):
        n = cols.shape[1] // 128
        return np.ascontiguousarray(cols.reshape(KC, 128, n, 128).transpose(2, 1, 0, 3))

    def tm_layout(cols, w):
        n = cols.shape[1] // w
        return np.ascontiguousarray(cols.reshape(KC, 128, n, w).transpose(2, 1, 0, 3))

    w_gate = fm_layout(w_in[:, 20560:24656])
    wpa_full = np.asarray(inp["w_proj_gdn"], f)[0]
    wpb_full = np.asarray(inp["w_proj_mlstm"], f)[0]
    wout_full = np.asarray(inp["w_out"], f)[0]
    wpa = np.ascontiguousarray(wpa_full.reshape(32, 128, 16, 128).transpose(2, 1, 0, 3))
    wpb = np.ascontiguousarray(wpb_full.reshape(16, 128, 16, 128).transpose(2, 1, 0, 3))
    wout = tm_layout(wout_full, 256)
    consts = make_consts()
    npw = np.asarray(inp["norm_post_w"], f)[0][None, :]
    normw_t = np.ascontiguousarray(np.asarray(inp["norm_pre_w"], f)[0].reshape(KC, 128).T)
    b_ada_t = np.ascontiguousarray(b_ada[:4096].reshape(32, 128).T)
    b_gate_row = np.ascontiguousarray(b_ada[4096:][None, :])
    maps = []
    for core in range(8):
        b, g = core // 4, core % 4
        gq = w_in[:, g * 512:(g + 1) * 512]
        gk = w_in[:, 2048 + g * 512:2048 + (g + 1) * 512]
        gv = w_in[:, 4096 + g * 1024:4096 + (g + 1) * 1024]
        mqc = w_in[:, 12352 + g * 256:12352 + (g + 1) * 256]
        mkc = w_in[:, 13376 + g * 256:13376 + (g + 1) * 256]
        w_fm = fm_layout(np.concatenate([gq, gk, gv, mqc, mkc], axis=1))
        gz = w_in[:, 8256 + g * 1024:8256 + (g + 1) * 1024]
        mv = w_in[:, 14400 + g * 512:14400 + (g + 1) * 512]
        mo = w_in[:, 16464 + g * 512:16464 + (g + 1) * 512]
        mz = w_in[:, 18512 + g * 512:18512 + (g + 1) * 512]
        w_tm = tm_layout(np.concatenate([gz, mv, mo, mz], axis=1), 256)
        sm = np.concatenate([w_in[:, 8192 + g * 8:8192 + (g + 1) * 8], w_in[:, 8224 + g * 8:8224 + (g + 1) * 8],
                             w_in[:, 16448 + g * 2:16448 + (g + 1) * 2], w_in[:, 16456 + g * 2:16456 + (g + 1) * 2]], axis=1)
        w_sm = np.ascontiguousarray(sm.reshape(KC, 128, 20).transpose(1, 0, 2))
        cv = np.concatenate([gconv[:, g * 512:(g + 1) * 512], gconv[:, 2048 + g * 512:2048 + (g + 1) * 512],
                             gconv[:, 4096 + g * 1024:4096 + (g + 1) * 1024],
                             mconv[:, g * 256:(g + 1) * 256], mconv[:, 1024 + g * 256:1024 + (g + 1) * 256]], axis=1)
        convw = np.ascontiguousarray(cv.reshape(4, 20, 128).transpose(2, 1, 0))
        m = {
            "x_full": x[b], "x_own": np.ascontiguousarray(x[b, g * 2048:(g + 1) * 2048]),
            "c_t": np.ascontiguousarray(c[b].reshape(KC, 128).T),
            "w_ada": w_ada, "b_ada_t": b_ada_t, "b_gate_row": b_gate_row, "normw_t": normw_t,
            "w_fm": w_fm, "w_tm": w_tm, "w_sm": w_sm, "w_gate": w_gate, "convw": convw,
            "alog": np.asarray(inp["gdn_A_log"], f)[0][None, g * 8:(g + 1) * 8].copy(),
            "dtb": np.asarray(inp["gdn_dt_bias"], f)[0][None, g * 8:(g + 1) * 8].copy(),
            "gnw": np.asarray(inp["gdn_norm_w"], f)[0][None, :].copy(),
            "mbi": np.asarray(inp["mlstm_b_i"], f)[0][None, g * 2:(g + 1) * 2].copy(),
            "mbf": np.asarray(inp["mlstm_b_f"], f)[0][None, g * 2:(g + 1) * 2].copy(),
            "mnw": np.asarray(inp["mlstm_norm_w"], f)[0][None, g * 512:(g + 1) * 512].copy(),
            "wpa": wpa, "wpb": wpb, "wout": wout, "npw": npw, "consts": consts,
        }
        maps.append(m)
    return maps


def kernel(**inputs):
    maps = host_inputs(inputs)
    nc = build_program()
    res = run_bass_kernel_spmd(nc, maps, core_ids=list(range(8)))
    out = np.zeros((2, SEQ, D), np.float32)
    for core in range(8):
        b, g = core // 4, core % 4
        out[b, g * 2048:(g + 1) * 2048] = res.results[core]["y_out"]
    return out
```
